# Optimizing a Trainium2 kernel written in Bass

```python
import math
import jax, jax.numpy as jnp
from jax import lax
import numpy as np

D_MODEL = 4096
BATCH = 4
SEQ = 2048
DEPTH = 1
DEC_BATCH = 128
DEC_SEQ = 8
PAST_LEN = 16384
PAGE_SIZE = 128

HM = 4
DK_M = 256
DV_M = 512
HR = 8
DK_R = 128
DV_R = 256
MIX = HM * DV_M + HR * DV_R
D_FF = 4 * D_MODEL
CHUNK = 64
GATE_SOFTCAP = 15.0
ROPE_BASE = 10000.0
EPS = 1e-6
COLS = (HM * DK_M, HM * DK_M, HM * DV_M, HM * DV_M, HM, HM,
        HR * DK_R, HR * DK_R, HR * DV_R, HR * DV_R)
IN_COLS = sum(COLS)

kernel_name = "hymba_mlstm_retention_decode_step"


def rmsnorm(x, g):
    x32 = x.astype(jnp.float32)
    y = x32 * lax.rsqrt(jnp.mean(x32 * x32, axis=-1, keepdims=True) + EPS)
    return (y * g.astype(jnp.float32)).astype(x.dtype)


def head_rmsnorm(h, g):
    y = h * lax.rsqrt(jnp.mean(h * h, axis=-1, keepdims=True) + EPS)
    return y * g.astype(jnp.float32)


def softcap(z):
    return GATE_SOFTCAP * jnp.tanh(z / GATE_SOFTCAP)


def rotary(x, pos):
    d = x.shape[-1]
    freqs = ROPE_BASE ** (-jnp.arange(0, d, 2, dtype=jnp.float32) / d)
    ang = pos[:, None] * freqs[None, :]
    cos = jnp.cos(ang)[None, :, None, :]
    sin = jnp.sin(ang)[None, :, None, :]
    x1, x2 = x[..., : d // 2], x[..., d // 2:]
    return jnp.concatenate([x1 * cos - x2 * sin, x1 * sin + x2 * cos], axis=-1)


def to_chunks(a, L):
    B, T = a.shape[:2]
    a = a.reshape((B, T // L, L) + a.shape[2:])
    if a.ndim == 5:
        return a.transpose(1, 0, 3, 2, 4)
    return a.transpose(1, 0, 3, 2)


def from_chunks(a):
    NC, B, H, L, d = a.shape
    return a.transpose(1, 0, 3, 2, 4).reshape(B, NC * L, H, d)


def mlstm_chunked(q, k, v, logi, logf, C0, n0, m0):
    T = q.shape[1]
    L = math.gcd(T, CHUNK)
    causal = jnp.tril(jnp.ones((L, L), dtype=bool))

    def step(carry, inp):
        C, n, m = carry
        qc, kc, vc, ic, fc = inp
        b = jnp.cumsum(fc, axis=-1)
        dlog = b[..., :, None] - b[..., None, :] + ic[..., None, :]
        dlog = jnp.where(causal, dlog, -jnp.inf)
        inter = b + m[..., None]
        mt = jnp.maximum(inter, jnp.max(dlog, axis=-1))
        dw = jnp.exp(dlog - mt[..., None])
        iw = jnp.exp(inter - mt)
        s = jnp.einsum('bhld,bhsd->bhls', qc, kc) * dw
        num = iw[..., None] * jnp.einsum('bhld,bhde->bhle', qc, C) + jnp.einsum('bhls,bhse->bhle', s, vc)
        den = iw * jnp.einsum('bhld,bhd->bhl', qc, n) + jnp.sum(s, axis=-1)
        h = num / jnp.maximum(jnp.abs(den), jnp.exp(-mt))[..., None]
        bL = b[..., -1]
        wlog = bL[..., None] - b + ic
        m_new = jnp.maximum(bL + m, jnp.max(wlog, axis=-1))
        decay = jnp.exp(bL + m - m_new)
        w = jnp.exp(wlog - m_new[..., None])
        C_new = decay[..., None, None] * C + jnp.einsum('bhs,bhsd,bhse->bhde', w, kc, vc)
        n_new = decay[..., None] * n + jnp.einsum('bhs,bhsd->bhd', w, kc)
        return (C_new, n_new, m_new), h

    xs = (to_chunks(q, L), to_chunks(k, L), to_chunks(v, L), to_chunks(logi, L), to_chunks(logf, L))
    (C, n, m), h = lax.scan(step, (C0, n0, m0), xs)
    return from_chunks(h), C, n, m


def retention_chunked(q, k, v, S0):
    T = q.shape[1]
    L = math.gcd(T, CHUNK)
    lg = jnp.log(1.0 - 2.0 ** (-5.0 - jnp.arange(HR, dtype=jnp.float32)))
    idx = jnp.arange(L, dtype=jnp.float32)
    diff = idx[:, None] - idx[None, :]
    dmat = jnp.where(diff >= 0, jnp.exp(jnp.maximum(diff, 0.0)[None] * lg[:, None, None]), 0.0)
    inter = jnp.exp((idx[None, :] + 1.0) * lg[:, None])
    kdec = jnp.exp((L - 1.0 - idx[None, :]) * lg[:, None])
    sdec = jnp.exp(L * lg)

    def step(S, inp):
        qc, kc, vc = inp
        s = jnp.einsum('bhld,bhsd->bhls', qc, kc) * dmat
        o = jnp.einsum('bhls,bhse->bhle', s, vc) + inter[..., None] * jnp.einsum('bhld,bhde->bhle', qc, S)
        S_new = sdec[:, None, None] * S + jnp.einsum('hs,bhsd,bhse->bhde', kdec, kc, vc)
        return S_new, o

    xs = (to_chunks(q, L), to_chunks(k, L), to_chunks(v, L))
    S, o = lax.scan(step, S0, xs)
    return from_chunks(o), S


def layer(x, C0, n0, m0, S0, pos0, w_in, b_igate, b_fgate, g_mlstm_head, g_ret_head,
          w_out, g_norm_mix, g_norm_ffn, w_up, w_down):
    B, T, _ = x.shape
    f32 = jnp.float32
    h = rmsnorm(x, g_norm_mix)
    proj = jnp.einsum('btd,dc->btc', h, w_in).astype(f32)
    splits = np.cumsum(COLS)[:-1].tolist()
    qm, km, vm, om, ig, fg, qr, kr, vr, gr = jnp.split(proj, splits, axis=-1)

    qm = qm.reshape(B, T, HM, DK_M) * (DK_M ** -0.5)
    km = km.reshape(B, T, HM, DK_M)
    vm = vm.reshape(B, T, HM, DV_M)
    logi = softcap(ig + b_igate.astype(f32))
    logf = jax.nn.log_sigmoid(softcap(fg + b_fgate.astype(f32)))
    hm, C, n, m = mlstm_chunked(qm, km, vm, logi, logf,
                                C0.astype(f32), n0.astype(f32), m0.astype(f32))
    hm = head_rmsnorm(hm, g_mlstm_head) * jax.nn.sigmoid(om.reshape(B, T, HM, DV_M))

    pos = jnp.arange(T, dtype=f32) + pos0
    qr = rotary(qr.reshape(B, T, HR, DK_R), pos)
    kr = rotary(kr.reshape(B, T, HR, DK_R), pos) * (DK_R ** -0.5)
    vr = vr.reshape(B, T, HR, DV_R)
    hr, S = retention_chunked(qr, kr, vr, S0.astype(f32))
    hr = head_rmsnorm(hr, g_ret_head) * jax.nn.silu(gr.reshape(B, T, HR, DV_R))

    cat = jnp.concatenate([hm.reshape(B, T, HM * DV_M), hr.reshape(B, T, HR * DV_R)], axis=-1)
    x = x + jnp.einsum('btc,cd->btd', cat.astype(x.dtype), w_out)

    u = jnp.einsum('btd,df->btf', rmsnorm(x, g_norm_ffn), w_up)
    x = x + jnp.einsum('btf,fd->btd', jnp.square(jax.nn.relu(u)), w_down)
    return x, C, n, m, S


def setup_inputs(seed: int = 0) -> dict:
    key = jax.random.key(seed)
    ks = jax.random.split(key, 20)
    nrm = jax.random.normal
    f32 = jnp.float32
    return {
        "x_prompt": nrm(ks[0], (BATCH, SEQ, D_MODEL), f32),
        "x_sample": nrm(ks[1], (DEC_BATCH, DEC_SEQ, D_MODEL), f32),
        "state_mlstm_C": 0.5 * nrm(ks[2], (DEPTH, DEC_BATCH, HM, DK_M, DV_M), f32),
        "state_mlstm_n": 0.5 * nrm(ks[3], (DEPTH, DEC_BATCH, HM, DK_M), f32),
        "state_mlstm_m": nrm(ks[4], (DEPTH, DEC_BATCH, HM), f32),
        "state_ret_S": 0.5 * nrm(ks[5], (DEPTH, DEC_BATCH, HR, DK_R, DV_R), f32),
        "w_in": nrm(ks[6], (DEPTH, D_MODEL, IN_COLS), f32) * D_MODEL ** -0.5,
        "b_igate": 0.1 * nrm(ks[7], (DEPTH, HM), f32),
        "b_fgate": jnp.linspace(3.0, 6.0, HM, dtype=f32)[None] + 0.1 * nrm(ks[8], (DEPTH, HM), f32),
        "g_mlstm_head": 1.0 + 0.02 * nrm(ks[9], (DEPTH, HM, DV_M), f32),
        "g_ret_head": 1.0 + 0.02 * nrm(ks[10], (DEPTH, HR, DV_R), f32),
        "w_out": nrm(ks[11], (DEPTH, MIX, D_MODEL), f32) * MIX ** -0.5,
        "g_norm_mix": 1.0 + 0.02 * nrm(ks[12], (DEPTH, D_MODEL), f32),
        "g_norm_ffn": 1.0 + 0.02 * nrm(ks[13], (DEPTH, D_MODEL), f32),
        "w_up": nrm(ks[14], (DEPTH, D_MODEL, D_FF), f32) * D_MODEL ** -0.5,
        "w_down": nrm(ks[15], (DEPTH, D_FF, D_MODEL), f32) * D_FF ** -0.5,
        "g_final": 1.0 + 0.02 * nrm(ks[16], (D_MODEL,), f32),
    }


def reference(x_prompt, x_sample, state_mlstm_C, state_mlstm_n, state_mlstm_m, state_ret_S,
              w_in, b_igate, b_fgate, g_mlstm_head, g_ret_head, w_out, g_norm_mix, g_norm_ffn,
              w_up, w_down, g_final):
    f32 = jnp.float32
    B = x_prompt.shape[0]
    yp, ys = x_prompt, x_sample
    pC, pn, pm, pS, sC, sn, sm, sS = [], [], [], [], [], [], [], []
    for l in range(DEPTH):
        w = (w_in[l], b_igate[l], b_fgate[l], g_mlstm_head[l], g_ret_head[l], w_out[l],
             g_norm_mix[l], g_norm_ffn[l], w_up[l], w_down[l])
        yp, c, n, m, s = layer(yp, jnp.zeros((B, HM, DK_M, DV_M), f32), jnp.zeros((B, HM, DK_M), f32),
                               jnp.zeros((B, HM), f32), jnp.zeros((B, HR, DK_R, DV_R), f32), 0.0, *w)
        pC.append(c); pn.append(n); pm.append(m); pS.append(s)
        ys, c, n, m, s = layer(ys, state_mlstm_C[l], state_mlstm_n[l], state_mlstm_m[l], state_ret_S[l],
                               float(PAST_LEN), *w)
        sC.append(c); sn.append(n); sm.append(m); sS.append(s)
    y_prompt = rmsnorm(yp, g_final)
    y_sample = rmsnorm(ys, g_final)
    return (y_prompt, y_sample,
            jnp.stack(pC), jnp.stack(pn), jnp.stack(pm), jnp.stack(pS),
            jnp.stack(sC), jnp.stack(sn), jnp.stack(sm), jnp.stack(sS))
```

```python
import math
from contextlib import ExitStack
import numpy as np
import concourse.bass as bass
import concourse.mybir as mybir
from concourse.bass_utils import run_bass_kernel_spmd

F32 = mybir.dt.float32
BF16 = mybir.dt.bfloat16
I32 = mybir.dt.int32
ALU = mybir.AluOpType
AF = mybir.ActivationFunctionType

D = 4096
NT = 9
NPRE = 8
TOK = NT * 128
PTOK = NPRE * 128
HM, DKM, DVM = 4, 256, 512
HR, DKR, DVR = 8, 128, 256
DFF = 16384
INC = 12296
QM, KM, VM, OM, IG, FG, QR, KR, VR, GR = 0, 1024, 2048, 4096, 6144, 6148, 6152, 7176, 8200, 10248
EPS = 1e-6
PI = math.pi


class Sem:
    def __init__(self, h):
        self.h = h
        self.count = 0


class Buf:
    def __init__(self, name, dsem=None):
        self.name = name
        self.wr = []
        self.rd = []
        self.dsem = dsem


class Prog:
    ENG = ("pe", "act", "dve", "pool", "sp")

    def __init__(self, nc, stack):
        self.nc = nc
        self.stack = stack
        self.q = {e: [] for e in self.ENG}
        self.sems = []
        self.free_dsems = []
        self.dsem_log = []
        self.esem = {e: self.new_sem("es_" + e) for e in self.ENG}
        self.seen = {e: {} for e in self.ENG}
        self.nins = {e: 0 for e in self.ENG}

    def new_sem(self, name):
        s = Sem(self.stack.enter_context(self.nc.semaphore(name)))
        self.sems.append(s)
        return s

    def buf(self, name, dma=False, fresh=False):
        b = Buf(name)
        if dma and fresh:
            b.dsem = self.new_sem("dq%d" % len(self.sems))
        elif dma:
            if self.free_dsems:
                b.dsem = self.free_dsems.pop()
            else:
                b.dsem = self.new_sem("ds%d" % len(self.sems))
            self.dsem_log.append(b.dsem)
        return b

    def mark(self):
        return len(self.dsem_log)

    def end_phase(self, mark):
        self.barrier()
        self.free_dsems.extend(self.dsem_log[mark:])
        del self.dsem_log[mark:]

    def _waits(self, eng, evs):
        w = {}
        for s, v in evs:
            if v > w.get(s, (s, 0))[1]:
                w[s] = (s, v)
        out = []
        seen = self.seen[eng]
        for s, v in w.values():
            if seen.get(s, 0) >= v:
                continue
            seen[s] = v
            out.append((s, v))
        return out

    def _deps(self, reads, writes):
        evs = []
        for b in reads:
            evs += b.wr
        for b in writes:
            evs += b.wr
            evs += b.rd
        return evs

    def op(self, eng, fn, reads=(), writes=(), skip_self=False):
        deps = self._deps(reads, writes)
        if skip_self:
            deps = [ev for ev in deps if ev[0] is not self.esem[eng]]
        waits = self._waits(eng, deps)
        sem = self.esem[eng]
        sem.count += 1
        ev = (sem, sem.count)
        snap = _snap(fn)
        import traceback
        where = traceback.extract_stack(limit=2)[0]

        def run(e, waits=waits, fn=fn, sem=sem, snap=snap, where=where):
            cur = _snap(fn)
            for k_, (a, b) in enumerate(zip(snap, cur)):
                if a is not b:
                    LATE.append("late-binding %s:%s var=%s" % (where.filename.split("/")[-1], where.lineno, fn.__code__.co_freevars[k_]))
                    return
            for s, v in waits:
                e.wait_ge(s.h, v)
            fn(e).then_inc(sem.h, 1)
        self.q[eng].append(run)
        for b in reads:
            b.rd.append(ev)
        for b in writes:
            b.wr = [ev]
            b.rd = []
        return ev

    def dma(self, eng, out_ap, in_ap, dsem, reads=(), writes=(), **kw):
        evs = []
        for b in reads:
            evs += b.wr
        for b in writes:
            evs += [ev for ev in b.wr if ev[0] is not dsem]
            evs += b.rd
        waits = self._waits(eng, evs)
        dsem.count += 16
        ev = (dsem, dsem.count)

        def run(e, waits=waits, dsem=dsem, out_ap=out_ap, in_ap=in_ap, kw=kw):
            for s, v in waits:
                e.wait_ge(s.h, v)
            e.dma_start(out=out_ap, in_=in_ap, **kw).then_inc(dsem.h, 16)
        self.q[eng].append(run)
        for b in reads:
            b.rd.append(ev)
        for b in writes:
            if b.wr and all(s is dsem for s, _ in b.wr) and not b.rd:
                b.wr.append(ev)
            else:
                b.wr = [ev]
                b.rd = []
        return ev

    def barrier(self):
        evs = [(s, s.count) for s in self.sems if s.count > 0]
        for eng in self.ENG:
            waits = self._waits(eng, evs)
            if not waits:
                continue

            def run(e, waits=waits):
                for s, v in waits:
                    e.wait_ge(s.h, v)
            self.q[eng].append(run)

    def emit(self):
        nc = self.nc
        with nc.Block() as block:
            @block.tensor
            def _(e):
                for f in self.q["pe"]:
                    f(e)

            @block.scalar
            def _(e):
                for f in self.q["act"]:
                    f(e)

            @block.vector
            def _(e):
                for f in self.q["dve"]:
                    f(e)

            @block.gpsimd
            def _(e):
                for f in self.q["pool"]:
                    f(e)

            @block.sync
            def _(e):
                for f in self.q["sp"]:
                    f(e)


LATE = []


def _snap(fn):
    out = []
    for c in (fn.__closure__ or ()):
        try:
            out.append(c.cell_contents)
        except ValueError:
            out.append(None)
    return out


def build_program(phases="all", debug=False):
    nc = bass.Bass("TRN2", target_bir_lowering=False)
    dt_in = lambda name, shape: nc.dram_tensor(name, list(shape), F32, kind="ExternalInput").ap()
    dt_out = lambda name, shape: nc.dram_tensor(name, list(shape), F32, kind="ExternalOutput").ap()
    okind = "ExternalOutput" if debug else None

    def dt_scr(name, shape, dt):
        if debug:
            return nc.dram_tensor(name, list(shape), dt, kind="ExternalOutput").ap()
        return nc.dram_tensor(name, list(shape), dt).ap()

    xm = dt_in("xm", [TOK, D])
    xp = dt_in("xp", [PTOK, D])
    C0 = dt_in("C0", [16 * HM * DKM, DVM])
    n0 = dt_in("n0", [128, 128])
    m0 = dt_in("m0", [16, HM])
    S0 = dt_in("S0", [16 * HR * DKR, DVR])
    w_in = dt_in("w_in", [D, INC])
    w_out = dt_in("w_out", [D, D])
    w_up = dt_in("w_up", [D, DFF])
    w_down = dt_in("w_down", [DFF, D])
    b_ig = dt_in("b_ig", [HM, 1])
    b_fg = dt_in("b_fg", [HM, 1])
    g_mh = dt_in("g_mh", [HM, DVM])
    g_rh = dt_in("g_rh", [HR, DVR])
    g_mix = dt_in("g_mix", [1, D])
    g_ffn = dt_in("g_ffn", [1, D])
    g_fin = dt_in("g_fin", [1, D])
    aux = dt_in("aux", [128, 16])
    frq = dt_in("frq", [128, 64])
    auxp = dt_in("auxp", [128, 8])

    y = dt_out("y", [TOK, D])
    pC = dt_out("pC", [HM * DKM, DVM])
    pn = dt_out("pn", [8, 128])
    pm = dt_out("pm", [HM, 1])
    pS = dt_out("pS", [HR * DKR, DVR])
    sC = dt_out("sC", [16 * HM * DKM, DVM])
    sn = dt_out("sn", [128, 128])
    sm = dt_out("sm", [16, HM])
    sS = dt_out("sS", [16 * HR * DKR, DVR])

    s_qmT = dt_scr("s_qmT", [HM * DKM, TOK], BF16)
    s_kmT = dt_scr("s_kmT", [HM * DKM, TOK], BF16)
    s_km = dt_scr("s_km", [TOK, HM * DKM], BF16)
    s_vm = dt_scr("s_vm", [TOK, HM * DVM], BF16)
    s_gsm = dt_scr("s_gsm", [TOK, HM * DVM], BF16)
    s_qrT = dt_scr("s_qrT", [HR * DKR, TOK], BF16)
    s_krT = dt_scr("s_krT", [HR * DKR, TOK], BF16)
    s_kr = dt_scr("s_kr", [TOK, HR * DKR], BF16)
    s_vr = dt_scr("s_vr", [TOK, HR * DVR], BF16)
    s_gsr = dt_scr("s_gsr", [TOK, HR * DVR], BF16)
    s_kmp = dt_scr("s_kmp", [PTOK, HM * DKM], BF16)
    s_vmp = dt_scr("s_vmp", [PTOK, HM * DVM], BF16)
    s_krp = dt_scr("s_krp", [PTOK, HR * DKR], BF16)
    s_vrp = dt_scr("s_vrp", [PTOK, HR * DVR], BF16)
    s_stC = dt_scr("s_stC", [HM * DKM, DVM], F32)
    s_stS = dt_scr("s_stS", [HR * DKR, DVR], F32)
    s_cat = dt_scr("s_cat", [TOK, D], F32)
    s_negG = dt_scr("s_negG", [4, TOK], F32)
    s_yacc = dt_scr("s_yacc", [TOK, D], F32)

    def want(ph):
        return phases == "all" or ph in phases

    with ExitStack() as st:
        P = Prog(nc, st)
        _uid = [0]

        def sb(stack, name, shape, dt):
            _uid[0] += 1
            return stack.enter_context(nc.sbuf_tensor("%s_%d" % (name, _uid[0]), list(shape), dt))

        pst = [st.enter_context(nc.psum_tensor("ps%d" % i, [128, 512], F32)) for i in range(7)]
        pstb = st.enter_context(nc.psum_tensor("ps7", [128, 1024], BF16))
        B_ps = [P.buf("ps%d" % i) for i in range(7)]
        B7 = P.buf("ps7")
        B_ST = B_ps[2]
        B_Gb = B7
        B_sm = B7
        B_sm2 = B7
        B_tp = [B_ps[6], B7]
        b7f = pstb[:].bitcast(F32)
        ps_ST = pst[2][:, 0:128]
        ps_Gb = b7f[:, 0:128]
        ps_sm = b7f[:, 128:256]
        ps_sm2 = b7f[:, 256:384]
        tp = [pst[6][:].bitcast(BF16)[:, 0:512], pstb[:, 0:512]]
        mmcnt = [0]

        def mmbank():
            i = mmcnt[0] % 2
            mmcnt[0] += 1
            return pst[i], B_ps[i]

        aux_t = sb(st, "aux_t", [128, 16], F32)
        auxp_t = sb(st, "auxp_t", [128, 8], F32)
        identf = sb(st, "identf", [128, 128], F32)
        identb = sb(st, "identb", [128, 128], BF16)
        causal = sb(st, "causal", [128, 128], F32)
        scausal = sb(st, "scausal", [128, 128], F32)
        rdiff = sb(st, "rdiff", [128, 128], F32)
        bm16 = sb(st, "bm16", [128, 16], F32)
        onesf = sb(st, "onesf", [128, 128], F32)
        ones_bf = sb(st, "ones_bf", [128, 1], BF16)
        pcols = sb(st, "pcols", [128, 4], F32)
        flagc = sb(st, "flagc", [128, 1], F32)
        tokc = sb(st, "tokc", [128, NT * 24], F32)
        wcolp = sb(st, "wcolp", [128, NPRE * 4], F32)
        wcolp_bf = sb(st, "wcolp_bf", [128, NPRE * 4], BF16)
        decb = sb(st, "decb", [128, 4 * 24], F32)
        B_aux = P.buf("aux", dma=True)
        B_const = P.buf("const")
        B_tokc = P.buf("tokc")
        B_wcolp = P.buf("wcolp")
        B_decb = P.buf("decb")

        P.dma("sp", aux_t[:], aux, B_aux.dsem, writes=[B_aux])
        P.dma("sp", auxp_t[:], auxp, B_aux.dsem, writes=[B_aux])
        with ExitStack() as ph:
            _mk353 = P.mark()
            di = sb(ph, "di", [128, 128], I32)
            df = sb(ph, "df", [128, 128], F32)
            e16i = sb(ph, "e16i", [16, 128], I32)
            e16 = sb(ph, "e16", [16, 128], F32)
            e16b = sb(ph, "e16b", [16, 128], F32)
            blk = sb(ph, "blk", [128, 128], F32)
            B_t = P.buf("s0tmp")

            P.op("pool", lambda e: e.iota(di[:], [[1, 128]], base=0, channel_multiplier=-1), writes=[B_t])
            P.op("dve", lambda e: e.tensor_copy(df[:], di[:]), reads=[B_t], writes=[B_const])
            P.op("dve", lambda e: e.tensor_single_scalar(identf[:], df[:], 0.0, ALU.is_equal), reads=[B_const], writes=[B_const])
            P.op("dve", lambda e: e.tensor_copy(identb[:], identf[:]), reads=[B_const], writes=[B_const])
            P.op("dve", lambda e: e.tensor_single_scalar(causal[:], df[:], 0.0, ALU.is_ge), reads=[B_const], writes=[B_const])
            P.op("dve", lambda e: e.tensor_scalar_max(rdiff[:], df[:], 0.0), reads=[B_const], writes=[B_const])
            P.op("dve", lambda e: e.memset(onesf[:], 1.0), writes=[B_const])
            P.op("dve", lambda e: e.memset(ones_bf[:], 1.0), writes=[B_const])
            P.op("pool", lambda e: e.iota(e16i[:], [[1, 128]], base=0, channel_multiplier=-8), writes=[B_t])
            P.op("dve", lambda e: e.tensor_copy(e16[:], e16i[:]), reads=[B_t], writes=[B_const])
            P.op("dve", lambda e: e.tensor_single_scalar(e16b[:], e16[:], 0.0, ALU.is_ge), reads=[B_const], writes=[B_const])
            P.op("dve", lambda e: e.tensor_single_scalar(e16[:], e16[:], 8.0, ALU.is_lt), reads=[B_const], writes=[B_const])
            P.op("dve", lambda e: e.tensor_mul(e16[:], e16[:], e16b[:]), reads=[B_const], writes=[B_const])
            P.op("pe", lambda e: e.matmul(ps_ST, e16[:], e16[:], start=True, stop=True), reads=[B_const], writes=[B_ST])
            P.op("pe", lambda e: e.matmul(ps_sm[:, 0:16], e16[:], identf[0:16, 0:16], start=True, stop=True),
                 reads=[B_const], writes=[B_sm])
            P.op("dve", lambda e: e.tensor_copy(blk[:], ps_ST), reads=[B_ST], writes=[B_const])
            P.op("dve", lambda e: e.tensor_copy(bm16[:], ps_sm[:, 0:16]), reads=[B_sm], writes=[B_const])
            P.op("dve", lambda e: e.tensor_mul(scausal[:], causal[:], blk[:]), reads=[B_const], writes=[B_const])
            P.op("dve", lambda e: e.tensor_scalar_add(pcols[:, 0:1], aux_t[:, 10:11], 1.0), reads=[B_aux], writes=[B_const])
            P.op("dve", lambda e: e.tensor_scalar(pcols[:, 1:2], aux_t[:, 10:11], -1.0, 127.0, ALU.mult, ALU.add),
                 reads=[B_aux], writes=[B_const])
            P.op("dve", lambda e: e.tensor_scalar_add(pcols[:, 2:3], aux_t[:, 11:12], 1.0), reads=[B_aux], writes=[B_const])
            P.op("dve", lambda e: e.tensor_scalar(pcols[:, 3:4], aux_t[:, 11:12], -1.0, 7.0, ALU.mult, ALU.add),
                 reads=[B_aux], writes=[B_const])
            P.op("dve", lambda e: e.tensor_copy(flagc[:], aux_t[:, 0:1]), reads=[B_aux], writes=[B_const])
            P.end_phase(_mk353)

        def alloc_gemm(ph, ntok):
            g = {}
            g["XT"] = sb(ph, "XT", [128, 32 * ntok], BF16)
            g["ntok"] = ntok
            g["B_XT"] = [[P.buf("XT%d_%d" % (t, q)) for q in range(8)] for t in range(ntok // 128)]
            g["wcnt"] = 0
            return g

        def alloc_w(ph, g):
            g["wt"] = [sb(ph, "wt%d" % i, [128, 16384], BF16) for i in range(2)]
            g["B_w"] = [[P.buf("w%d_%d" % (i, q), dma=True, fresh=True) for q in range(4)] for i in range(2)]

        def xt_ap(g, kc, t0, n):
            nt = g["ntok"]
            return g["XT"][:, kc * nt + t0: kc * nt + t0 + n]

        def load_w(g, segs, nk):
            slot = g["wcnt"] % 2
            g["wcnt"] += 1
            NC = sum(s.shape[1] for s in segs)
            wt = g["wt"][slot]
            view = wt[:, 0:nk * NC].rearrange("p (k n) -> p k n", k=nk)
            kq = nk // 4
            for q in range(4):
                off = 0
                for s in segs:
                    nc_ = s.shape[1]
                    src = s.rearrange("(k p) n -> p k n", p=128)[:, q * kq:(q + 1) * kq, :]
                    P.dma("pool", view[:, q * kq:(q + 1) * kq, off:off + nc_], src, g["B_w"][slot][q].dsem,
                          writes=[g["B_w"][slot][q]])
                    off += nc_
            return (slot, nk, NC, view)

        def mm_tm(g, w, t, c0, ncols, xsrc=None, xbufs=None):
            slot, nk, NC, view = w
            ps, B = mmbank()
            lhs = xsrc if xsrc is not None else (lambda kc: xt_ap(g, kc, t * 128, 128))
            rb = xbufs if xbufs is not None else list(g["B_XT"][t])

            kq = nk // 4
            for q in range(4):
                def fn(e, q=q):
                    for kc in range(q * kq, (q + 1) * kq):
                        ins = e.matmul(ps[:, 0:ncols], lhs(kc), view[:, kc, c0:c0 + ncols], start=(kc == 0), stop=(kc == nk - 1))
                    return ins
                P.op("pe", fn, reads=(rb if q == 0 else []) + [g["B_w"][slot][q]], writes=[B], skip_self=(q > 0))
            return ps[:, 0:ncols], B

        def mm_fm(g, w, c0, ncols, t0, ntk):
            slot, nk, NC, view = w
            ps, B = mmbank()

            tiles = list(range(t0 // 128, (t0 + ntk + 127) // 128))
            kq = nk // 4
            for q in range(4):
                def fn(e, q=q):
                    for kc in range(q * kq, (q + 1) * kq):
                        ins = e.matmul(ps[0:ncols, 0:ntk], view[:, kc, c0:c0 + ncols], xt_ap(g, kc, t0, ntk),
                                       start=(kc == 0), stop=(kc == nk - 1))
                    return ins
                P.op("pe", fn, reads=([b for t in tiles for b in g["B_XT"][t]] if q == 0 else []) + [g["B_w"][slot][q]], writes=[B],
                     skip_self=(q > 0))
            return ps[0:ncols, 0:ntk], B

        def loadt(g, src2d, ntiles, norm, gsrc):
            with ExitStack() as ph:
                _mk = P.mark()
                xs = [sb(ph, "lt_xs%d" % i, [128, D], F32) for i in range(3)]
                xb = [sb(ph, "lt_xb%d" % i, [128, D], BF16) for i in range(2)]
                B_xs = [P.buf("lt_xs%d" % i, dma=True) for i in range(3)]
                B_xb = [P.buf("lt_xb%d" % i) for i in range(2)]
                st_ = sb(ph, "lt_st", [128, 12], F32)
                B_st = [P.buf("lt_st%d" % i) for i in range(3)]
                ljunk = sb(ph, "lt_junk", [128, D], BF16)
                B_ljunk = P.buf("lt_junk")
                if gsrc is not None:
                    gb = sb(ph, "lt_gb", [128, D], F32)
                    B_gb = P.buf("lt_gb", dma=True)
                    P.dma("sp", gb[:], gsrc.to_broadcast([128, D]), B_gb.dsem, writes=[B_gb])
                nt = g["ntok"]

                def stA(t):
                    s3 = t % 3
                    c = s3 * 4
                    P.dma("sp", xs[s3][:], src2d[t * 128:(t + 1) * 128, :], B_xs[s3].dsem, writes=[B_xs[s3]])
                    if norm:
                        P.op("act", lambda e: e.activation(ljunk[:], xs[s3][:], AF.Square, accum_out=st_[:, c:c + 1]),
                             reads=[B_xs[s3]], writes=[B_ljunk, B_st[s3]])
                        P.op("act", lambda e: e.activation(st_[:, c + 1:c + 2], st_[:, c:c + 1], AF.Ln, scale=1.0 / D, bias=EPS),
                             reads=[B_st[s3]], writes=[B_st[s3]])
                        P.op("act", lambda e: e.activation(st_[:, c + 2:c + 3], st_[:, c + 1:c + 2], AF.Exp, scale=-0.5),
                             reads=[B_st[s3]], writes=[B_st[s3]])

                def stB(t):
                    s3 = t % 3
                    s = t % 2
                    c = s3 * 4
                    if norm:
                        P.op("dve", lambda e: e.scalar_tensor_tensor(
                            out=xb[s][:], in0=xs[s3][:], scalar=st_[:, c + 2:c + 3], in1=gb[:], op0=ALU.mult, op1=ALU.mult),
                            reads=[B_xs[s3], B_st[s3], B_gb], writes=[B_xb[s]])
                    else:
                        P.op("dve", lambda e: e.tensor_copy(xb[s][:], xs[s3][:]), reads=[B_xs[s3]], writes=[B_xb[s]])

                def stC(t):
                    s = t % 2
                    for grp in range(8):
                        h = grp % 2

                        def tr(e, grp=grp, h=h):
                            for i in range(4):
                                kc = grp * 4 + i
                                ins = e.transpose(tp[h][:, i * 128:(i + 1) * 128], xb[s][:, kc * 128:(kc + 1) * 128], identb[:])
                            return ins
                        P.op("pe", tr, reads=[B_xb[s], B_const], writes=[B_tp[h]])
                        dst = g["XT"][:, grp * 4 * nt:(grp * 4 + 4) * nt].rearrange("p (k n) -> p k n", k=4)[:, :, t * 128:(t + 1) * 128]
                        src = tp[h].rearrange("p (k n) -> p k n", k=4)
                        if grp % 2 == 0:
                            P.op("act", lambda e, dst=dst, src=src: e.copy(dst, src), reads=[B_tp[h]], writes=[g["B_XT"][t][grp]])
                        else:
                            P.op("dve", lambda e, dst=dst, src=src: e.tensor_copy(dst, src), reads=[B_tp[h]], writes=[g["B_XT"][t][grp]])
                stA(0)
                if ntiles > 1:
                    stA(1)
                stB(0)
                for t in range(ntiles):
                    stC(t)
                    if t + 2 < ntiles:
                        stA(t + 2)
                    if t + 1 < ntiles:
                        stB(t + 1)
                P.end_phase(_mk)

        class Stage:
            def __init__(self, ph, name, n, cols, dt):
                self.t = [sb(ph, "%s%d" % (name, i), [128, cols], dt) for i in range(n)]
                self.B = [P.buf("%s%d" % (name, i), dma=True) for i in range(n)]
                self.i = 0

            def nxt(self):
                k = self.i % len(self.t)
                self.i += 1
                return self.t[k], self.B[k]

        evc = [0]

        def evac_copy(out_ap, in_ap, reads, writes, scale=None):
            k = evc[0] % 2
            evc[0] += 1
            if k == 0:
                if scale is None:
                    P.op("act", lambda e: e.copy(out_ap, in_ap), reads=reads, writes=writes)
                else:
                    P.op("act", lambda e: e.mul(out_ap, in_ap, scale), reads=reads, writes=writes)
            else:
                if scale is None:
                    P.op("dve", lambda e: e.tensor_copy(out_ap, in_ap), reads=reads, writes=writes)
                else:
                    P.op("dve", lambda e: e.tensor_scalar_mul(out_ap, in_ap, scale), reads=reads, writes=writes)

        def rotary_tables(ph, pos_t, ntl, name):
            cosT = sb(ph, name + "cos", [128, ntl * 128], F32)
            sinT = sb(ph, name + "sin", [128, ntl * 128], F32)
            B_rt = P.buf(name + "rt")
            with ExitStack() as p2:
                _mk548 = P.mark()
                frq_t = sb(p2, name + "frq", [128, 64], F32)
                ang = sb(p2, name + "ang", [128, ntl * 64], F32)
                a2 = sb(p2, name + "a2", [128, ntl * 64], F32)
                B_f = P.buf(name + "frq", dma=True)
                B_a = P.buf(name + "ang")
                P.dma("sp", frq_t[:], frq, B_f.dsem, writes=[B_f])
                for t in range(ntl):
                    P.op("dve", lambda e, t=t: e.tensor_scalar_mul(ang[:, t * 64:(t + 1) * 64], frq_t[:], pos_t[:, t:t + 1]),
                         reads=[B_f, B_aux], writes=[B_a])
                ki = sb(p2, name + "ki", [128, ntl * 64], I32)
                kf = sb(p2, name + "kf", [128, ntl * 64], F32)
                mk = sb(p2, name + "mk", [128, ntl * 64], F32)
                C1 = 6.28125
                C2 = 2.0 * PI - 6.28125
                for (shift, dstT) in ((0.0, sinT), (0.5 * PI, cosT)):
                    P.op("dve", lambda e, shift=shift: e.tensor_scalar_add(a2[:], ang[:], shift), reads=[B_a], writes=[B_a])
                    P.op("dve", lambda e: e.tensor_scalar_mul(kf[:], a2[:], 1.0 / (2.0 * PI)), reads=[B_a], writes=[B_a])
                    P.op("dve", lambda e: e.tensor_copy(ki[:], kf[:]), reads=[B_a], writes=[B_a])
                    P.op("dve", lambda e: e.tensor_copy(kf[:], ki[:]), reads=[B_a], writes=[B_a])
                    P.op("dve", lambda e: e.scalar_tensor_tensor(out=a2[:], in0=kf[:], scalar=-C1, in1=a2[:], op0=ALU.mult, op1=ALU.add),
                         reads=[B_a], writes=[B_a])
                    P.op("dve", lambda e: e.scalar_tensor_tensor(out=a2[:], in0=kf[:], scalar=-C2, in1=a2[:], op0=ALU.mult, op1=ALU.add),
                         reads=[B_a], writes=[B_a])
                    P.op("dve", lambda e: e.tensor_scalar(mk[:], a2[:], PI, 2.0 * PI, ALU.is_gt, ALU.mult), reads=[B_a], writes=[B_a])
                    P.op("dve", lambda e: e.tensor_sub(a2[:], a2[:], mk[:]), reads=[B_a], writes=[B_a])
                    P.op("dve", lambda e: e.tensor_scalar(mk[:], a2[:], -PI, 2.0 * PI, ALU.is_lt, ALU.mult), reads=[B_a], writes=[B_a])
                    P.op("dve", lambda e: e.tensor_add(a2[:], a2[:], mk[:]), reads=[B_a], writes=[B_a])
                    P.op("dve", lambda e: e.tensor_scalar(a2[:], a2[:], PI, -PI, ALU.min, ALU.max), reads=[B_a], writes=[B_a])
                    d4 = dstT[:].rearrange("p (t a j) -> p t a j", t=ntl, a=2)
                    s3 = a2[:].rearrange("p (t j) -> p t j", t=ntl)
                    P.op("act", lambda e, d4=d4, s3=s3: e.activation(d4[:, :, 0, :], s3, AF.Sin), reads=[B_a], writes=[B_rt])
                    P.op("dve", lambda e, d4=d4: e.tensor_scalar_mul(d4[:, :, 1, :], d4[:, :, 0, :], DKR ** -0.5),
                         reads=[B_rt], writes=[B_rt])
                P.end_phase(_mk548)
            return cosT, sinT, B_rt

        def rotary_evac(ph_bufs, ps_ap, B_ps_, cosT, sinT, B_rt, t, ntl, which, nh, out_bf, B_out):
            xf, ta, tb_, B_x = ph_bufs
            a = 0 if which == "q" else 1
            cs = cosT[:].rearrange("p (t a j) -> p t a j", t=ntl, a=2)[:, t, a, :].unsqueeze(1).to_broadcast([128, nh, 64])
            sn_ = sinT[:].rearrange("p (t a j) -> p t a j", t=ntl, a=2)[:, t, a, :].unsqueeze(1).to_broadcast([128, nh, 64])
            n = nh * 128
            x4 = xf[:, 0:n].rearrange("p (h a j) -> p h a j", h=nh, a=2)
            o4 = out_bf.rearrange("p (h a j) -> p h a j", h=nh, a=2)
            a3 = ta[:, 0:nh * 64].rearrange("p (h j) -> p h j", h=nh)
            b3 = tb_[:, 0:nh * 64].rearrange("p (h j) -> p h j", h=nh)
            P.op("act", lambda e: e.copy(xf[:, 0:n], ps_ap), reads=[B_ps_], writes=[B_x])
            P.op("dve", lambda e: e.tensor_mul(a3, x4[:, :, 0, :], cs), reads=[B_x, B_rt], writes=[B_x])
            P.op("dve", lambda e: e.tensor_mul(b3, x4[:, :, 1, :], sn_), reads=[B_x, B_rt], writes=[B_x])
            P.op("dve", lambda e: e.tensor_sub(o4[:, :, 0, :], a3, b3), reads=[B_x], writes=[B_out])
            P.op("dve", lambda e: e.tensor_mul(a3, x4[:, :, 0, :], sn_), reads=[B_x, B_rt], writes=[B_x])
            P.op("dve", lambda e: e.tensor_mul(b3, x4[:, :, 1, :], cs), reads=[B_x, B_rt], writes=[B_x])
            P.op("dve", lambda e: e.tensor_add(o4[:, :, 1, :], a3, b3), reads=[B_x, B_out], writes=[B_out])

        def gate_rows(ph, g, n, name):
            th = sb(ph, name + "th", [4, n], F32)
            lf = sb(ph, name + "lf", [4, n], F32)
            tmp = sb(ph, name + "tmp", [4, n], F32)
            bi = sb(ph, name + "bi", [4, 2], F32)
            B_g = P.buf(name + "g")
            B_b = P.buf(name + "b", dma=True)
            P.dma("sp", bi[:, 0:1], b_ig, B_b.dsem, writes=[B_b])
            P.dma("sp", bi[:, 1:2], b_fg, B_b.dsem, writes=[B_b])
            P.op("dve", lambda e: e.tensor_scalar_mul(bi[:], bi[:], 1.0 / 15.0), reads=[B_b], writes=[B_b])
            w = load_w(g, [w_in[:, IG:IG + 8]], 32)
            nb = (n + 383) // 384
            for b in range(nb):
                t0 = b * 384
                ntk = min(384, n - t0)
                ps, B = mm_fm(g, w, 0, 4, t0, ntk)
                P.op("act", lambda e, ps=ps, t0=t0, ntk=ntk: e.activation(th[:, t0:t0 + ntk], ps, AF.Tanh, bias=bi[:, 0:1],
                                                                           scale=1.0 / 15.0), reads=[B, B_b], writes=[B_g])
                ps, B = mm_fm(g, w, 4, 4, t0, ntk)
                P.op("act", lambda e, ps=ps, t0=t0, ntk=ntk: e.activation(tmp[:, t0:t0 + ntk], ps, AF.Tanh, bias=bi[:, 1:2],
                                                                           scale=1.0 / 15.0), reads=[B, B_b], writes=[B_g])
            P.op("act", lambda e: e.activation(tmp[:], tmp[:], AF.Exp, scale=-15.0), reads=[B_g], writes=[B_g])
            P.op("act", lambda e: e.activation(tmp[:], tmp[:], AF.Ln, bias=1.0), reads=[B_g], writes=[B_g])
            P.op("dve", lambda e: e.tensor_scalar_mul(lf[:], tmp[:], -1.0), reads=[B_g], writes=[B_g])
            return th, lf, tmp, B_g

        def rows_to_cols(rows_ap_fn, nq, ntl, dst, dst_stride, dst_off, B_rows, B_dst):
            for t in range(ntl):
                def fn(e, t=t):
                    for q in range(nq):
                        ins = e.matmul(ps_sm[:, 4 * q:4 * q + 4], rows_ap_fn(q, t), identf[0:4, 0:4], start=True, stop=True)
                    return ins
                P.op("pe", fn, reads=[B_rows, B_const], writes=[B_sm])
                P.op("dve", lambda e, t=t: e.tensor_copy(dst[:, t * dst_stride + dst_off: t * dst_stride + dst_off + 4 * nq],
                                                         ps_sm[:, 0:4 * nq]), reads=[B_sm], writes=[B_dst])

        carry = sb(st, "carry", [4, 8], F32)
        B_carry = P.buf("carry")
        ones_row = sb(st, "ones_row", [4, TOK], F32)
        P.op("dve", lambda e: e.memset(ones_row[:], 1.0), writes=[B_const])
        P.op("dve", lambda e: e.memset(carry[:], 0.0), writes=[B_carry])
        npre = sb(st, "npre", [128, 8], F32)
        B_npre = P.buf("npre")
        P.op("dve", lambda e: e.memset(npre[:], 0.0), writes=[B_npre])

        if want("pre"):
            with ExitStack() as ph:
                _mk658 = P.mark()
                g = alloc_gemm(ph, PTOK)
                loadt(g, xp, NPRE, True, g_mix)
                alloc_w(ph, g)
                cosP, sinP, B_rtP = rotary_tables(ph, auxp_t, NPRE, "rp")
                with ExitStack() as p3:
                    _mkp3 = P.mark()
                    thP, lfP, tmpP, B_g = gate_rows(p3, g, PTOK, "gp")
                    FrP = sb(p3, "gpF", [4, PTOK], F32)
                    ArP = sb(p3, "gpA", [4, PTOK], F32)
                    GrP = sb(p3, "gpG", [4, PTOK], F32)
                    P.op("dve", lambda e: e.tensor_tensor_scan(FrP[:], ones_row[:, 0:PTOK], lfP[:], 0.0, ALU.mult, ALU.add),
                         reads=[B_g, B_const], writes=[B_g])
                    P.op("dve", lambda e: e.scalar_tensor_tensor(out=ArP[:], in0=thP[:], scalar=15.0, in1=FrP[:], op0=ALU.mult,
                                                                 op1=ALU.subtract), reads=[B_g], writes=[B_g])
                    P.op("dve", lambda e: e.tensor_tensor_scan(GrP[:], ones_row[:, 0:PTOK], ArP[:], 0.0, ALU.mult, ALU.max),
                         reads=[B_g], writes=[B_g])
                    P.op("dve", lambda e: e.tensor_mul(carry[:, 0:1], FrP[:, PTOK - 1:PTOK], flagc[0:4, :]), reads=[B_g, B_const],
                         writes=[B_carry])
                    P.op("dve", lambda e: e.tensor_mul(carry[:, 1:2], GrP[:, PTOK - 1:PTOK], flagc[0:4, :]), reads=[B_g, B_const],
                         writes=[B_carry])
                    P.op("dve", lambda e: e.tensor_scalar_mul(carry[:, 2:3], GrP[:, PTOK - 1:PTOK], -1.0), reads=[B_g],
                         writes=[B_carry])
                    P.op("act", lambda e: e.activation(tmpP[:], ArP[:], AF.Exp, bias=carry[:, 2:3]), reads=[B_g, B_carry], writes=[B_g])
                    rows_to_cols(lambda q, t: tmpP[:, t * 128:(t + 1) * 128], 1, NPRE, wcolp, 4, 0, B_g, B_wcolp)
                    P.op("dve", lambda e: e.tensor_copy(wcolp_bf[:], wcolp[:]), reads=[B_wcolp], writes=[B_wcolp])
                    P.end_phase(_mkp3)
                kpm = sb(ph, "kpm", [128, NPRE * 1024], BF16)
                kpr = sb(ph, "kpr", [128, NPRE * 1024], BF16)
                B_kpm = [[P.buf("kpm%d_%d" % (t, i)) for i in range(2)] for t in range(NPRE)]
                B_kpr = [[P.buf("kpr%d_%d" % (t, i)) for i in range(2)] for t in range(NPRE)]
                rb = (sb(ph, "prx", [128, 512], F32), sb(ph, "pra", [128, 256], F32), sb(ph, "prb", [128, 256], F32), P.buf("prx"))
                wv = [sb(ph, "pwv%d" % i, [128, 512], BF16) for i in range(3)]
                B_wv = [P.buf("pwv%d" % i) for i in range(3)]
                cst = [sb(ph, "pcst%d" % i, [128, 2 * 512], F32) for i in range(2)]
                B_cst = [P.buf("pcst%d" % i, dma=True) for i in range(2)]
                nT = sb(ph, "pnT", [128, 8], F32)
                B_nT = P.buf("pnT")
                kd = sb(ph, "pkd", [128, 8], F32)
                B_kd = P.buf("pkd")
                for h in range(HR):
                    lg = math.log(1.0 - 2.0 ** (-5.0 - h))
                    P.op("act", lambda e, h=h, lg=lg: e.activation(kd[:, h:h + 1], pcols[:, 1:2], AF.Exp, scale=lg),
                         reads=[B_const], writes=[B_kd])
                jobs = []
                for i in range(2):
                    jobs.append(("km", w_in[:, KM + i * 512: KM + (i + 1) * 512], i))
                for i in range(2):
                    jobs.append(("kr", w_in[:, KR + i * 512: KR + (i + 1) * 512], i))
                for i in range(4):
                    jobs.append(("vm", w_in[:, VM + i * 512: VM + (i + 1) * 512], i))
                for i in range(4):
                    jobs.append(("vr", w_in[:, VR + i * 512: VR + (i + 1) * 512], i))
                wn = load_w(g, [jobs[0][1]], 32)
                wvc = 0
                for ji, (kind, src, idx) in enumerate(jobs):
                    w = wn
                    if ji + 1 < len(jobs):
                        wn = load_w(g, [jobs[ji + 1][1]], 32)
                    pend = []
                    for t in range(NPRE):
                        ps, B = mm_tm(g, w, t, 0, 512)
                        for f_ in pend:
                            f_()
                        pend = []
                        if kind == "km":
                            evac_copy(kpm[:, t * 1024 + idx * 512: t * 1024 + (idx + 1) * 512], ps, [B], [B_kpm[t][idx]])
                        elif kind == "kr":
                            rotary_evac(rb, ps, B, cosP, sinP, B_rtP, t, NPRE, "k", 4,
                                        kpr[:, t * 1024 + idx * 512: t * 1024 + (idx + 1) * 512], B_kpr[t][idx])
                        elif kind == "vm":
                            h = idx
                            i = wvc % 3
                            wvc += 1
                            P.op("dve", lambda e, i=i, ps=ps, t=t, h=h: e.tensor_scalar_mul(wv[i][:], ps, wcolp[:, t * 4 + h: t * 4 + h + 1]),
                                 reads=[B, B_wcolp], writes=[B_wv[i]])

                            def fn(e, t=t, i=i, h=h):
                                for c in range(2):
                                    e.matmul(pst[5 + c][:], kpm[:, t * 1024 + h * 256 + c * 128: t * 1024 + h * 256 + (c + 1) * 128], wv[i][:],
                                             start=(t == 0), stop=(t == NPRE - 1))
                                for c in range(2):
                                    ins = e.matmul(ps_sm2[:, c:c + 1], kpm[:, t * 1024 + h * 256 + c * 128: t * 1024 + h * 256 + (c + 1) * 128],
                                                   wcolp_bf[:, t * 4 + h: t * 4 + h + 1], start=(t == 0 and c == 0), stop=(t == NPRE - 1))
                                return ins
                            def later(fn=fn, t=t, h=h, i=i):
                                P.op("pe", fn, reads=[B_kpm[t][h // 2], B_wv[i], B_wcolp], writes=[B_ps[5], B_ps[6], B7])
                                if t == NPRE - 1:
                                    s_ = h % 2
                                    P.op("dve", lambda e, s_=s_: e.tensor_scalar_mul(cst[s_][:, 0:512], pst[5][:], flagc[:, 0:1]),
                                         reads=[B_ps[5], B_const], writes=[B_cst[s_]])
                                    P.op("act", lambda e, s_=s_: e.mul(cst[s_][:, 512:1024], pst[6][:], flagc[:, 0:1]),
                                         reads=[B_ps[6], B_const], writes=[B_cst[s_]])
                                    P.op("dve", lambda e, h=h: e.tensor_scalar_mul(nT[:, 2 * h:2 * h + 2], ps_sm2[:, 0:2], flagc[:, 0:1]),
                                         reads=[B7, B_const], writes=[B_nT])
                                    P.dma("sp", s_stC[h * 256:(h + 1) * 256, :].rearrange("(c p) e -> p c e", p=128),
                                          cst[s_][:].rearrange("p (c e) -> p c e", c=2), B_cst[s_].dsem, reads=[B_cst[s_]])
                            pend.append(later)
                        else:
                            i = wvc % 3
                            wvc += 1
                            for k2 in range(2):
                                h = idx * 2 + k2
                                lg = math.log(1.0 - 2.0 ** (-5.0 - h))
                                cst_ = math.exp(lg * 128.0 * (NPRE - 1 - t))
                                P.op("dve", lambda e, i=i, ps=ps, k2=k2, h=h, cst_=cst_: e.tensor_scalar(
                                    wv[i][:, k2 * 256:(k2 + 1) * 256], ps[:, k2 * 256:(k2 + 1) * 256], kd[:, h:h + 1], cst_, ALU.mult, ALU.mult),
                                    reads=[B, B_kd], writes=[B_wv[i]])

                            def fn(e, t=t, i=i, idx=idx):
                                for k2 in range(2):
                                    h = idx * 2 + k2
                                    ins = e.matmul(pst[5][:, k2 * 256:(k2 + 1) * 256], kpr[:, t * 1024 + h * 128: t * 1024 + (h + 1) * 128],
                                                   wv[i][:, k2 * 256:(k2 + 1) * 256], start=(t == 0 and k2 == 0), stop=(t == NPRE - 1))
                                return ins
                            def later(fn=fn, t=t, idx=idx, i=i):
                                P.op("pe", fn, reads=[B_kpr[t][idx // 2], B_wv[i]], writes=[B_ps[5]])
                                if t == NPRE - 1:
                                    s_ = idx % 2
                                    P.op("dve", lambda e, s_=s_: e.tensor_scalar_mul(cst[s_][:, 0:512], pst[5][:], flagc[:, 0:1]),
                                         reads=[B_ps[5], B_const], writes=[B_cst[s_]])
                                    P.dma("sp", s_stS[idx * 256:(idx + 1) * 256, :].rearrange("(k p) e -> p k e", p=128),
                                          cst[s_][:, 0:512].rearrange("p (k e) -> p k e", k=2), B_cst[s_].dsem, reads=[B_cst[s_]])
                            pend.append(later)
                    for f_ in pend:
                        f_()
                    pend = []
                P.op("dve", lambda e: e.tensor_copy(npre[:], nT[:]), reads=[B_nT], writes=[B_npre])
                P.end_phase(_mk658)

        negG_keep = None
        if want("proj"):
            with ExitStack() as ph:
                _mk786 = P.mark()
                g = alloc_gemm(ph, TOK)
                loadt(g, xm, NT, True, g_mix)
                alloc_w(ph, g)
                cosM, sinM, B_rtM = rotary_tables(ph, aux_t[:, 1:10], NT, "rm")
                with ExitStack() as p2:
                    _mk792 = P.mark()
                    th, lf, tmp, B_g = gate_rows(p2, g, TOK, "gm")
                    Fr = sb(p2, "gmF", [4, TOK], F32)
                    Ar = sb(p2, "gmA", [4, TOK], F32)
                    Gr = sb(p2, "gmG", [4, TOK], F32)
                    nG = sb(p2, "gmnG", [4, TOK], F32)
                    iw = sb(p2, "gmiw", [4, TOK], F32)
                    em = sb(p2, "gmem", [4, TOK], F32)
                    ww = sb(p2, "gmw", [4, TOK], F32)
                    em2 = sb(p2, "gmem2", [4, TOK], F32)
                    m0T = sb(p2, "gmm0", [4, 16], F32)
                    decr = sb(p2, "gmdec", [4, 24], F32)
                    mo = sb(p2, "gmmo", [4, 20], F32)
                    sel = sb(p2, "gmsel", [4, 4 * 128], F32)
                    seli = sb(p2, "gmseli", [4, 4 * 128], I32)
                    B_m0 = P.buf("gmm0", dma=True)
                    B_mo = P.buf("gmmo", dma=True)
                    P.dma("sp", m0T[:], m0.rearrange("j h -> h j"), B_m0.dsem, writes=[B_m0], allow_slow_non_contiguous=True)
                    NPR = 1024
                    S3 = lambda r: r[:, NPR:TOK].rearrange("h (j t) -> h j t", j=16)
                    P.op("dve", lambda e: e.tensor_tensor_scan(Fr[:, 0:NPR], ones_row[:, 0:NPR], lf[:, 0:NPR], carry[:, 0:1],
                                                               ALU.mult, ALU.add), reads=[B_g, B_const, B_carry], writes=[B_g])
                    P.op("dve", lambda e: e.tensor_copy(S3(Fr)[:, :, 0], S3(lf)[:, :, 0]), reads=[B_g], writes=[B_g])
                    for t in range(1, 8):
                        P.op("dve", lambda e, t=t: e.tensor_add(S3(Fr)[:, :, t], S3(Fr)[:, :, t - 1], S3(lf)[:, :, t]),
                             reads=[B_g], writes=[B_g])
                    P.op("dve", lambda e: e.scalar_tensor_tensor(out=Ar[:], in0=th[:], scalar=15.0, in1=Fr[:], op0=ALU.mult,
                                                                 op1=ALU.subtract), reads=[B_g], writes=[B_g])
                    P.op("dve", lambda e: e.tensor_tensor_scan(Gr[:, 0:NPR], ones_row[:, 0:NPR], Ar[:, 0:NPR], carry[:, 1:2],
                                                               ALU.mult, ALU.max), reads=[B_g, B_carry], writes=[B_g])
                    P.op("dve", lambda e: e.tensor_max(S3(Gr)[:, :, 0], S3(Ar)[:, :, 0], m0T[:]), reads=[B_g, B_m0], writes=[B_g])
                    for t in range(1, 8):
                        P.op("dve", lambda e, t=t: e.tensor_max(S3(Gr)[:, :, t], S3(Gr)[:, :, t - 1], S3(Ar)[:, :, t]),
                             reads=[B_g], writes=[B_g])
                    P.op("dve", lambda e: e.tensor_scalar_mul(nG[:], Gr[:], -1.0), reads=[B_g], writes=[B_g])
                    for t in range(8):
                        bias = carry[:, 1:2] if t == 0 else Gr[:, t * 128 - 1:t * 128]
                        P.op("act", lambda e, t=t, bias=bias: e.activation(iw[:, t * 128:(t + 1) * 128], Gr[:, t * 128:(t + 1) * 128],
                                                                           AF.Exp, bias=bias, scale=-1.0),
                             reads=[B_g, B_carry], writes=[B_g])
                    for t in range(8):
                        P.op("dve", lambda e, t=t: e.tensor_sub(S3(tmp)[:, :, t], m0T[:], S3(Gr)[:, :, t]), reads=[B_g, B_m0],
                             writes=[B_g])
                    P.op("act", lambda e: e.activation(iw[:, NPR:TOK], tmp[:, NPR:TOK], AF.Exp), reads=[B_g], writes=[B_g])
                    P.op("dve", lambda e: e.tensor_add(em[:], Fr[:], Gr[:]), reads=[B_g], writes=[B_g])
                    P.op("act", lambda e: e.activation(em2[:], em[:], AF.Exp, scale=-2.0), reads=[B_g], writes=[B_g])
                    P.op("act", lambda e: e.activation(em[:], em[:], AF.Exp, scale=-1.0), reads=[B_g], writes=[B_g])
                    for t in range(8):
                        P.op("act", lambda e, t=t: e.activation(ww[:, t * 128:(t + 1) * 128], Ar[:, t * 128:(t + 1) * 128], AF.Exp,
                                                                bias=nG[:, t * 128 + 127:t * 128 + 128]), reads=[B_g], writes=[B_g])
                    for t in range(8):
                        P.op("dve", lambda e, t=t: e.tensor_sub(S3(tmp)[:, :, t], S3(Ar)[:, :, t], S3(Gr)[:, :, 7]), reads=[B_g],
                             writes=[B_g])
                    P.op("act", lambda e: e.activation(ww[:, NPR:TOK], tmp[:, NPR:TOK], AF.Exp), reads=[B_g], writes=[B_g])
                    P.op("dve", lambda e: e.tensor_copy(decr[:, 0:8], iw[:, 0:NPR].rearrange("h (t p) -> h t p", p=128)[:, :, 127]),
                         reads=[B_g], writes=[B_g])
                    P.op("dve", lambda e: e.tensor_copy(decr[:, 8:24], S3(iw)[:, :, 7]), reads=[B_g], writes=[B_g])
                    P.op("dve", lambda e: e.tensor_add(mo[:, 0:1], Fr[:, NPR - 1:NPR], Gr[:, NPR - 1:NPR]), reads=[B_g], writes=[B_mo])
                    P.op("dve", lambda e: e.tensor_add(mo[:, 1:17], S3(Fr)[:, :, 7], S3(Gr)[:, :, 7]), reads=[B_g, B_mo], writes=[B_mo])
                    P.dma("sp", pm, mo[:, 0:1], B_mo.dsem, reads=[B_mo])
                    P.dma("sp", sm.rearrange("j h -> h j"), mo[:, 1:17], B_mo.dsem, reads=[B_mo], allow_slow_non_contiguous=True)
                    rows = [Ar, iw, em, ww, nG, em2]
                    rows_to_cols(lambda q, t: rows[q][:, t * 128:(t + 1) * 128], 6, NT, tokc, 24, 0, B_g, B_tokc)
                    B_nGd = P.buf("nGd", dma=True)
                    P.dma("sp", s_negG, nG[:], B_nGd.dsem, reads=[B_g])
                    P.op("pool", lambda e: e.iota(seli[:], [[1, 4], [0, 128]], base=0, channel_multiplier=-1), writes=[B_g])
                    P.op("dve", lambda e: e.tensor_copy(sel[:], seli[:]), reads=[B_g], writes=[B_g])
                    P.op("dve", lambda e: e.tensor_single_scalar(sel[:], sel[:], 0.0, ALU.is_equal), reads=[B_g], writes=[B_g])

                    def fdec(e):
                        for h in range(4):
                            ins = e.matmul(ps_sm[:, h * 24:(h + 1) * 24], sel[:, h * 128:(h + 1) * 128], decr[:], start=True, stop=True)
                        return ins
                    P.op("pe", fdec, reads=[B_g], writes=[B_sm])
                    P.op("dve", lambda e: e.tensor_copy(decb[:], ps_sm[:, 0:96]), reads=[B_sm], writes=[B_decb])
                    P.end_phase(_mk792)

                stg = Stage(ph, "mstg", 5, 512, BF16)
                stf = Stage(ph, "mstf", 2, 512, F32)
                gbh = [sb(ph, "gbh%d" % i, [128, 512], F32) for i in range(2)]
                B_gbh = [P.buf("gbh%d" % i, dma=True) for i in range(2)]
                rb = (sb(ph, "mrx", [128, 512], F32), sb(ph, "mra", [128, 256], F32), sb(ph, "mrb", [128, 256], F32), P.buf("mrx"))
                rot_bf = [sb(ph, "mrot%d" % i, [128, 512], BF16) for i in range(3)]
                B_rot = [P.buf("mrot%d" % i, dma=True) for i in range(3)]
                jobs = []
                for h in range(HM):
                    jobs.append(("qk", [w_in[:, QM + h * 256: QM + (h + 1) * 256], w_in[:, KM + h * 256: KM + (h + 1) * 256]], h))
                    jobs.append(("v", [w_in[:, VM + h * 512: VM + (h + 1) * 512]], h))
                    jobs.append(("o", [w_in[:, OM + h * 512: OM + (h + 1) * 512]], h))
                for i in range(2):
                    jobs.append(("rq", [w_in[:, QR + i * 512: QR + (i + 1) * 512]], i))
                for i in range(2):
                    jobs.append(("rk", [w_in[:, KR + i * 512: KR + (i + 1) * 512]], i))
                for i in range(4):
                    jobs.append(("rv", [w_in[:, VR + i * 512: VR + (i + 1) * 512]], i))
                for i in range(4):
                    jobs.append(("rg", [w_in[:, GR + i * 512: GR + (i + 1) * 512]], i))
                wn = load_w(g, jobs[0][1], 32)
                rc = 0
                for ji, (kind, segs, idx) in enumerate(jobs):
                    w = wn
                    if ji + 1 < len(jobs):
                        wn = load_w(g, jobs[ji + 1][1], 32)
                    if kind == "qk":
                        h = idx
                        pendk = []
                        kc_ = 0
                        for sbk in range(4):
                            dstT = s_qmT if sbk < 2 else s_kmT
                            row0 = h * 256 + (sbk % 2) * 128
                            for b in range(3):
                                ps, B = mm_fm(g, w, sbk * 128, 128, b * 384, 384)
                                for f_ in pendk:
                                    f_()
                                pendk = []
                                sg, Bs = stg.nxt()
                                evac_copy(sg[:, 0:384], ps, [B], [Bs], scale=(DKM ** -0.5 if sbk < 2 else None))
                                P.dma("sp", dstT[row0:row0 + 128, b * 384:(b + 1) * 384], sg[:, 0:384], Bs.dsem, reads=[Bs])
                                if sbk >= 2:
                                    def later(sg=sg, Bs=Bs, b=b, col0=h * 256 + (sbk - 2) * 128, hh_=kc_ % 2):
                                        def tr(e):
                                            for j in range(3):
                                                ins = e.transpose(tp[hh_][:, j * 128:(j + 1) * 128], sg[:, j * 128:(j + 1) * 128], identb[:])
                                            return ins
                                        P.op("pe", tr, reads=[Bs, B_const], writes=[B_tp[hh_]])
                                        s2, Bs2 = stg.nxt()
                                        evac_copy(s2[:, 0:384], tp[hh_][:, 0:384], [B_tp[hh_]], [Bs2])
                                        P.dma("sp", s_km[b * 384:(b + 1) * 384, col0:col0 + 128].rearrange("(j p) c -> p j c", p=128),
                                              s2[:, 0:384].rearrange("p (j c) -> p j c", j=3), Bs2.dsem, reads=[Bs2])
                                    pendk.append(later)
                                    kc_ += 1
                        for f_ in pendk:
                            f_()
                        pendk = []
                    elif kind in ("v", "rv"):
                        dstd = s_vm if kind == "v" else s_vr
                        for t in range(NT):
                            ps, B = mm_tm(g, w, t, 0, 512)
                            sg, Bs = stg.nxt()
                            evac_copy(sg[:], ps, [B], [Bs])
                            P.dma("sp", dstd[t * 128:(t + 1) * 128, idx * 512:(idx + 1) * 512], sg[:], Bs.dsem, reads=[Bs])
                    elif kind in ("o", "rg"):
                        gi = idx % 2
                        if kind == "o":
                            P.dma("sp", gbh[gi][:], g_mh[idx:idx + 1, :].to_broadcast([128, 512]), B_gbh[gi].dsem, writes=[B_gbh[gi]])
                            dstd = s_gsm
                        else:
                            for k2 in range(2):
                                P.dma("sp", gbh[gi][:, k2 * 256:(k2 + 1) * 256],
                                      g_rh[idx * 2 + k2: idx * 2 + k2 + 1, :].to_broadcast([128, 256]), B_gbh[gi].dsem,
                                      writes=[B_gbh[gi]])
                            dstd = s_gsr
                        for t in range(NT):
                            ps, B = mm_tm(g, w, t, 0, 512)
                            sf, Bf = stf.nxt()
                            sg, Bs = stg.nxt()
                            func = AF.Sigmoid if kind == "o" else AF.Silu
                            P.op("act", lambda e, sf=sf, ps=ps, func=func: e.activation(sf[:], ps, func), reads=[B], writes=[Bf])
                            P.op("dve", lambda e, sf=sf, sg=sg, gi=gi: e.tensor_mul(sg[:], sf[:], gbh[gi][:]),
                                 reads=[Bf, B_gbh[gi]], writes=[Bs])
                            P.dma("sp", dstd[t * 128:(t + 1) * 128, idx * 512:(idx + 1) * 512], sg[:], Bs.dsem, reads=[Bs])
                    elif kind in ("rq", "rk"):
                        dstT = s_qrT if kind == "rq" else s_krT

                        def tr_out(t, r, dstT=dstT, idx=idx):
                            hh_ = t % 2

                            def tr(e):
                                for i in range(4):
                                    ins = e.transpose(tp[hh_][:, i * 128:(i + 1) * 128], rot_bf[r][:, i * 128:(i + 1) * 128], identb[:])
                                return ins
                            P.op("pe", tr, reads=[B_rot[r], B_const], writes=[B_tp[hh_]])
                            sg, Bs = stg.nxt()
                            evac_copy(sg[:], tp[hh_], [B_tp[hh_]], [Bs])
                            P.dma("sp", dstT[idx * 512:(idx + 1) * 512, t * 128:(t + 1) * 128].rearrange("(h p) c -> p h c", p=128),
                                  sg[:].rearrange("p (h c) -> p h c", h=4), Bs.dsem, reads=[Bs])
                        prev = None
                        for t in range(NT):
                            ps, B = mm_tm(g, w, t, 0, 512)
                            r = rc % 3
                            rc += 1
                            rotary_evac(rb, ps, B, cosM, sinM, B_rtM, t, NT, "q" if kind == "rq" else "k", 4, rot_bf[r][:], B_rot[r])
                            if kind == "rk":
                                P.dma("sp", s_kr[t * 128:(t + 1) * 128, idx * 512:(idx + 1) * 512], rot_bf[r][:], B_rot[r].dsem,
                                      reads=[B_rot[r]])
                            if prev is not None:
                                tr_out(*prev)
                            prev = (t, r)
                        tr_out(*prev)
                P.end_phase(_mk786)

        if want("mix"):
            with ExitStack() as ph:
                _mk974 = P.mark()
                qT = [sb(ph, "qT%d" % i, [128, 2 * TOK], BF16) for i in range(2)]
                kT = [sb(ph, "kT%d" % i, [128, 2 * TOK], BF16) for i in range(2)]
                ktm = [sb(ph, "ktm%d" % i, [128, NT * 256], BF16) for i in range(2)]
                vtm = [sb(ph, "vtm%d" % i, [128, NT * 512], BF16) for i in range(2)]
                gsm_ = [sb(ph, "gsm%d" % i, [128, NT * 512], BF16) for i in range(2)]
                B_hd = [P.buf("hd%d" % i, dma=True) for i in range(2)]
                Cst = sb(ph, "Cst", [128, 2 * 512], F32)
                Cbf = sb(ph, "Cbf", [128, 2 * 512], BF16)
                nst = sb(ph, "nst", [128, 2], F32)
                nbf = sb(ph, "nbf", [128, 2], BF16)
                B_C = P.buf("Cst", dma=True)
                B_Cbf = P.buf("Cbf")
                B_Cbfh = [P.buf("Cbf0"), P.buf("Cbf1")]
                Cin = [sb(ph, "Cin%d" % i, [128, 2 * 512], F32) for i in range(2)]
                B_Cin = [P.buf("Cin%d" % i, dma=True) for i in range(2)]
                B_n = P.buf("nst")
                B_nbf = P.buf("nbf")
                qTz = sb(ph, "qTz", [128, 2 * 16 * 128], BF16)
                B_qTz = P.buf("qTz")
                C0b = [sb(ph, "C0b%d" % i, [128, 2 * 512], BF16) for i in range(4)]
                B_C0b = [P.buf("C0b%d" % i, dma=True, fresh=True) for i in range(4)]
                C0f = [sb(ph, "C0f%d" % i, [128, 2 * 512], F32) for i in range(4)]
                B_C0f = [P.buf("C0f%d" % i, dma=True) for i in range(4)]
                n0T = sb(ph, "n0T", [128, 128], F32)
                n0b = sb(ph, "n0b", [128, 128], BF16)
                n0r = sb(ph, "n0r", [128, 128], F32)
                snT = sb(ph, "snT", [128, 128], F32)
                pnT = sb(ph, "pnT", [128, 8], F32)
                B_n0 = P.buf("n0", dma=True)
                B_snT = P.buf("snT")
                B_pnT = P.buf("pnT")
                R2 = range(2)
                negGb = [sb(ph, "negGb%d" % i, [128, TOK], F32) for i in R2]
                B_negGb = [P.buf("negGb%d" % i, dma=True) for i in R2]
                ex_all = sb(ph, "ex_all", [128, TOK], F32)
                B_exall = P.buf("ex_all")
                cbp = sb(ph, "cbp", [128, 128], F32)
                cbs = sb(ph, "cbs", [128, 128], F32)
                DT = [sb(ph, "DT%d" % i, [128, 128], F32) for i in R2]
                sT = [sb(ph, "sT%d" % i, [128, 128], BF16) for i in R2]
                B_DT = [P.buf("DT%d" % i) for i in R2]
                B_sT = [P.buf("sT%d" % i) for i in R2]
                tmpB = [sb(ph, "tmpB%d" % i, [128, 512], F32) for i in R2]
                num = [sb(ph, "num%d" % i, [128, 512], F32) for i in R2]
                B_tmpB = [P.buf("tmpB%d" % i) for i in R2]
                B_num = [P.buf("num%d" % i) for i in R2]
                junk = sb(ph, "junk", [128, 512], BF16)
                B_junk = P.buf("junk")
                smc = [sb(ph, "smc%d" % i, [128, 16], F32) for i in R2]
                B_smc = [P.buf("smc%d" % i) for i in R2]
                og = Stage(ph, "og", 3, 512, F32)
                wvb = [sb(ph, "wvb%d" % i, [128, 512], BF16) for i in range(4)]
                B_wvb = [P.buf("wvb%d" % i) for i in range(4)]
                wcb = [sb(ph, "wcb%d" % i, [128, 1], BF16) for i in R2]
                B_wcb = [P.buf("wcb%d" % i) for i in R2]
                wm = sb(ph, "wm", [128, 16], F32)
                wmb = sb(ph, "wmb", [128, 16], BF16)
                B_wm = P.buf("wm")
                Dp = sb(ph, "Dp", [128, 128], F32)
                Ds = sb(ph, "Ds", [128, 128], F32)
                rcol = sb(ph, "rcol", [128, 4], F32)
                kd16 = sb(ph, "kd16", [128, 16], F32)
                B_rc = P.buf("rc")
                pnst = sb(ph, "pnst", [8, 128], F32)
                prod = [sb(ph, "prod%d" % i, [128, 128], F32) for i in range(2)]
                B_prod = [P.buf("prod%d" % i) for i in range(2)]
                B_pnst = P.buf("pnst", dma=True)
                STb = [pst[2], pst[1]]
                B_STb = [B_ps[2], B_ps[1]]
                Ab = [pst[3], pst[0]]
                B_Ab = [B_ps[3], B_ps[0]]

                P.op("dve", lambda e: e.memset(qTz[:], 0.0), writes=[B_qTz])
                P.op("dve", lambda e: e.tensor_scalar(cbp[:], causal[:], -1.0, 30000.0, ALU.add, ALU.mult), reads=[B_const], writes=[B_exall])
                P.op("dve", lambda e: e.tensor_scalar(cbs[:], scausal[:], -1.0, 30000.0, ALU.add, ALU.mult), reads=[B_const], writes=[B_exall])
                P.dma("sp", n0r[:], n0, B_n0.dsem, writes=[B_n0])
                P.op("pe", lambda e: e.matmul(ps_ST, n0r[:], identf[:], start=True, stop=True), reads=[B_n0, B_const], writes=[B_ST])
                P.op("dve", lambda e: e.tensor_copy(n0T[:], ps_ST), reads=[B_ST], writes=[B_n0])
                P.op("dve", lambda e: e.tensor_copy(n0b[:], n0T[:]), reads=[B_n0], writes=[B_n0])

                cnt = {"c0f": 0, "c0b": 0, "wv": 0}

                def load_head_m(h, s):
                    d = B_hd[s].dsem
                    P.dma("sp", qT[s][:].rearrange("p (c n) -> p c n", c=2),
                          s_qmT[h * 256:(h + 1) * 256, :].rearrange("(c p) n -> p c n", p=128), d, writes=[B_hd[s]])
                    P.dma("sp", kT[s][:].rearrange("p (c n) -> p c n", c=2),
                          s_kmT[h * 256:(h + 1) * 256, :].rearrange("(c p) n -> p c n", p=128), d, writes=[B_hd[s]])
                    P.dma("sp", ktm[s][:].rearrange("p (t c) -> p t c", t=NT),
                          s_km[:, h * 256:(h + 1) * 256].rearrange("(t p) c -> p t c", p=128), d, writes=[B_hd[s]])
                    P.dma("sp", vtm[s][:].rearrange("p (t c) -> p t c", t=NT),
                          s_vm[:, h * 512:(h + 1) * 512].rearrange("(t p) c -> p t c", p=128), d, writes=[B_hd[s]])
                    P.dma("sp", gsm_[s][:].rearrange("p (t c) -> p t c", t=NT),
                          s_gsm[:, h * 512:(h + 1) * 512].rearrange("(t p) c -> p t c", p=128), d, writes=[B_hd[s]])

                def load_head_r(h, s):
                    d = B_hd[s].dsem
                    P.dma("sp", qT[s][:, 0:TOK], s_qrT[h * 128:(h + 1) * 128, :], d, writes=[B_hd[s]])
                    P.dma("sp", kT[s][:, 0:TOK], s_krT[h * 128:(h + 1) * 128, :], d, writes=[B_hd[s]])
                    P.dma("sp", ktm[s][:, 0:NT * 128].rearrange("p (t c) -> p t c", t=NT),
                          s_kr[:, h * 128:(h + 1) * 128].rearrange("(t p) c -> p t c", p=128), d, writes=[B_hd[s]])
                    P.dma("sp", vtm[s][:, 0:NT * 256].rearrange("p (t c) -> p t c", t=NT),
                          s_vr[:, h * 256:(h + 1) * 256].rearrange("(t p) c -> p t c", p=128), d, writes=[B_hd[s]])
                    P.dma("sp", gsm_[s][:, 0:NT * 256].rearrange("p (t c) -> p t c", t=NT),
                          s_gsr[:, h * 256:(h + 1) * 256].rearrange("(t p) c -> p t c", p=128), d, writes=[B_hd[s]])

                def out_gated(r, dv, t, s, col0):
                    o, Bo = og.nxt()
                    P.op("dve", lambda e: e.scalar_tensor_tensor(out=o[:, 0:dv], in0=num[r][:, 0:dv], scalar=smc[r][:, 9:10],
                                                                 in1=gsm_[s][:, t * dv:(t + 1) * dv], op0=ALU.mult, op1=ALU.mult),
                         reads=[B_num[r], B_smc[r], B_hd[s]], writes=[Bo])
                    P.dma("sp", s_cat[t * 128:(t + 1) * 128, col0:col0 + dv], o[:, 0:dv], Bo.dsem, reads=[Bo])

                def m_cols(h, t):
                    tc0 = t * 24
                    return dict(A=tokc[:, tc0 + h: tc0 + h + 1], iw=tokc[:, tc0 + 4 + h: tc0 + 5 + h],
                                em=tokc[:, tc0 + 8 + h: tc0 + 9 + h], w=tokc[:, tc0 + 12 + h: tc0 + 13 + h],
                                nG=tokc[:, tc0 + 16 + h: tc0 + 17 + h], em2=tokc[:, tc0 + 20 + h: tc0 + 21 + h])

                def m_front_a(h, s, t):
                    r = t % 2
                    c = m_cols(h, t)
                    smp = (t == NT - 1)
                    st_ap = STb[r][:, 0:128]
                    P.op("pe", lambda e: (e.matmul(st_ap, kT[s][:, t * 128:(t + 1) * 128], qT[s][:, t * 128:(t + 1) * 128], start=True, stop=False),
                                          e.matmul(st_ap, kT[s][:, TOK + t * 128:TOK + (t + 1) * 128],
                                                   qT[s][:, TOK + t * 128:TOK + (t + 1) * 128], start=False, stop=True))[1],
                         reads=[B_hd[s]], writes=[B_STb[r]])
                    P.op("act", lambda e: e.activation(DT[r][:], ex_all[:, t * 128:(t + 1) * 128], AF.Exp, bias=c["A"]),
                         reads=[B_exall, B_tokc], writes=[B_DT[r]])
                    P.op("dve", lambda e: e.tensor_mul(sT[r][:], st_ap, DT[r][:]), reads=[B_STb[r], B_DT[r]], writes=[B_sT[r]])
                    if not smp:
                        i = t % 2
                        P.op("dve", lambda e: e.tensor_scalar_mul(wvb[i][:], vtm[s][:, t * 512:(t + 1) * 512], c["w"]),
                             reads=[B_hd[s], B_tokc], writes=[B_wvb[i]])
                        P.op("dve", lambda e: e.tensor_copy(wcb[r][:], c["w"]), reads=[B_tokc], writes=[B_wcb[r]])
                        return i
                    return None

                def m_front_b(h, s, t):
                    r = t % 2
                    P.op("pe", lambda e: (e.matmul(Ab[r][:], sT[r][:], vtm[s][:, t * 512:(t + 1) * 512], start=True, stop=True),
                                          e.matmul(STb[r][:, 128:129], sT[r][:], ones_bf[:], start=True, stop=True))[1],
                         reads=[B_sT[r], B_hd[s], B_const], writes=[B_Ab[r], B_STb[r]])

                def m_inter(h, s, t):
                    r = t % 2

                    def finter(e):
                        for c in range(2):
                            e.matmul(pst[4][:], qT[s][:, c * TOK + t * 128:c * TOK + (t + 1) * 128], Cbf[:, c * 512:(c + 1) * 512],
                                     start=(c == 0), stop=(c == 1))
                        for c in range(2):
                            ins = e.matmul(STb[r][:, 129:130], qT[s][:, c * TOK + t * 128:c * TOK + (t + 1) * 128], nbf[:, c:c + 1],
                                           start=(c == 0), stop=(c == 1))
                        return ins
                    P.op("pe", finter, reads=[B_hd[s], B_Cbfh[0], B_Cbfh[1], B_nbf], writes=[B_ps[4], B_STb[r]])

                def m_upd_mm(h, s, t, i):
                    r = t % 2

                    def fupd(e):
                        for c in range(2):
                            e.matmul(pst[5 + c][:], ktm[s][:, t * 256 + c * 128:t * 256 + (c + 1) * 128], wvb[i][:], start=True, stop=True)
                        for c in range(2):
                            ins = e.matmul(STb[r][:, 130 + c:131 + c], ktm[s][:, t * 256 + c * 128:t * 256 + (c + 1) * 128], wcb[r][:],
                                           start=True, stop=True)
                        return ins
                    P.op("pe", fupd, reads=[B_hd[s], B_wvb[i], B_wcb[r]], writes=[B_ps[5], B_ps[6], B_STb[r]])

                def m_upd_ew(h, s, t):
                    r = t % 2
                    dc = decb[:, h * 24 + t: h * 24 + t + 1]
                    csrc, Bsrc = (Cin[h % 2], B_Cin[h % 2]) if t == 0 else (Cst, B_C)
                    for c in range(2):
                        P.op("dve", lambda e, c=c: e.scalar_tensor_tensor(
                            out=Cst[:, c * 512:(c + 1) * 512], in0=csrc[:, c * 512:(c + 1) * 512], scalar=dc, in1=pst[5 + c][:],
                            op0=ALU.mult, op1=ALU.add), reads=[Bsrc, B_C, B_decb, B_ps[5 + c]], writes=[B_C])
                    P.op("act", lambda e: e.copy(Cbf[:, 0:512], Cst[:, 0:512]), reads=[B_C], writes=[B_Cbfh[0]])
                    P.op("act", lambda e: e.copy(Cbf[:, 512:1024], Cst[:, 512:1024]), reads=[B_C], writes=[B_Cbfh[1]])
                    P.op("dve", lambda e: e.scalar_tensor_tensor(out=nst[:], in0=nst[:], scalar=dc, in1=STb[r][:, 130:132],
                                                                 op0=ALU.mult, op1=ALU.add), reads=[B_n, B_decb, B_STb[r]], writes=[B_n])
                    P.op("dve", lambda e: e.tensor_copy(nbf[:], nst[:]), reads=[B_n], writes=[B_nbf])
                    if t == NT - 2:
                        P.dma("sp", pC[h * 256:(h + 1) * 256, :].rearrange("(c p) e -> p c e", p=128),
                              Cst[:].rearrange("p (c e) -> p c e", c=2), B_C.dsem, reads=[B_C])
                        P.op("dve", lambda e: e.tensor_copy(pnT[:, 2 * h:2 * h + 2], nst[:]), reads=[B_n], writes=[B_pnT])

                def m_back1(h, s, t):
                    r = t % 2
                    c = m_cols(h, t)
                    bsrc, Bb = (b7f[:, :], B7) if t == NT - 1 else (pst[4][:], B_ps[4])
                    P.op("act", lambda e: e.activation(tmpB[r][:], bsrc, AF.Copy, scale=c["iw"]), reads=[Bb, B_tokc], writes=[B_tmpB[r]])
                    P.op("dve", lambda e: e.tensor_add(num[r][:], tmpB[r][:], Ab[r][:]), reads=[B_tmpB[r], B_Ab[r]], writes=[B_num[r]])
                    P.op("act", lambda e: e.activation(junk[:], num[r][:], AF.Square, accum_out=smc[r][:, 0:1]), reads=[B_num[r]],
                         writes=[B_smc[r], B_junk])
                    P.op("dve", lambda e: e.tensor_copy(smc[r][:, 1:3], STb[r][:, 128:130]), reads=[B_STb[r], B_smc[r]], writes=[B_smc[r]])
                    P.op("dve", lambda e: e.scalar_tensor_tensor(out=smc[r][:, 3:4], in0=smc[r][:, 2:3], scalar=c["iw"], in1=smc[r][:, 1:2],
                                                                 op0=ALU.mult, op1=ALU.add), reads=[B_smc[r], B_tokc], writes=[B_smc[r]])
                    P.op("dve", lambda e: e.tensor_mul(smc[r][:, 4:5], smc[r][:, 3:4], smc[r][:, 3:4]), reads=[B_smc[r]], writes=[B_smc[r]])
                    P.op("dve", lambda e: e.tensor_scalar(smc[r][:, 4:5], smc[r][:, 4:5], c["em2"], EPS, ALU.max, ALU.mult),
                         reads=[B_smc[r], B_tokc], writes=[B_smc[r]])
                    P.op("dve", lambda e: e.scalar_tensor_tensor(out=smc[r][:, 6:7], in0=smc[r][:, 0:1], scalar=1.0 / DVM, in1=smc[r][:, 4:5],
                                                                 op0=ALU.mult, op1=ALU.add), reads=[B_smc[r]], writes=[B_smc[r]])
                    P.op("act", lambda e: e.activation(smc[r][:, 7:8], smc[r][:, 6:7], AF.Ln), reads=[B_smc[r]], writes=[B_smc[r]])
                    P.op("act", lambda e: e.activation(smc[r][:, 9:10], smc[r][:, 7:8], AF.Exp, scale=-0.5), reads=[B_smc[r]], writes=[B_smc[r]])

                def m_back2(h, s, t):
                    r = t % 2
                    out_gated(r, DVM, t, s, h * DVM)

                def m_sample_pre(h, s):
                    t = NT - 1
                    c = m_cols(h, t)
                    for cc in range(2):
                        P.op("dve", lambda e, cc=cc: e.tensor_copy(
                            _diag_view(qTz, cc), qT[s][:, cc * TOK + 1024: cc * TOK + 1152].rearrange("p (j t) -> p j t", j=16)),
                            reads=[B_hd[s]], writes=[B_qTz])
                    P.op("dve", lambda e: e.tensor_scalar_mul(wm[:], bm16[:], c["w"]), reads=[B_tokc, B_const], writes=[B_wm])
                    P.op("dve", lambda e: e.tensor_copy(wmb[:], wm[:]), reads=[B_wm], writes=[B_wm])
                    nu = STb[1][:, 160:192]
                    P.op("pe", lambda e: (e.matmul(nu[:, 0:16], ktm[s][:, t * 256:t * 256 + 128], wmb[:], start=True, stop=True),
                                          e.matmul(nu[:, 16:32], ktm[s][:, t * 256 + 128:t * 256 + 256], wmb[:], start=True, stop=True))[1],
                         reads=[B_hd[s], B_wm], writes=[B_STb[1]])
                    v_o = snT[:].rearrange("p (j h c) -> p j h c", j=16, h=HM)[:, :, h, :]
                    v_i = n0T[:].rearrange("p (j h c) -> p j h c", j=16, h=HM)[:, :, h, :]
                    v_d = decb[:, h * 24 + 8: h * 24 + 24].unsqueeze(2).to_broadcast([128, 16, 2])
                    v_u = nu.rearrange("p (c j) -> p j c", c=2)
                    P.op("dve", lambda e: e.tensor_mul(v_o, v_i, v_d), reads=[B_n0, B_decb, B_snT], writes=[B_snT])
                    P.op("dve", lambda e: e.tensor_add(v_o, v_o, v_u), reads=[B_STb[1], B_snT], writes=[B_snT])

                c0slot = {}

                def m_sample_load(h, j):
                    k4 = cnt["c0f"] % 4
                    cnt["c0f"] += 1
                    c0slot[j] = k4
                    r0 = (j * HM + h) * 256
                    P.dma("sp", C0f[k4][:].rearrange("p (c e) -> p c e", c=2),
                          C0[r0:r0 + 256, :].rearrange("(c p) e -> p c e", p=128), B_C0f[k4].dsem, writes=[B_C0f[k4]])
                    P.dma("pool", C0b[k4][:].rearrange("p (c e) -> p c e", c=2),
                          C0[r0:r0 + 256, :].rearrange("(c p) e -> p c e", p=128), B_C0b[k4].dsem, writes=[B_C0b[k4]])

                def m_sample_j(h, s, j):
                    t = NT - 1
                    k4 = c0slot[j]
                    i2 = k4
                    iw_ = 2 + cnt["wv"] % 2
                    cnt["wv"] += 1
                    r0 = (j * HM + h) * 256

                    def fint(e):
                        for c_ in range(2):
                            ins = e.matmul(b7f[:, :], qTz[:, (c_ * 16 + j) * 128:(c_ * 16 + j + 1) * 128], C0b[i2][:, c_ * 512:(c_ + 1) * 512],
                                           start=(j == 0 and c_ == 0), stop=(j == 15 and c_ == 1))
                        return ins
                    P.op("pe", fint, reads=[B_qTz, B_C0b[i2]], writes=[B7])
                    P.op("dve", lambda e: e.tensor_scalar_mul(wvb[iw_][:], vtm[s][:, t * 512:(t + 1) * 512], wm[:, j:j + 1]),
                         reads=[B_hd[s], B_wm], writes=[B_wvb[iw_]])

                    def fupd(e):
                        for c_ in range(2):
                            ins = e.matmul(pst[5 + c_][:], ktm[s][:, t * 256 + c_ * 128:t * 256 + (c_ + 1) * 128], wvb[iw_][:],
                                           start=True, stop=True)
                        return ins
                    P.op("pe", fupd, reads=[B_hd[s], B_wvb[iw_]], writes=[B_ps[5], B_ps[6]])
                    dc = decb[:, h * 24 + 8 + j: h * 24 + 9 + j]
                    for c_ in range(2):
                        P.op("dve", lambda e, c_=c_: e.scalar_tensor_tensor(
                            out=C0f[k4][:, c_ * 512:(c_ + 1) * 512], in0=C0f[k4][:, c_ * 512:(c_ + 1) * 512], scalar=dc,
                            in1=pst[5 + c_][:], op0=ALU.mult, op1=ALU.add), reads=[B_C0f[k4], B_decb, B_ps[5 + c_], B_C0b[i2]],
                            writes=[B_C0f[k4]])
                    P.dma("sp", sC[r0:r0 + 256, :].rearrange("(c p) e -> p c e", p=128),
                          C0f[k4][:].rearrange("p (c e) -> p c e", c=2), B_C0f[k4].dsem, reads=[B_C0f[k4]])

                def m_sample_post(h, s):
                    for c_ in range(2):
                        nv = n0T[:].rearrange("p (j h c) -> p j h c", j=16, h=HM)[:, :, h, c_].unsqueeze(2).to_broadcast([128, 16, 8])
                        P.op("dve", lambda e, c_=c_, nv=nv: e.tensor_tensor(
                            prod[c_][:].rearrange("p (j t) -> p j t", j=16),
                            qT[s][:, c_ * TOK + 1024: c_ * TOK + 1152].rearrange("p (j t) -> p j t", j=16), nv, ALU.mult),
                            reads=[B_hd[s], B_n0], writes=[B_prod[c_]])
                    P.op("pe", lambda e: (e.matmul(STb[0][:, 129:130], prod[0][:], onesf[:, 0:1], start=True, stop=False),
                                          e.matmul(STb[0][:, 129:130], prod[1][:], onesf[:, 0:1], start=False, stop=True))[1],
                         reads=[B_prod[0], B_prod[1], B_const], writes=[B_STb[0]])

                def load_cin_m(h):
                    P.dma("sp", Cin[h % 2][:].rearrange("p (c e) -> p c e", c=2),
                          s_stC[h * 256:(h + 1) * 256, :].rearrange("(c p) e -> p c e", p=128), B_Cin[h % 2].dsem, writes=[B_Cin[h % 2]])

                def load_cin_r(h):
                    P.dma("sp", Cin[h % 2][:, 0:256], s_stS[h * 128:(h + 1) * 128, :], B_Cin[h % 2].dsem, writes=[B_Cin[h % 2]])
                load_cin_m(0)
                load_head_m(0, 0)
                P.dma("sp", negGb[0][:], s_negG[0:1, :].to_broadcast([128, TOK]), B_negGb[0].dsem, writes=[B_negGb[0]])
                for h in range(HM):
                    s = h % 2
                    if h + 1 < HM:
                        load_cin_m(h + 1)
                        load_head_m(h + 1, (h + 1) % 2)
                    else:
                        load_cin_r(0)
                        load_head_r(0, (h + 1) % 2)
                    hp = h % 2
                    P.op("act", lambda e, hp=hp: e.copy(Cbf[:, 0:512], Cin[hp][:, 0:512]), reads=[B_Cin[hp]], writes=[B_Cbfh[0]])
                    P.op("act", lambda e, hp=hp: e.copy(Cbf[:, 512:1024], Cin[hp][:, 512:1024]), reads=[B_Cin[hp]], writes=[B_Cbfh[1]])
                    P.op("dve", lambda e, h=h: e.tensor_copy(nst[:], npre[:, 2 * h:2 * h + 2]), reads=[B_npre], writes=[B_n])
                    P.op("dve", lambda e: e.tensor_copy(nbf[:], nst[:]), reads=[B_n], writes=[B_nbf])
                    nb_ = h % 2
                    P.op("dve", lambda e, nb_=nb_: e.tensor_tensor(
                        ex_all[:, 0:1024].rearrange("p (t l) -> p t l", t=8), negGb[nb_][:, 0:1024].rearrange("p (t l) -> p t l", t=8),
                        cbp[:].unsqueeze(1).to_broadcast([128, 8, 128]), ALU.add), reads=[B_negGb[nb_]], writes=[B_exall])
                    P.op("dve", lambda e, nb_=nb_: e.tensor_add(ex_all[:, 1024:1152], negGb[nb_][:, 1024:1152], cbs[:]),
                         reads=[B_negGb[nb_], B_exall], writes=[B_exall])
                    if h + 1 < HM:
                        P.dma("sp", negGb[1 - nb_][:], s_negG[h + 1:h + 2, :].to_broadcast([128, TOK]), B_negGb[1 - nb_].dsem,
                              writes=[B_negGb[1 - nb_]])
                    wi = {}
                    m_sample_load(h, 0)
                    m_sample_load(h, 1)
                    m_sample_pre(h, s)
                    wi[0] = m_front_a(h, s, 0)
                    m_front_b(h, s, 0)
                    for t in range(NT):
                        if 2 * t + 3 < 16:
                            m_sample_load(h, 2 * t + 2)
                            m_sample_load(h, 2 * t + 3)
                        if t + 1 < NT:
                            wi[t + 1] = m_front_a(h, s, t + 1)
                        if t < NT - 1:
                            m_inter(h, s, t)
                            m_upd_mm(h, s, t, wi[t])
                            if t + 1 < NT:
                                m_front_b(h, s, t + 1)
                            m_upd_ew(h, s, t)
                            m_sample_j(h, s, 2 * t)
                        else:
                            m_sample_post(h, s)
                        if t > 0:
                            m_back2(h, s, t - 1)
                        m_back1(h, s, t)
                        if t < NT - 1:
                            m_sample_j(h, s, 2 * t + 1)
                    m_back2(h, s, NT - 1)

                P.op("pe", lambda e: e.matmul(ps_ST, snT[:], identf[:], start=True, stop=True), reads=[B_snT, B_const], writes=[B_ST])
                P.op("dve", lambda e: e.tensor_copy(n0r[:], ps_ST), reads=[B_ST, B_n0], writes=[B_n0])
                P.dma("sp", sn, n0r[:], B_n0.dsem, reads=[B_n0])
                P.op("pe", lambda e: e.matmul(ps_Gb[0:8, :], pnT[:], identf[:], start=True, stop=True), reads=[B_pnT, B_const],
                     writes=[B7])
                P.op("dve", lambda e: e.tensor_copy(pnst[:], ps_Gb[0:8, :]), reads=[B7], writes=[B_pnst])
                P.dma("sp", pn, pnst[:], B_pnst.dsem, reads=[B_pnst])

                def r_front_a(h, s, t):
                    r = t % 2
                    smp = (t == NT - 1)
                    st_ap = STb[r][:, 0:128]
                    P.op("pe", lambda e: e.matmul(st_ap, kT[s][:, t * 128:(t + 1) * 128], qT[s][:, t * 128:(t + 1) * 128], start=True, stop=True),
                         reads=[B_hd[s]], writes=[B_STb[r]])
                    Dm = Ds if smp else Dp
                    P.op("dve", lambda e: e.tensor_mul(sT[r][:], st_ap, Dm[:]), reads=[B_STb[r], B_rc], writes=[B_sT[r]])
                    if not smp:
                        i = t % 2
                        P.op("dve", lambda e: e.tensor_scalar_mul(wvb[i][:, 0:256], vtm[s][:, t * 256:(t + 1) * 256], rcol[:, 1:2]),
                             reads=[B_hd[s], B_rc], writes=[B_wvb[i]])
                        return i
                    return None

                def r_front_b(h, s, t):
                    r = t % 2
                    P.op("pe", lambda e: e.matmul(Ab[r][:, 0:256], sT[r][:], vtm[s][:, t * 256:(t + 1) * 256], start=True, stop=True),
                         reads=[B_sT[r], B_hd[s]], writes=[B_Ab[r]])

                def r_inter(h, s, t):
                    P.op("pe", lambda e: e.matmul(pst[4][:, 0:256], qT[s][:, t * 128:(t + 1) * 128], Cbf[:, 0:256], start=True, stop=True),
                         reads=[B_hd[s], B_Cbf], writes=[B_ps[4]])

                def r_upd(h, s, t, i, sdec_p):
                    P.op("pe", lambda e: e.matmul(pst[5][:, 0:256], ktm[s][:, t * 128:(t + 1) * 128], wvb[i][:, 0:256], start=True, stop=True),
                         reads=[B_hd[s], B_wvb[i]], writes=[B_ps[5]])
                    csrc, Bsrc = (Cin[h % 2], B_Cin[h % 2]) if t == 0 else (Cst, B_C)
                    P.op("dve", lambda e: e.scalar_tensor_tensor(out=Cst[:, 0:256], in0=csrc[:, 0:256], scalar=sdec_p, in1=pst[5][:, 0:256],
                                                                 op0=ALU.mult, op1=ALU.add), reads=[Bsrc, B_C, B_ps[5]], writes=[B_C])
                    P.op("act", lambda e: e.copy(Cbf[:, 0:256], Cst[:, 0:256]), reads=[B_C], writes=[B_Cbf])
                    if t == NT - 2:
                        P.dma("sp", pS[h * 128:(h + 1) * 128, :], Cst[:, 0:256], B_C.dsem, reads=[B_C])

                def r_back1(h, s, t):
                    r = t % 2
                    smp = (t == NT - 1)
                    ic = rcol[:, 2:3] if smp else rcol[:, 0:1]
                    bsrc, Bb = (b7f[:, 0:256], B7) if smp else (pst[4][:, 0:256], B_ps[4])
                    P.op("act", lambda e: e.activation(tmpB[r][:, 0:256], bsrc, AF.Copy, scale=ic), reads=[Bb, B_rc],
                         writes=[B_tmpB[r]])
                    P.op("dve", lambda e: e.tensor_add(num[r][:, 0:256], tmpB[r][:, 0:256], Ab[r][:, 0:256]), reads=[B_tmpB[r], B_Ab[r]],
                         writes=[B_num[r]])
                    P.op("act", lambda e: e.activation(junk[:, 0:256], num[r][:, 0:256], AF.Square, accum_out=smc[r][:, 0:1]), reads=[B_num[r]],
                         writes=[B_smc[r], B_junk])
                    P.op("dve", lambda e: e.tensor_scalar(smc[r][:, 6:7], smc[r][:, 0:1], 1.0 / DVR, EPS, ALU.mult, ALU.add), reads=[B_smc[r]],
                         writes=[B_smc[r]])
                    P.op("act", lambda e: e.activation(smc[r][:, 7:8], smc[r][:, 6:7], AF.Ln), reads=[B_smc[r]], writes=[B_smc[r]])
                    P.op("act", lambda e: e.activation(smc[r][:, 9:10], smc[r][:, 7:8], AF.Exp, scale=-0.5), reads=[B_smc[r]], writes=[B_smc[r]])

                def r_back2(h, s, t):
                    r = t % 2
                    out_gated(r, DVR, t, s, HM * DVM + h * DVR)

                def r_sample_pre(h, s):
                    P.op("dve", lambda e: e.tensor_copy(_diag_view(qTz, 0), qT[s][:, 1024:1152].rearrange("p (j t) -> p j t", j=16)),
                         reads=[B_hd[s]], writes=[B_qTz])

                def r_sample_load(h, j):
                    k4 = cnt["c0f"] % 4
                    cnt["c0f"] += 1
                    c0slot[j] = k4
                    r0 = (j * HR + h) * 128
                    P.dma("sp", C0f[k4][:, 0:256], S0[r0:r0 + 128, :], B_C0f[k4].dsem, writes=[B_C0f[k4]])
                    P.dma("pool", C0b[k4][:, 0:256], S0[r0:r0 + 128, :], B_C0b[k4].dsem, writes=[B_C0b[k4]])

                def r_sample_j(h, s, j, sdec_s):
                    t = NT - 1
                    k4 = c0slot[j]
                    i2 = k4
                    iw_ = 2 + cnt["wv"] % 2
                    cnt["wv"] += 1
                    r0 = (j * HR + h) * 128
                    P.op("pe", lambda e: e.matmul(b7f[:, 0:256], qTz[:, j * 128:(j + 1) * 128], C0b[i2][:, 0:256],
                                                  start=(j == 0), stop=(j == 15)), reads=[B_qTz, B_C0b[i2]], writes=[B7])
                    P.op("dve", lambda e: e.tensor_scalar_mul(wvb[iw_][:, 0:256], vtm[s][:, t * 256:(t + 1) * 256],
                                                              kd16[:, j:j + 1]), reads=[B_hd[s], B_rc], writes=[B_wvb[iw_]])
                    P.op("pe", lambda e: e.matmul(pst[6][:, 0:256], ktm[s][:, t * 128:(t + 1) * 128], wvb[iw_][:, 0:256],
                                                  start=True, stop=True), reads=[B_hd[s], B_wvb[iw_]], writes=[B_ps[6]])
                    P.op("dve", lambda e: e.scalar_tensor_tensor(
                        out=C0f[k4][:, 0:256], in0=C0f[k4][:, 0:256], scalar=sdec_s, in1=pst[6][:, 0:256], op0=ALU.mult,
                        op1=ALU.add), reads=[B_C0f[k4], B_ps[6], B_C0b[i2]], writes=[B_C0f[k4]])
                    P.dma("sp", sS[r0:r0 + 128, :], C0f[k4][:, 0:256], B_C0f[k4].dsem, reads=[B_C0f[k4]])

                for h in range(HR):
                    s = (HM + h) % 2
                    if h + 1 < HR:
                        load_cin_r(h + 1)
                        load_head_r(h + 1, (HM + h + 1) % 2)
                    lg = math.log(1.0 - 2.0 ** (-5.0 - h))
                    P.op("act", lambda e, lg=lg: e.activation(Dp[:], rdiff[:], AF.Exp, scale=lg), reads=[B_const, B_sT[0], B_sT[1]], writes=[B_rc])
                    P.op("dve", lambda e: e.tensor_mul(Ds[:], Dp[:], scausal[:]), reads=[B_rc, B_const], writes=[B_rc])
                    P.op("dve", lambda e: e.tensor_mul(Dp[:], Dp[:], causal[:]), reads=[B_rc, B_const], writes=[B_rc])
                    for k_ in range(4):
                        P.op("act", lambda e, k_=k_, lg=lg: e.activation(rcol[:, k_:k_ + 1], pcols[:, k_:k_ + 1], AF.Exp, scale=lg),
                             reads=[B_const, B_rc], writes=[B_rc])
                    P.op("dve", lambda e: e.tensor_scalar_mul(kd16[:], bm16[:], rcol[:, 3:4]), reads=[B_rc, B_const], writes=[B_rc])
                    sdec_p = math.exp(lg * 128.0)
                    sdec_s = math.exp(lg * 8.0)
                    hp = h % 2
                    P.op("act", lambda e, hp=hp: e.copy(Cbf[:, 0:256], Cin[hp][:, 0:256]), reads=[B_Cin[hp]], writes=[B_Cbf])
                    wi = {}
                    r_sample_load(h, 0)
                    r_sample_load(h, 1)
                    r_sample_pre(h, s)
                    wi[0] = r_front_a(h, s, 0)
                    r_front_b(h, s, 0)
                    for t in range(NT):
                        if 2 * t + 3 < 16:
                            r_sample_load(h, 2 * t + 2)
                            r_sample_load(h, 2 * t + 3)
                        if t + 1 < NT:
                            wi[t + 1] = r_front_a(h, s, t + 1)
                        if t < NT - 1:
                            r_inter(h, s, t)
                            if t + 1 < NT:
                                r_front_b(h, s, t + 1)
                            r_upd(h, s, t, wi[t], sdec_p)
                            r_sample_j(h, s, 2 * t, sdec_s)
                        if t > 0:
                            r_back2(h, s, t - 1)
                        r_back1(h, s, t)
                        if t < NT - 1:
                            r_sample_j(h, s, 2 * t + 1, sdec_s)
                    r_back2(h, s, NT - 1)
                P.end_phase(_mk974)

        if want("wout"):
            with ExitStack() as ph:
                _mk1445 = P.mark()
                g = alloc_gemm(ph, TOK)
                loadt(g, s_cat, NT, False, None)
                alloc_w(ph, g)
                xr = Stage(ph, "xr", 3, 512, F32)
                wn = load_w(g, [w_out[:, 0:512]], 32)
                for j in range(8):
                    w = wn
                    if j + 1 < 8:
                        wn = load_w(g, [w_out[:, (j + 1) * 512:(j + 2) * 512]], 32)
                    ld = {}

                    def issue_x(t, j=j):
                        xt_, Bx = xr.nxt()
                        P.dma("act", xt_[:], xm[t * 128:(t + 1) * 128, j * 512:(j + 1) * 512], Bx.dsem, writes=[Bx])
                        ld[t] = (xt_, Bx)
                    issue_x(0)
                    issue_x(1)
                    for t in range(NT):
                        xt_, Bx = ld.pop(t)
                        ps, B = mm_tm(g, w, t, 0, 512)
                        P.op("dve", lambda e, xt_=xt_, ps=ps: e.tensor_add(xt_[:], xt_[:], ps), reads=[B, Bx], writes=[Bx])
                        P.dma("sp", s_yacc[t * 128:(t + 1) * 128, j * 512:(j + 1) * 512], xt_[:], Bx.dsem, reads=[Bx])
                        if t + 2 < NT:
                            issue_x(t + 2)
                P.end_phase(_mk1445)

        if want("ffn"):
            with ExitStack() as ph:
                _mk1476 = P.mark()
                g = alloc_gemm(ph, TOK)
                loadt(g, s_yacc, NT, True, g_ffn)
                alloc_w(ph, g)
                aT = sb(ph, "aT", [128, 16 * TOK], BF16)
                B_aT = [[P.buf("aT%d_%d" % (i, b)) for b in range(3)] for i in range(16)]
                rl = Stage(ph, "rl", 2, 384, F32)
                ya = Stage(ph, "ya", 4, 512, F32)
                B_y = [[P.buf("y%d_%d" % (t, c)) for c in range(8)] for t in range(NT)]
                NG = DFF // 2048
                seq = []
                for gi in range(NG):
                    for i in range(4):
                        seq.append(("up", gi, i))
                    for i in range(4):
                        seq.append(("dn", gi, i))

                def wsrc(job):
                    kind, gi, i = job
                    if kind == "up":
                        return [w_up[:, gi * 2048 + i * 512: gi * 2048 + (i + 1) * 512]], 32
                    return [w_down[gi * 2048:(gi + 1) * 2048, i * 1024:(i + 1) * 1024]], 16
                a, b_ = wsrc(seq[0])
                wn = load_w(g, a, b_)
                for si, job in enumerate(seq):
                    kind, gi, i = job
                    w = wn
                    if si + 1 < len(seq):
                        a, b_ = wsrc(seq[si + 1])
                        wn = load_w(g, a, b_)
                    if kind == "up":
                        for sbk in range(4):
                            fc = i * 4 + sbk
                            for b in range(3):
                                ps, B = mm_fm(g, w, sbk * 128, 128, b * 384, 384)
                                r_, Br = rl.nxt()
                                P.op("act", lambda e, r_=r_, ps=ps: e.activation(r_[:], ps, AF.Relu), reads=[B], writes=[Br])
                                P.op("dve", lambda e, r_=r_, fc=fc, b=b: e.tensor_mul(aT[:, fc * TOK + b * 384: fc * TOK + (b + 1) * 384],
                                                                                     r_[:], r_[:]), reads=[Br], writes=[B_aT[fc][b]])
                    else:
                        blocks = [(cb, t) for cb in range(2) for t in range(NT)]
                        loaded = {}

                        def issue(n, i=i):
                            cb, t = blocks[n]
                            c0 = i * 1024 + cb * 512
                            yt, By = ya.nxt()
                            P.dma("act", yt[:], s_yacc[t * 128:(t + 1) * 128, c0:c0 + 512], By.dsem, reads=[B_y[t][c0 // 512]], writes=[By])
                            loaded[n] = (yt, By)
                        for n in range(min(3, len(blocks))):
                            issue(n)
                        for n, (cb, t) in enumerate(blocks):
                            c0 = i * 1024 + cb * 512
                            yt, By = loaded.pop(n)
                            ps, B = mm_tm(g, w, t, cb * 512, 512,
                                          xsrc=(lambda kc, t=t: aT[:, kc * TOK + t * 128: kc * TOK + (t + 1) * 128]),
                                          xbufs=[B_aT[fc][t // 3] for fc in range(16)])
                            P.op("dve", lambda e, yt=yt, ps=ps: e.tensor_add(yt[:], yt[:], ps), reads=[B, By], writes=[By])
                            P.dma("sp", s_yacc[t * 128:(t + 1) * 128, c0:c0 + 512], yt[:], By.dsem, reads=[By], writes=[B_y[t][c0 // 512]])
                            if n + 3 < len(blocks):
                                issue(n + 3)
                P.end_phase(_mk1476)

        if want("fin"):
            with ExitStack() as ph:
                _mk1543 = P.mark()
                xs = [sb(ph, "fx%d" % i, [128, D], F32) for i in range(2)]
                jb = sb(ph, "fj", [128, D], BF16)
                gb = sb(ph, "fgb", [128, D], F32)
                st_ = sb(ph, "fst", [128, 8], F32)
                B_xs = [P.buf("fx%d" % i, dma=True) for i in range(2)]
                B_gb = P.buf("fgb", dma=True)
                B_st = [P.buf("fst0"), P.buf("fst1")]
                B_jb = P.buf("fj")
                P.dma("sp", gb[:], g_fin.to_broadcast([128, D]), B_gb.dsem, writes=[B_gb])
                for t in range(NT):
                    s = t % 2
                    c = s * 4
                    P.dma("sp", xs[s][:], s_yacc[t * 128:(t + 1) * 128, :], B_xs[s].dsem, writes=[B_xs[s]])
                    P.op("act", lambda e, s=s, c=c: e.activation(jb[:], xs[s][:], AF.Square, accum_out=st_[:, c:c + 1]),
                         reads=[B_xs[s]], writes=[B_jb, B_st[s]])
                    P.op("act", lambda e, c=c: e.activation(st_[:, c + 2:c + 3], st_[:, c:c + 1], AF.Ln, scale=1.0 / D, bias=EPS), reads=[B_st[s]],
                         writes=[B_st[s]])
                    P.op("act", lambda e, c=c: e.activation(st_[:, c + 3:c + 4], st_[:, c + 2:c + 3], AF.Exp, scale=-0.5), reads=[B_st[s]],
                         writes=[B_st[s]])
                    P.op("dve", lambda e, s=s, c=c: e.scalar_tensor_tensor(out=xs[s][:], in0=xs[s][:], scalar=st_[:, c + 3:c + 4], in1=gb[:],
                                                                           op0=ALU.mult, op1=ALU.mult),
                         reads=[B_xs[s], B_st[s], B_gb], writes=[B_xs[s]])
                    P.dma("sp", y[t * 128:(t + 1) * 128, :], xs[s][:], B_xs[s].dsem, reads=[B_xs[s]])
                P.end_phase(_mk1543)
        P.barrier()
        P.emit()
        if LATE:
            raise RuntimeError("late binding bugs:\n" + "\n".join(sorted(set(LATE))))
        P.stats = {e: len(P.q[e]) for e in P.ENG}
        nc._prog_stats = (P.stats, {e: P.esem[e].count for e in P.ENG}, len(P.sems))
    return nc


def _diag_view(qTz, c):
    base = qTz[:, c * 16 * 128:(c + 1) * 16 * 128]
    a = base.ap
    return bass.AP(base.tensor, base.offset, [list(a[0]), [a[-1][0] * 136, 16], [a[-1][0], 8]])


def _mk_dsem(P, B):
    B.dsem = P.new_sem("dsx%d" % len(P.sems))
    return B.dsem


_NC_CACHE = {}


def _get_nc():
    if "nc" not in _NC_CACHE:
        _NC_CACHE["nc"] = build_program()
    return _NC_CACHE["nc"]


def make_in_maps(inputs, cores=range(8)):
    f = lambda a: np.ascontiguousarray(np.asarray(a, dtype=np.float32))
    x_prompt = f(inputs["x_prompt"])
    x_sample = f(inputs["x_sample"])
    sCs = f(inputs["state_mlstm_C"])[0]
    sns = f(inputs["state_mlstm_n"])[0]
    sms = f(inputs["state_mlstm_m"])[0]
    sSs = f(inputs["state_ret_S"])[0]
    shared = {
        "w_in": f(inputs["w_in"])[0], "w_out": f(inputs["w_out"])[0], "w_up": f(inputs["w_up"])[0],
        "w_down": f(inputs["w_down"])[0],
        "b_ig": f(inputs["b_igate"]).reshape(HM, 1), "b_fg": f(inputs["b_fgate"]).reshape(HM, 1),
        "g_mh": f(inputs["g_mlstm_head"])[0], "g_rh": f(inputs["g_ret_head"])[0],
        "g_mix": f(inputs["g_norm_mix"]).reshape(1, D), "g_ffn": f(inputs["g_norm_ffn"]).reshape(1, D),
        "g_fin": f(inputs["g_final"]).reshape(1, D),
    }
    freqs = (np.float32(10000.0) ** (-np.arange(0, DKR, 2, dtype=np.float32) / np.float32(DKR))).astype(np.float32)
    frq = np.ascontiguousarray(np.broadcast_to(freqs[None, :], (128, 64))).astype(np.float32)
    p = np.arange(128, dtype=np.float32)
    maps = []
    for c in cores:
        b, half = c // 2, c % 2
        xmain = np.concatenate([x_prompt[b, half * 1024:(half + 1) * 1024, :],
                                x_sample[c * 16:(c + 1) * 16].reshape(128, D)], axis=0)
        xpre = x_prompt[b, 0:1024, :]
        aux = np.zeros((128, 16), np.float32)
        aux[:, 0] = float(half)
        for t in range(8):
            aux[:, 1 + t] = half * 1024 + t * 128 + p
        aux[:, 9] = 16384.0 + (p % 8)
        aux[:, 10] = p
        aux[:, 11] = p % 8
        auxp = np.zeros((128, 8), np.float32)
        for t in range(8):
            auxp[:, t] = t * 128 + p
        m = dict(shared)
        m.update({
            "xm": np.ascontiguousarray(xmain), "xp": np.ascontiguousarray(xpre),
            "C0": np.ascontiguousarray(sCs[c * 16:(c + 1) * 16].reshape(16 * HM * DKM, DVM)),
            "n0": np.ascontiguousarray(sns[c * 16:(c + 1) * 16].reshape(128, 128)),
            "m0": np.ascontiguousarray(sms[c * 16:(c + 1) * 16]),
            "S0": np.ascontiguousarray(sSs[c * 16:(c + 1) * 16].reshape(16 * HR * DKR, DVR)),
            "aux": aux, "frq": frq, "auxp": auxp,
        })
        maps.append(m)
    return maps


def kernel(**inputs):
    nc = _get_nc()
    maps = make_in_maps(inputs)
    res = run_bass_kernel_spmd(nc, maps, core_ids=list(range(8))).results
    B = 4
    y_prompt = np.zeros((B, 2048, D), np.float32)
    y_sample = np.zeros((128, 8, D), np.float32)
    pC = np.zeros((1, B, HM, DKM, DVM), np.float32)
    pn = np.zeros((1, B, HM, DKM), np.float32)
    pm = np.zeros((1, B, HM), np.float32)
    pS = np.zeros((1, B, HR, DKR, DVR), np.float32)
    sC = np.zeros((1, 128, HM, DKM, DVM), np.float32)
    sn = np.zeros((1, 128, HM, DKM), np.float32)
    sm = np.zeros((1, 128, HM), np.float32)
    sS = np.zeros((1, 128, HR, DKR, DVR), np.float32)
    for c in range(8):
        r = res[c]
        b, half = c // 2, c % 2
        y_prompt[b, half * 1024:(half + 1) * 1024] = r["y"][0:1024]
        y_sample[c * 16:(c + 1) * 16] = r["y"][1024:1152].reshape(16, 8, D)
        if half == 1:
            pC[0, b] = r["pC"].reshape(HM, DKM, DVM)
            pn[0, b] = r["pn"].reshape(HM, DKM)
            pm[0, b] = r["pm"].reshape(HM)
            pS[0, b] = r["pS"].reshape(HR, DKR, DVR)
        sC[0, c * 16:(c + 1) * 16] = r["sC"].reshape(16, HM, DKM, DVM)
        sn[0, c * 16:(c + 1) * 16] = r["sn"].reshape(16, HM, DKM)
        sm[0, c * 16:(c + 1) * 16] = r["sm"]
        sS[0, c * 16:(c + 1) * 16] = r["sS"].reshape(16, HR, DKR, DVR)
    return (y_prompt, y_sample, pC, pn, pm, pS, sC, sn, sm, sS)
```

```python
import math
from contextlib import ExitStack
import numpy as np
import concourse.bass as bass
import concourse.mybir as mybir
from concourse.bass_utils import run_bass_kernel_spmd

F32 = mybir.dt.float32
BF16 = mybir.dt.bfloat16
I32 = mybir.dt.int32
ALU = mybir.AluOpType
AF = mybir.ActivationFunctionType

D = 4096
NT = 9
NPRE = 8
TOK = NT * 128
PTOK = NPRE * 128
HM, DKM, DVM = 4, 256, 512
HR, DKR, DVR = 8, 128, 256
DFF = 16384
INC = 12296
QM, KM, VM, OM, IG, FG, QR, KR, VR, GR = 0, 1024, 2048, 4096, 6144, 6148, 6152, 7176, 8200, 10248
EPS = 1e-6
PI = math.pi


class Sem:
    def __init__(self, h):
        self.h = h
        self.count = 0


class Buf:
    def __init__(self, name, dsem=None):
        self.name = name
        self.wr = []
        self.rd = []
        self.dsem = dsem


class Prog:
    ENG = ("pe", "act", "dve", "pool", "sp")

    def __init__(self, nc, stack):
        self.nc = nc
        self.stack = stack
        self.q = {e: [] for e in self.ENG}
        self.sems = []
        self.free_dsems = []
        self.dsem_log = []
        self.esem = {e: self.new_sem("es_" + e) for e in self.ENG}
        self.seen = {e: {} for e in self.ENG}
        self.nins = {e: 0 for e in self.ENG}

    def new_sem(self, name):
        s = Sem(self.stack.enter_context(self.nc.semaphore(name)))
        self.sems.append(s)
        return s

    def buf(self, name, dma=False, fresh=False):
        b = Buf(name)
        if dma and fresh:
            b.dsem = self.new_sem("dq%d" % len(self.sems))
        elif dma:
            if self.free_dsems:
                b.dsem = self.free_dsems.pop()
            else:
                b.dsem = self.new_sem("ds%d" % len(self.sems))
            self.dsem_log.append(b.dsem)
        return b

    def mark(self):
        return len(self.dsem_log)

    def end_phase(self, mark):
        self.barrier()
        self.free_dsems.extend(self.dsem_log[mark:])
        del self.dsem_log[mark:]

    def _waits(self, eng, evs):
        w = {}
        for s, v in evs:
            if v > w.get(s, (s, 0))[1]:
                w[s] = (s, v)
        out = []
        seen = self.seen[eng]
        for s, v in w.values():
            if seen.get(s, 0) >= v:
                continue
            seen[s] = v
            out.append((s, v))
        return out

    def _deps(self, reads, writes):
        evs = []
        for b in reads:
            evs += b.wr
        for b in writes:
            evs += b.wr
            evs += b.rd
        return evs

    def op(self, eng, fn, reads=(), writes=(), skip_self=False):
        deps = self._deps(reads, writes)
        if skip_self:
            deps = [ev for ev in deps if ev[0] is not self.esem[eng]]
        waits = self._waits(eng, deps)
        sem = self.esem[eng]
        sem.count += 1
        ev = (sem, sem.count)
        snap = _snap(fn)
        import traceback
        where = traceback.extract_stack(limit=2)[0]

        def run(e, waits=waits, fn=fn, sem=sem, snap=snap, where=where):
            cur = _snap(fn)
            for k_, (a, b) in enumerate(zip(snap, cur)):
                if a is not b:
                    LATE.append("late-binding %s:%s var=%s" % (where.filename.split("/")[-1], where.lineno, fn.__code__.co_freevars[k_]))
                    return
            for s, v in waits:
                e.wait_ge(s.h, v)
            fn(e).then_inc(sem.h, 1)
        self.q[eng].append(run)
        for b in reads:
            b.rd.append(ev)
        for b in writes:
            b.wr = [ev]
            b.rd = []
        return ev

    def dma(self, eng, out_ap, in_ap, dsem, reads=(), writes=(), **kw):
        evs = []
        for b in reads:
            evs += b.wr
        for b in writes:
            evs += [ev for ev in b.wr if ev[0] is not dsem]
            evs += b.rd
        waits = self._waits(eng, evs)
        dsem.count += 16
        ev = (dsem, dsem.count)

        def run(e, waits=waits, dsem=dsem, out_ap=out_ap, in_ap=in_ap, kw=kw):
            for s, v in waits:
                e.wait_ge(s.h, v)
            e.dma_start(out=out_ap, in_=in_ap, **kw).then_inc(dsem.h, 16)
        self.q[eng].append(run)
        for b in reads:
            b.rd.append(ev)
        for b in writes:
            if b.wr and all(s is dsem for s, _ in b.wr) and not b.rd:
                b.wr.append(ev)
            else:
                b.wr = [ev]
                b.rd = []
        return ev

    def barrier(self):
        evs = [(s, s.count) for s in self.sems if s.count > 0]
        for eng in self.ENG:
            waits = self._waits(eng, evs)
            if not waits:
                continue

            def run(e, waits=waits):
                for s, v in waits:
                    e.wait_ge(s.h, v)
            self.q[eng].append(run)

    def emit(self):
        nc = self.nc
        with nc.Block() as block:
            @block.tensor
            def _(e):
                for f in self.q["pe"]:
                    f(e)

            @block.scalar
            def _(e):
                for f in self.q["act"]:
                    f(e)

            @block.vector
            def _(e):
                for f in self.q["dve"]:
                    f(e)

            @block.gpsimd
            def _(e):
                for f in self.q["pool"]:
                    f(e)

            @block.sync
            def _(e):
                for f in self.q["sp"]:
                    f(e)


LATE = []


def _snap(fn):
    out = []
    for c in (fn.__closure__ or ()):
        try:
            out.append(c.cell_contents)
        except ValueError:
            out.append(None)
    return out


def build_program(phases="all", debug=False):
    nc = bass.Bass("TRN2", target_bir_lowering=False)
    dt_in = lambda name, shape: nc.dram_tensor(name, list(shape), F32, kind="ExternalInput").ap()
    dt_out = lambda name, shape: nc.dram_tensor(name, list(shape), F32, kind="ExternalOutput").ap()
    okind = "ExternalOutput" if debug else None

    def dt_scr(name, shape, dt):
        if debug:
            return nc.dram_tensor(name, list(shape), dt, kind="ExternalOutput").ap()
        return nc.dram_tensor(name, list(shape), dt).ap()

    xm = dt_in("xm", [TOK, D])
    xp = dt_in("xp", [PTOK, D])
    C0 = dt_in("C0", [16 * HM * DKM, DVM])
    n0 = dt_in("n0", [128, 128])
    m0 = dt_in("m0", [16, HM])
    S0 = dt_in("S0", [16 * HR * DKR, DVR])
    w_in = dt_in("w_in", [D, INC])
    w_out = dt_in("w_out", [D, D])
    w_up = dt_in("w_up", [D, DFF])
    w_down = dt_in("w_down", [DFF, D])
    b_ig = dt_in("b_ig", [HM, 1])
    b_fg = dt_in("b_fg", [HM, 1])
    g_mh = dt_in("g_mh", [HM, DVM])
    g_rh = dt_in("g_rh", [HR, DVR])
    g_mix = dt_in("g_mix", [1, D])
    g_ffn = dt_in("g_ffn", [1, D])
    g_fin = dt_in("g_fin", [1, D])
    aux = dt_in("aux", [128, 16])
    frq = dt_in("frq", [128, 64])
    auxp = dt_in("auxp", [128, 8])

    y = dt_out("y", [TOK, D])
    pC = dt_out("pC", [HM * DKM, DVM])
    pn = dt_out("pn", [8, 128])
    pm = dt_out("pm", [HM, 1])
    pS = dt_out("pS", [HR * DKR, DVR])
    sC = dt_out("sC", [16 * HM * DKM, DVM])
    sn = dt_out("sn", [128, 128])
    sm = dt_out("sm", [16, HM])
    sS = dt_out("sS", [16 * HR * DKR, DVR])

    s_qmT = dt_scr("s_qmT", [HM * DKM, TOK], BF16)
    s_kmT = dt_scr("s_kmT", [HM * DKM, TOK], BF16)
    s_km = dt_scr("s_km", [TOK, HM * DKM], BF16)
    s_vm = dt_scr("s_vm", [TOK, HM * DVM], BF16)
    s_gsm = dt_scr("s_gsm", [TOK, HM * DVM], BF16)
    s_qrT = dt_scr("s_qrT", [HR * DKR, TOK], BF16)
    s_krT = dt_scr("s_krT", [HR * DKR, TOK], BF16)
    s_kr = dt_scr("s_kr", [TOK, HR * DKR], BF16)
    s_vr = dt_scr("s_vr", [TOK, HR * DVR], BF16)
    s_gsr = dt_scr("s_gsr", [TOK, HR * DVR], BF16)
    s_kmp = dt_scr("s_kmp", [PTOK, HM * DKM], BF16)
    s_vmp = dt_scr("s_vmp", [PTOK, HM * DVM], BF16)
    s_krp = dt_scr("s_krp", [PTOK, HR * DKR], BF16)
    s_vrp = dt_scr("s_vrp", [PTOK, HR * DVR], BF16)
    s_stC = dt_scr("s_stC", [HM * DKM, DVM], F32)
    s_stS = dt_scr("s_stS", [HR * DKR, DVR], F32)
    s_cat = dt_scr("s_cat", [TOK, D], F32)
    s_negG = dt_scr("s_negG", [4, TOK], F32)
    s_yacc = dt_scr("s_yacc", [TOK, D], F32)

    def want(ph):
        return phases == "all" or ph in phases

    with ExitStack() as st:
        P = Prog(nc, st)
        _uid = [0]

        def sb(stack, name, shape, dt):
            _uid[0] += 1
            return stack.enter_context(nc.sbuf_tensor("%s_%d" % (name, _uid[0]), list(shape), dt))

        pst = [st.enter_context(nc.psum_tensor("ps%d" % i, [128, 512], F32)) for i in range(7)]
        pstb = st.enter_context(nc.psum_tensor("ps7", [128, 1024], BF16))
        B_ps = [P.buf("ps%d" % i) for i in range(7)]
        B7 = P.buf("ps7")
        B_ST = B_ps[2]
        B_Gb = B7
        B_sm = B7
        B_sm2 = B7
        B_tp = [B_ps[6], B7]
        b7f = pstb[:].bitcast(F32)
        ps_ST = pst[2][:, 0:128]
        ps_Gb = b7f[:, 0:128]
        ps_sm = b7f[:, 128:256]
        ps_sm2 = b7f[:, 256:384]
        tp = [pst[6][:].bitcast(BF16)[:, 0:512], pstb[:, 0:512]]
        mmcnt = [0]

        def mmbank():
            i = mmcnt[0] % 2
            mmcnt[0] += 1
            return pst[i], B_ps[i]

        aux_t = sb(st, "aux_t", [128, 16], F32)
        auxp_t = sb(st, "auxp_t", [128, 8], F32)
        identf = sb(st, "identf", [128, 128], F32)
        identb = sb(st, "identb", [128, 128], BF16)
        causal = sb(st, "causal", [128, 128], F32)
        scausal = sb(st, "scausal", [128, 128], F32)
        rdiff = sb(st, "rdiff", [128, 128], F32)
        bm16 = sb(st, "bm16", [128, 16], F32)
        onesf = sb(st, "onesf", [128, 128], F32)
        ones_bf = sb(st, "ones_bf", [128, 1], BF16)
        pcols = sb(st, "pcols", [128, 4], F32)
        flagc = sb(st, "flagc", [128, 1], F32)
        tokc = sb(st, "tokc", [128, NT * 24], F32)
        wcolp = sb(st, "wcolp", [128, NPRE * 4], F32)
        wcolp_bf = sb(st, "wcolp_bf", [128, NPRE * 4], BF16)
        decb = sb(st, "decb", [128, 4 * 24], F32)
        B_aux = P.buf("aux", dma=True)
        B_const = P.buf("const")
        B_tokc = P.buf("tokc")
        B_wcolp = P.buf("wcolp")
        B_decb = P.buf("decb")

        P.dma("sp", aux_t[:], aux, B_aux.dsem, writes=[B_aux])
        P.dma("sp", auxp_t[:], auxp, B_aux.dsem, writes=[B_aux])
        with ExitStack() as ph:
            _mk353 = P.mark()
            di = sb(ph, "di", [128, 128], I32)
            df = sb(ph, "df", [128, 128], F32)
            e16i = sb(ph, "e16i", [16, 128], I32)
            e16 = sb(ph, "e16", [16, 128], F32)
            e16b = sb(ph, "e16b", [16, 128], F32)
            blk = sb(ph, "blk", [128, 128], F32)
            B_t = P.buf("s0tmp")

            P.op("pool", lambda e: e.iota(di[:], [[1, 128]], base=0, channel_multiplier=-1), writes=[B_t])
            P.op("dve", lambda e: e.tensor_copy(df[:], di[:]), reads=[B_t], writes=[B_const])
            P.op("dve", lambda e: e.tensor_single_scalar(identf[:], df[:], 0.0, ALU.is_equal), reads=[B_const], writes=[B_const])
            P.op("dve", lambda e: e.tensor_copy(identb[:], identf[:]), reads=[B_const], writes=[B_const])
            P.op("dve", lambda e: e.tensor_single_scalar(causal[:], df[:], 0.0, ALU.is_ge), reads=[B_const], writes=[B_const])
            P.op("dve", lambda e: e.tensor_scalar_max(rdiff[:], df[:], 0.0), reads=[B_const], writes=[B_const])
            P.op("dve", lambda e: e.memset(onesf[:], 1.0), writes=[B_const])
            P.op("dve", lambda e: e.memset(ones_bf[:], 1.0), writes=[B_const])
            P.op("pool", lambda e: e.iota(e16i[:], [[1, 128]], base=0, channel_multiplier=-8), writes=[B_t])
            P.op("dve", lambda e: e.tensor_copy(e16[:], e16i[:]), reads=[B_t], writes=[B_const])
            P.op("dve", lambda e: e.tensor_single_scalar(e16b[:], e16[:], 0.0, ALU.is_ge), reads=[B_const], writes=[B_const])
            P.op("dve", lambda e: e.tensor_single_scalar(e16[:], e16[:], 8.0, ALU.is_lt), reads=[B_const], writes=[B_const])
            P.op("dve", lambda e: e.tensor_mul(e16[:], e16[:], e16b[:]), reads=[B_const], writes=[B_const])
            P.op("pe", lambda e: e.matmul(ps_ST, e16[:], e16[:], start=True, stop=True), reads=[B_const], writes=[B_ST])
            P.op("pe", lambda e: e.matmul(ps_sm[:, 0:16], e16[:], identf[0:16, 0:16], start=True, stop=True),
                 reads=[B_const], writes=[B_sm])
            P.op("dve", lambda e: e.tensor_copy(blk[:], ps_ST), reads=[B_ST], writes=[B_const])
            P.op("dve", lambda e: e.tensor_copy(bm16[:], ps_sm[:, 0:16]), reads=[B_sm], writes=[B_const])
            P.op("dve", lambda e: e.tensor_mul(scausal[:], causal[:], blk[:]), reads=[B_const], writes=[B_const])
            P.op("dve", lambda e: e.tensor_scalar_add(pcols[:, 0:1], aux_t[:, 10:11], 1.0), reads=[B_aux], writes=[B_const])
            P.op("dve", lambda e: e.tensor_scalar(pcols[:, 1:2], aux_t[:, 10:11], -1.0, 127.0, ALU.mult, ALU.add),
                 reads=[B_aux], writes=[B_const])
            P.op("dve", lambda e: e.tensor_scalar_add(pcols[:, 2:3], aux_t[:, 11:12], 1.0), reads=[B_aux], writes=[B_const])
            P.op("dve", lambda e: e.tensor_scalar(pcols[:, 3:4], aux_t[:, 11:12], -1.0, 7.0, ALU.mult, ALU.add),
                 reads=[B_aux], writes=[B_const])
            P.op("dve", lambda e: e.tensor_copy(flagc[:], aux_t[:, 0:1]), reads=[B_aux], writes=[B_const])
            P.end_phase(_mk353)

        def alloc_gemm(ph, ntok):
            g = {}
            g["XT"] = sb(ph, "XT", [128, 32 * ntok], BF16)
            g["ntok"] = ntok
            g["B_XT"] = [[P.buf("XT%d_%d" % (t, q)) for q in range(8)] for t in range(ntok // 128)]
            g["wcnt"] = 0
            return g

        def alloc_w(ph, g):
            g["wt"] = [sb(ph, "wt%d" % i, [128, 16384], BF16) for i in range(2)]
            g["B_w"] = [[P.buf("w%d_%d" % (i, q), dma=True, fresh=True) for q in range(4)] for i in range(2)]

        def xt_ap(g, kc, t0, n):
            nt = g["ntok"]
            return g["XT"][:, kc * nt + t0: kc * nt + t0 + n]

        def load_w(g, segs, nk):
            slot = g["wcnt"] % 2
            g["wcnt"] += 1
            NC = sum(s.shape[1] for s in segs)
            wt = g["wt"][slot]
            view = wt[:, 0:nk * NC].rearrange("p (k n) -> p k n", k=nk)
            kq = nk // 4
            for q in range(4):
                off = 0
                for s in segs:
                    nc_ = s.shape[1]
                    src = s.rearrange("(k p) n -> p k n", p=128)[:, q * kq:(q + 1) * kq, :]
                    P.dma("pool", view[:, q * kq:(q + 1) * kq, off:off + nc_], src, g["B_w"][slot][q].dsem,
                          writes=[g["B_w"][slot][q]])
                    off += nc_
            return (slot, nk, NC, view)

        def mm_tm(g, w, t, c0, ncols, xsrc=None, xbufs=None):
            slot, nk, NC, view = w
            ps, B = mmbank()
            lhs = xsrc if xsrc is not None else (lambda kc: xt_ap(g, kc, t * 128, 128))
            rb = xbufs if xbufs is not None else list(g["B_XT"][t])

            kq = nk // 4
            for q in range(4):
                def fn(e, q=q):
                    for kc in range(q * kq, (q + 1) * kq):
                        ins = e.matmul(ps[:, 0:ncols], lhs(kc), view[:, kc, c0:c0 + ncols], start=(kc == 0), stop=(kc == nk - 1))
                    return ins
                P.op("pe", fn, reads=(rb if q == 0 else []) + [g["B_w"][slot][q]], writes=[B], skip_self=(q > 0))
            return ps[:, 0:ncols], B

        def mm_fm(g, w, c0, ncols, t0, ntk):
            slot, nk, NC, view = w
            ps, B = mmbank()

            tiles = list(range(t0 // 128, (t0 + ntk + 127) // 128))
            kq = nk // 4
            for q in range(4):
                def fn(e, q=q):
                    for kc in range(q * kq, (q + 1) * kq):
                        ins = e.matmul(ps[0:ncols, 0:ntk], view[:, kc, c0:c0 + ncols], xt_ap(g, kc, t0, ntk),
                                       start=(kc == 0), stop=(kc == nk - 1))
                    return ins
                P.op("pe", fn, reads=([b for t in tiles for b in g["B_XT"][t]] if q == 0 else []) + [g["B_w"][slot][q]], writes=[B],
                     skip_self=(q > 0))
            return ps[0:ncols, 0:ntk], B

        def loadt(g, src2d, ntiles, norm, gsrc):
            with ExitStack() as ph:
                _mk = P.mark()
                xs = [sb(ph, "lt_xs%d" % i, [128, D], F32) for i in range(3)]
                xb = [sb(ph, "lt_xb%d" % i, [128, D], BF16) for i in range(2)]
                B_xs = [P.buf("lt_xs%d" % i, dma=True) for i in range(3)]
                B_xb = [P.buf("lt_xb%d" % i) for i in range(2)]
                st_ = sb(ph, "lt_st", [128, 12], F32)
                B_st = [P.buf("lt_st%d" % i) for i in range(3)]
                ljunk = sb(ph, "lt_junk", [128, D], BF16)
                B_ljunk = P.buf("lt_junk")
                if gsrc is not None:
                    gb = sb(ph, "lt_gb", [128, D], F32)
                    B_gb = P.buf("lt_gb", dma=True)
                    P.dma("sp", gb[:], gsrc.to_broadcast([128, D]), B_gb.dsem, writes=[B_gb])
                nt = g["ntok"]

                def stA(t):
                    s3 = t % 3
                    c = s3 * 4
                    P.dma("sp", xs[s3][:], src2d[t * 128:(t + 1) * 128, :], B_xs[s3].dsem, writes=[B_xs[s3]])
                    if norm:
                        P.op("act", lambda e: e.activation(ljunk[:], xs[s3][:], AF.Square, accum_out=st_[:, c:c + 1]),
                             reads=[B_xs[s3]], writes=[B_ljunk, B_st[s3]])
                        P.op("act", lambda e: e.activation(st_[:, c + 1:c + 2], st_[:, c:c + 1], AF.Ln, scale=1.0 / D, bias=EPS),
                             reads=[B_st[s3]], writes=[B_st[s3]])
                        P.op("act", lambda e: e.activation(st_[:, c + 2:c + 3], st_[:, c + 1:c + 2], AF.Exp, scale=-0.5),
                             reads=[B_st[s3]], writes=[B_st[s3]])

                def stB(t):
                    s3 = t % 3
                    s = t % 2
                    c = s3 * 4
                    if norm:
                        P.op("dve", lambda e: e.scalar_tensor_tensor(
                            out=xb[s][:], in0=xs[s3][:], scalar=st_[:, c + 2:c + 3], in1=gb[:], op0=ALU.mult, op1=ALU.mult),
                            reads=[B_xs[s3], B_st[s3], B_gb], writes=[B_xb[s]])
                    else:
                        P.op("dve", lambda e: e.tensor_copy(xb[s][:], xs[s3][:]), reads=[B_xs[s3]], writes=[B_xb[s]])

                def stC(t):
                    s = t % 2
                    for grp in range(8):
                        h = grp % 2

                        def tr(e, grp=grp, h=h):
                            for i in range(4):
                                kc = grp * 4 + i
                                ins = e.transpose(tp[h][:, i * 128:(i + 1) * 128], xb[s][:, kc * 128:(kc + 1) * 128], identb[:])
                            return ins
                        P.op("pe", tr, reads=[B_xb[s], B_const], writes=[B_tp[h]])
                        dst = g["XT"][:, grp * 4 * nt:(grp * 4 + 4) * nt].rearrange("p (k n) -> p k n", k=4)[:, :, t * 128:(t + 1) * 128]
                        src = tp[h].rearrange("p (k n) -> p k n", k=4)
                        if grp % 2 == 0:
                            P.op("act", lambda e, dst=dst, src=src: e.copy(dst, src), reads=[B_tp[h]], writes=[g["B_XT"][t][grp]])
                        else:
                            P.op("dve", lambda e, dst=dst, src=src: e.tensor_copy(dst, src), reads=[B_tp[h]], writes=[g["B_XT"][t][grp]])
                stA(0)
                if ntiles > 1:
                    stA(1)
                stB(0)
                for t in range(ntiles):
                    stC(t)
                    if t + 2 < ntiles:
                        stA(t + 2)
                    if t + 1 < ntiles:
                        stB(t + 1)
                P.end_phase(_mk)

        class Stage:
            def __init__(self, ph, name, n, cols, dt):
                self.t = [sb(ph, "%s%d" % (name, i), [128, cols], dt) for i in range(n)]
                self.B = [P.buf("%s%d" % (name, i), dma=True) for i in range(n)]
                self.i = 0

            def nxt(self):
                k = self.i % len(self.t)
                self.i += 1
                return self.t[k], self.B[k]

        evc = [0]

        def evac_copy(out_ap, in_ap, reads, writes, scale=None):
            k = evc[0] % 2
            evc[0] += 1
            if k == 0:
                if scale is None:
                    P.op("act", lambda e: e.copy(out_ap, in_ap), reads=reads, writes=writes)
                else:
                    P.op("act", lambda e: e.mul(out_ap, in_ap, scale), reads=reads, writes=writes)
            else:
                if scale is None:
                    P.op("dve", lambda e: e.tensor_copy(out_ap, in_ap), reads=reads, writes=writes)
                else:
                    P.op("dve", lambda e: e.tensor_scalar_mul(out_ap, in_ap, scale), reads=reads, writes=writes)

        def rotary_tables(ph, pos_t, ntl, name):
            cosT = sb(ph, name + "cos", [128, ntl * 128], F32)
            sinT = sb(ph, name + "sin", [128, ntl * 128], F32)
            B_rt = P.buf(name + "rt")
            with ExitStack() as p2:
                _mk548 = P.mark()
                frq_t = sb(p2, name + "frq", [128, 64], F32)
                ang = sb(p2, name + "ang", [128, ntl * 64], F32)
                a2 = sb(p2, name + "a2", [128, ntl * 64], F32)
                B_f = P.buf(name + "frq", dma=True)
                B_a = P.buf(name + "ang")
                P.dma("sp", frq_t[:], frq, B_f.dsem, writes=[B_f])
                for t in range(ntl):
                    P.op("dve", lambda e, t=t: e.tensor_scalar_mul(ang[:, t * 64:(t + 1) * 64], frq_t[:], pos_t[:, t:t + 1]),
                         reads=[B_f, B_aux], writes=[B_a])
                ki = sb(p2, name + "ki", [128, ntl * 64], I32)
                kf = sb(p2, name + "kf", [128, ntl * 64], F32)
                mk = sb(p2, name + "mk", [128, ntl * 64], F32)
                C1 = 6.28125
                C2 = 2.0 * PI - 6.28125
                for (shift, dstT) in ((0.0, sinT), (0.5 * PI, cosT)):
                    P.op("dve", lambda e, shift=shift: e.tensor_scalar_add(a2[:], ang[:], shift), reads=[B_a], writes=[B_a])
                    P.op("dve", lambda e: e.tensor_scalar_mul(kf[:], a2[:], 1.0 / (2.0 * PI)), reads=[B_a], writes=[B_a])
                    P.op("dve", lambda e: e.tensor_copy(ki[:], kf[:]), reads=[B_a], writes=[B_a])
                    P.op("dve", lambda e: e.tensor_copy(kf[:], ki[:]), reads=[B_a], writes=[B_a])
                    P.op("dve", lambda e: e.scalar_tensor_tensor(out=a2[:], in0=kf[:], scalar=-C1, in1=a2[:], op0=ALU.mult, op1=ALU.add),
                         reads=[B_a], writes=[B_a])
                    P.op("dve", lambda e: e.scalar_tensor_tensor(out=a2[:], in0=kf[:], scalar=-C2, in1=a2[:], op0=ALU.mult, op1=ALU.add),
                         reads=[B_a], writes=[B_a])
                    P.op("dve", lambda e: e.tensor_scalar(mk[:], a2[:], PI, 2.0 * PI, ALU.is_gt, ALU.mult), reads=[B_a], writes=[B_a])
                    P.op("dve", lambda e: e.tensor_sub(a2[:], a2[:], mk[:]), reads=[B_a], writes=[B_a])
                    P.op("dve", lambda e: e.tensor_scalar(mk[:], a2[:], -PI, 2.0 * PI, ALU.is_lt, ALU.mult), reads=[B_a], writes=[B_a])
                    P.op("dve", lambda e: e.tensor_add(a2[:], a2[:], mk[:]), reads=[B_a], writes=[B_a])
                    P.op("dve", lambda e: e.tensor_scalar(a2[:], a2[:], PI, -PI, ALU.min, ALU.max), reads=[B_a], writes=[B_a])
                    d4 = dstT[:].rearrange("p (t a j) -> p t a j", t=ntl, a=2)
                    s3 = a2[:].rearrange("p (t j) -> p t j", t=ntl)
                    P.op("act", lambda e, d4=d4, s3=s3: e.activation(d4[:, :, 0, :], s3, AF.Sin), reads=[B_a], writes=[B_rt])
                    P.op("dve", lambda e, d4=d4: e.tensor_scalar_mul(d4[:, :, 1, :], d4[:, :, 0, :], DKR ** -0.5),
                         reads=[B_rt], writes=[B_rt])
                P.end_phase(_mk548)
            return cosT, sinT, B_rt

        def rotary_evac(ph_bufs, ps_ap, B_ps_, cosT, sinT, B_rt, t, ntl, which, nh, out_bf, B_out):
            xf, ta, tb_, B_x = ph_bufs
            a = 0 if which == "q" else 1
            cs = cosT[:].rearrange("p (t a j) -> p t a j", t=ntl, a=2)[:, t, a, :].unsqueeze(1).to_broadcast([128, nh, 64])
            sn_ = sinT[:].rearrange("p (t a j) -> p t a j", t=ntl, a=2)[:, t, a, :].unsqueeze(1).to_broadcast([128, nh, 64])
            n = nh * 128
            x4 = xf[:, 0:n].rearrange("p (h a j) -> p h a j", h=nh, a=2)
            o4 = out_bf.rearrange("p (h a j) -> p h a j", h=nh, a=2)
            a3 = ta[:, 0:nh * 64].rearrange("p (h j) -> p h j", h=nh)
            b3 = tb_[:, 0:nh * 64].rearrange("p (h j) -> p h j", h=nh)
            P.op("act", lambda e: e.copy(xf[:, 0:n], ps_ap), reads=[B_ps_], writes=[B_x])
            P.op("dve", lambda e: e.tensor_mul(a3, x4[:, :, 0, :], cs), reads=[B_x, B_rt], writes=[B_x])
            P.op("dve", lambda e: e.tensor_mul(b3, x4[:, :, 1, :], sn_), reads=[B_x, B_rt], writes=[B_x])
            P.op("dve", lambda e: e.tensor_sub(o4[:, :, 0, :], a3, b3), reads=[B_x], writes=[B_out])
            P.op("dve", lambda e: e.tensor_mul(a3, x4[:, :, 0, :], sn_), reads=[B_x, B_rt], writes=[B_x])
            P.op("dve", lambda e: e.tensor_mul(b3, x4[:, :, 1, :], cs), reads=[B_x, B_rt], writes=[B_x])
            P.op("dve", lambda e: e.tensor_add(o4[:, :, 1, :], a3, b3), reads=[B_x, B_out], writes=[B_out])

        def gate_rows(ph, g, n, name):
            th = sb(ph, name + "th", [4, n], F32)
            lf = sb(ph, name + "lf", [4, n], F32)
            tmp = sb(ph, name + "tmp", [4, n], F32)
            bi = sb(ph, name + "bi", [4, 2], F32)
            B_g = P.buf(name + "g")
            B_b = P.buf(name + "b", dma=True)
            P.dma("sp", bi[:, 0:1], b_ig, B_b.dsem, writes=[B_b])
            P.dma("sp", bi[:, 1:2], b_fg, B_b.dsem, writes=[B_b])
            P.op("dve", lambda e: e.tensor_scalar_mul(bi[:], bi[:], 1.0 / 15.0), reads=[B_b], writes=[B_b])
            w = load_w(g, [w_in[:, IG:IG + 8]], 32)
            nb = (n + 383) // 384
            for b in range(nb):
                t0 = b * 384
                ntk = min(384, n - t0)
                ps, B = mm_fm(g, w, 0, 4, t0, ntk)
                P.op("act", lambda e, ps=ps, t0=t0, ntk=ntk: e.activation(th[:, t0:t0 + ntk], ps, AF.Tanh, bias=bi[:, 0:1],
                                                                           scale=1.0 / 15.0), reads=[B, B_b], writes=[B_g])
                ps, B = mm_fm(g, w, 4, 4, t0, ntk)
                P.op("act", lambda e, ps=ps, t0=t0, ntk=ntk: e.activation(tmp[:, t0:t0 + ntk], ps, AF.Tanh, bias=bi[:, 1:2],
                                                                           scale=1.0 / 15.0), reads=[B, B_b], writes=[B_g])
            P.op("act", lambda e: e.activation(tmp[:], tmp[:], AF.Exp, scale=-15.0), reads=[B_g], writes=[B_g])
            P.op("act", lambda e: e.activation(tmp[:], tmp[:], AF.Ln, bias=1.0), reads=[B_g], writes=[B_g])
            P.op("dve", lambda e: e.tensor_scalar_mul(lf[:], tmp[:], -1.0), reads=[B_g], writes=[B_g])
            return th, lf, tmp, B_g

        def rows_to_cols(rows_ap_fn, nq, ntl, dst, dst_stride, dst_off, B_rows, B_dst):
            for t in range(ntl):
                def fn(e, t=t):
                    for q in range(nq):
                        ins = e.matmul(ps_sm[:, 4 * q:4 * q + 4], rows_ap_fn(q, t), identf[0:4, 0:4], start=True, stop=True)
                    return ins
                P.op("pe", fn, reads=[B_rows, B_const], writes=[B_sm])
                P.op("dve", lambda e, t=t: e.tensor_copy(dst[:, t * dst_stride + dst_off: t * dst_stride + dst_off + 4 * nq],
                                                         ps_sm[:, 0:4 * nq]), reads=[B_sm], writes=[B_dst])

        carry = sb(st, "carry", [4, 8], F32)
        B_carry = P.buf("carry")
        ones_row = sb(st, "ones_row", [4, TOK], F32)
        P.op("dve", lambda e: e.memset(ones_row[:], 1.0), writes=[B_const])
        P.op("dve", lambda e: e.memset(carry[:], 0.0), writes=[B_carry])
        npre = sb(st, "npre", [128, 8], F32)
        B_npre = P.buf("npre")
        P.op("dve", lambda e: e.memset(npre[:], 0.0), writes=[B_npre])

        if want("pre"):
            with ExitStack() as ph:
                _mk658 = P.mark()
                g = alloc_gemm(ph, PTOK)
                loadt(g, xp, NPRE, True, g_mix)
                alloc_w(ph, g)
                cosP, sinP, B_rtP = rotary_tables(ph, auxp_t, NPRE, "rp")
                with ExitStack() as p3:
                    _mkp3 = P.mark()
                    thP, lfP, tmpP, B_g = gate_rows(p3, g, PTOK, "gp")
                    FrP = sb(p3, "gpF", [4, PTOK], F32)
                    ArP = sb(p3, "gpA", [4, PTOK], F32)
                    GrP = sb(p3, "gpG", [4, PTOK], F32)
                    P.op("dve", lambda e: e.tensor_tensor_scan(FrP[:], ones_row[:, 0:PTOK], lfP[:], 0.0, ALU.mult, ALU.add),
                         reads=[B_g, B_const], writes=[B_g])
                    P.op("dve", lambda e: e.scalar_tensor_tensor(out=ArP[:], in0=thP[:], scalar=15.0, in1=FrP[:], op0=ALU.mult,
                                                                 op1=ALU.subtract), reads=[B_g], writes=[B_g])
                    P.op("dve", lambda e: e.tensor_tensor_scan(GrP[:], ones_row[:, 0:PTOK], ArP[:], 0.0, ALU.mult, ALU.max),
                         reads=[B_g], writes=[B_g])
                    P.op("dve", lambda e: e.tensor_mul(carry[:, 0:1], FrP[:, PTOK - 1:PTOK], flagc[0:4, :]), reads=[B_g, B_const],
                         writes=[B_carry])
                    P.op("dve", lambda e: e.tensor_mul(carry[:, 1:2], GrP[:, PTOK - 1:PTOK], flagc[0:4, :]), reads=[B_g, B_const],
                         writes=[B_carry])
                    P.op("dve", lambda e: e.tensor_scalar_mul(carry[:, 2:3], GrP[:, PTOK - 1:PTOK], -1.0), reads=[B_g],
                         writes=[B_carry])
                    P.op("act", lambda e: e.activation(tmpP[:], ArP[:], AF.Exp, bias=carry[:, 2:3]), reads=[B_g, B_carry], writes=[B_g])
                    rows_to_cols(lambda q, t: tmpP[:, t * 128:(t + 1) * 128], 1, NPRE, wcolp, 4, 0, B_g, B_wcolp)
                    P.op("dve", lambda e: e.tensor_copy(wcolp_bf[:], wcolp[:]), reads=[B_wcolp], writes=[B_wcolp])
                    P.end_phase(_mkp3)
                kpm = sb(ph, "kpm", [128, NPRE * 1024], BF16)
                kpr = sb(ph, "kpr", [128, NPRE * 1024], BF16)
                B_kpm = [[P.buf("kpm%d_%d" % (t, i)) for i in range(2)] for t in range(NPRE)]
                B_kpr = [[P.buf("kpr%d_%d" % (t, i)) for i in range(2)] for t in range(NPRE)]
                rb = (sb(ph, "prx", [128, 512], F32), sb(ph, "pra", [128, 256], F32), sb(ph, "prb", [128, 256], F32), P.buf("prx"))
                wv = [sb(ph, "pwv%d" % i, [128, 512], BF16) for i in range(3)]
                B_wv = [P.buf("pwv%d" % i) for i in range(3)]
                cst = [sb(ph, "pcst%d" % i, [128, 2 * 512], F32) for i in range(2)]
                B_cst = [P.buf("pcst%d" % i, dma=True) for i in range(2)]
                nT = sb(ph, "pnT", [128, 8], F32)
                B_nT = P.buf("pnT")
                kd = sb(ph, "pkd", [128, 8], F32)
                B_kd = P.buf("pkd")
                for h in range(HR):
                    lg = math.log(1.0 - 2.0 ** (-5.0 - h))
                    P.op("act", lambda e, h=h, lg=lg: e.activation(kd[:, h:h + 1], pcols[:, 1:2], AF.Exp, scale=lg),
                         reads=[B_const], writes=[B_kd])
                jobs = []
                for i in range(2):
                    jobs.append(("km", w_in[:, KM + i * 512: KM + (i + 1) * 512], i))
                for i in range(2):
                    jobs.append(("kr", w_in[:, KR + i * 512: KR + (i + 1) * 512], i))
                for i in range(4):
                    jobs.append(("vm", w_in[:, VM + i * 512: VM + (i + 1) * 512], i))
                for i in range(4):
                    jobs.append(("vr", w_in[:, VR + i * 512: VR + (i + 1) * 512], i))
                wn = load_w(g, [jobs[0][1]], 32)
                wvc = 0
                for ji, (kind, src, idx) in enumerate(jobs):
                    w = wn
                    if ji + 1 < len(jobs):
                        wn = load_w(g, [jobs[ji + 1][1]], 32)
                    pend = []
                    for t in range(NPRE):
                        ps, B = mm_tm(g, w, t, 0, 512)
                        for f_ in pend:
                            f_()
                        pend = []
                        if kind == "km":
                            evac_copy(kpm[:, t * 1024 + idx * 512: t * 1024 + (idx + 1) * 512], ps, [B], [B_kpm[t][idx]])
                        elif kind == "kr":
                            rotary_evac(rb, ps, B, cosP, sinP, B_rtP, t, NPRE, "k", 4,
                                        kpr[:, t * 1024 + idx * 512: t * 1024 + (idx + 1) * 512], B_kpr[t][idx])
                        elif kind == "vm":
                            h = idx
                            i = wvc % 3
                            wvc += 1
                            P.op("dve", lambda e, i=i, ps=ps, t=t, h=h: e.tensor_scalar_mul(wv[i][:], ps, wcolp[:, t * 4 + h: t * 4 + h + 1]),
                                 reads=[B, B_wcolp], writes=[B_wv[i]])

                            def fn(e, t=t, i=i, h=h):
                                for c in range(2):
                                    e.matmul(pst[5 + c][:], kpm[:, t * 1024 + h * 256 + c * 128: t * 1024 + h * 256 + (c + 1) * 128], wv[i][:],
                                             start=(t == 0), stop=(t == NPRE - 1))
                                for c in range(2):
                                    ins = e.matmul(ps_sm2[:, c:c + 1], kpm[:, t * 1024 + h * 256 + c * 128: t * 1024 + h * 256 + (c + 1) * 128],
                                                   wcolp_bf[:, t * 4 + h: t * 4 + h + 1], start=(t == 0 and c == 0), stop=(t == NPRE - 1))
                                return ins
                            def later(fn=fn, t=t, h=h, i=i):
                                P.op("pe", fn, reads=[B_kpm[t][h // 2], B_wv[i], B_wcolp], writes=[B_ps[5], B_ps[6], B7])
                                if t == NPRE - 1:
                                    s_ = h % 2
                                    P.op("dve", lambda e, s_=s_: e.tensor_scalar_mul(cst[s_][:, 0:512], pst[5][:], flagc[:, 0:1]),
                                         reads=[B_ps[5], B_const], writes=[B_cst[s_]])
                                    P.op("act", lambda e, s_=s_: e.mul(cst[s_][:, 512:1024], pst[6][:], flagc[:, 0:1]),
                                         reads=[B_ps[6], B_const], writes=[B_cst[s_]])
                                    P.op("dve", lambda e, h=h: e.tensor_scalar_mul(nT[:, 2 * h:2 * h + 2], ps_sm2[:, 0:2], flagc[:, 0:1]),
                                         reads=[B7, B_const], writes=[B_nT])
                                    P.dma("sp", s_stC[h * 256:(h + 1) * 256, :].rearrange("(c p) e -> p c e", p=128),
                                          cst[s_][:].rearrange("p (c e) -> p c e", c=2), B_cst[s_].dsem, reads=[B_cst[s_]])
                            pend.append(later)
                        else:
                            i = wvc % 3
                            wvc += 1
                            for k2 in range(2):
                                h = idx * 2 + k2
                                lg = math.log(1.0 - 2.0 ** (-5.0 - h))
                                cst_ = math.exp(lg * 128.0 * (NPRE - 1 - t))
                                P.op("dve", lambda e, i=i, ps=ps, k2=k2, h=h, cst_=cst_: e.tensor_scalar(
                                    wv[i][:, k2 * 256:(k2 + 1) * 256], ps[:, k2 * 256:(k2 + 1) * 256], kd[:, h:h + 1], cst_, ALU.mult, ALU.mult),
                                    reads=[B, B_kd], writes=[B_wv[i]])

                            def fn(e, t=t, i=i, idx=idx):
                                for k2 in range(2):
                                    h = idx * 2 + k2
                                    ins = e.matmul(pst[5][:, k2 * 256:(k2 + 1) * 256], kpr[:, t * 1024 + h * 128: t * 1024 + (h + 1) * 128],
                                                   wv[i][:, k2 * 256:(k2 + 1) * 256], start=(t == 0 and k2 == 0), stop=(t == NPRE - 1))
                                return ins
                            def later(fn=fn, t=t, idx=idx, i=i):
                                P.op("pe", fn, reads=[B_kpr[t][idx // 2], B_wv[i]], writes=[B_ps[5]])
                                if t == NPRE - 1:
                                    s_ = idx % 2
                                    P.op("dve", lambda e, s_=s_: e.tensor_scalar_mul(cst[s_][:, 0:512], pst[5][:], flagc[:, 0:1]),
                                         reads=[B_ps[5], B_const], writes=[B_cst[s_]])
                                    P.dma("sp", s_stS[idx * 256:(idx + 1) * 256, :].rearrange("(k p) e -> p k e", p=128),
                                          cst[s_][:, 0:512].rearrange("p (k e) -> p k e", k=2), B_cst[s_].dsem, reads=[B_cst[s_]])
                            pend.append(later)
                    for f_ in pend:
                        f_()
                    pend = []
                P.op("dve", lambda e: e.tensor_copy(npre[:], nT[:]), reads=[B_nT], writes=[B_npre])
                P.end_phase(_mk658)

        negG_keep = None
        if want("proj"):
            with ExitStack() as ph:
                _mk786 = P.mark()
                g = alloc_gemm(ph, TOK)
                loadt(g, xm, NT, True, g_mix)
                alloc_w(ph, g)
                cosM, sinM, B_rtM = rotary_tables(ph, aux_t[:, 1:10], NT, "rm")
                with ExitStack() as p2:
                    _mk792 = P.mark()
                    th, lf, tmp, B_g = gate_rows(p2, g, TOK, "gm")
                    Fr = sb(p2, "gmF", [4, TOK], F32)
                    Ar = sb(p2, "gmA", [4, TOK], F32)
                    Gr = sb(p2, "gmG", [4, TOK], F32)
                    nG = sb(p2, "gmnG", [4, TOK], F32)
                    iw = sb(p2, "gmiw", [4, TOK], F32)
                    em = sb(p2, "gmem", [4, TOK], F32)
                    ww = sb(p2, "gmw", [4, TOK], F32)
                    em2 = sb(p2, "gmem2", [4, TOK], F32)
                    m0T = sb(p2, "gmm0", [4, 16], F32)
                    decr = sb(p2, "gmdec", [4, 24], F32)
                    mo = sb(p2, "gmmo", [4, 20], F32)
                    sel = sb(p2, "gmsel", [4, 4 * 128], F32)
                    seli = sb(p2, "gmseli", [4, 4 * 128], I32)
                    B_m0 = P.buf("gmm0", dma=True)
                    B_mo = P.buf("gmmo", dma=True)
                    P.dma("sp", m0T[:], m0.rearrange("j h -> h j"), B_m0.dsem, writes=[B_m0], allow_slow_non_contiguous=True)
                    NPR = 1024
                    S3 = lambda r: r[:, NPR:TOK].rearrange("h (j t) -> h j t", j=16)
                    P.op("dve", lambda e: e.tensor_tensor_scan(Fr[:, 0:NPR], ones_row[:, 0:NPR], lf[:, 0:NPR], carry[:, 0:1],
                                                               ALU.mult, ALU.add), reads=[B_g, B_const, B_carry], writes=[B_g])
                    P.op("dve", lambda e: e.tensor_copy(S3(Fr)[:, :, 0], S3(lf)[:, :, 0]), reads=[B_g], writes=[B_g])
                    for t in range(1, 8):
                        P.op("dve", lambda e, t=t: e.tensor_add(S3(Fr)[:, :, t], S3(Fr)[:, :, t - 1], S3(lf)[:, :, t]),
                             reads=[B_g], writes=[B_g])
                    P.op("dve", lambda e: e.scalar_tensor_tensor(out=Ar[:], in0=th[:], scalar=15.0, in1=Fr[:], op0=ALU.mult,
                                                                 op1=ALU.subtract), reads=[B_g], writes=[B_g])
                    P.op("dve", lambda e: e.tensor_tensor_scan(Gr[:, 0:NPR], ones_row[:, 0:NPR], Ar[:, 0:NPR], carry[:, 1:2],
                                                               ALU.mult, ALU.max), reads=[B_g, B_carry], writes=[B_g])
                    P.op("dve", lambda e: e.tensor_max(S3(Gr)[:, :, 0], S3(Ar)[:, :, 0], m0T[:]), reads=[B_g, B_m0], writes=[B_g])
                    for t in range(1, 8):
                        P.op("dve", lambda e, t=t: e.tensor_max(S3(Gr)[:, :, t], S3(Gr)[:, :, t - 1], S3(Ar)[:, :, t]),
                             reads=[B_g], writes=[B_g])
                    P.op("dve", lambda e: e.tensor_scalar_mul(nG[:], Gr[:], -1.0), reads=[B_g], writes=[B_g])
                    for t in range(8):
                        bias = carry[:, 1:2] if t == 0 else Gr[:, t * 128 - 1:t * 128]
                        P.op("act", lambda e, t=t, bias=bias: e.activation(iw[:, t * 128:(t + 1) * 128], Gr[:, t * 128:(t + 1) * 128],
                                                                           AF.Exp, bias=bias, scale=-1.0),
                             reads=[B_g, B_carry], writes=[B_g])
                    for t in range(8):
                        P.op("dve", lambda e, t=t: e.tensor_sub(S3(tmp)[:, :, t], m0T[:], S3(Gr)[:, :, t]), reads=[B_g, B_m0],
                             writes=[B_g])
                    P.op("act", lambda e: e.activation(iw[:, NPR:TOK], tmp[:, NPR:TOK], AF.Exp), reads=[B_g], writes=[B_g])
                    P.op("dve", lambda e: e.tensor_add(em[:], Fr[:], Gr[:]), reads=[B_g], writes=[B_g])
                    P.op("act", lambda e: e.activation(em2[:], em[:], AF.Exp, scale=-2.0), reads=[B_g], writes=[B_g])
                    P.op("act", lambda e: e.activation(em[:], em[:], AF.Exp, scale=-1.0), reads=[B_g], writes=[B_g])
                    for t in range(8):
                        P.op("act", lambda e, t=t: e.activation(ww[:, t * 128:(t + 1) * 128], Ar[:, t * 128:(t + 1) * 128], AF.Exp,
                                                                bias=nG[:, t * 128 + 127:t * 128 + 128]), reads=[B_g], writes=[B_g])
                    for t in range(8):
                        P.op("dve", lambda e, t=t: e.tensor_sub(S3(tmp)[:, :, t], S3(Ar)[:, :, t], S3(Gr)[:, :, 7]), reads=[B_g],
                             writes=[B_g])
                    P.op("act", lambda e: e.activation(ww[:, NPR:TOK], tmp[:, NPR:TOK], AF.Exp), reads=[B_g], writes=[B_g])
                    P.op("dve", lambda e: e.tensor_copy(decr[:, 0:8], iw[:, 0:NPR].rearrange("h (t p) -> h t p", p=128)[:, :, 127]),
                         reads=[B_g], writes=[B_g])
                    P.op("dve", lambda e: e.tensor_copy(decr[:, 8:24], S3(iw)[:, :, 7]), reads=[B_g], writes=[B_g])
                    P.op("dve", lambda e: e.tensor_add(mo[:, 0:1], Fr[:, NPR - 1:NPR], Gr[:, NPR - 1:NPR]), reads=[B_g], writes=[B_mo])
                    P.op("dve", lambda e: e.tensor_add(mo[:, 1:17], S3(Fr)[:, :, 7], S3(Gr)[:, :, 7]), reads=[B_g, B_mo], writes=[B_mo])
                    P.dma("sp", pm, mo[:, 0:1], B_mo.dsem, reads=[B_mo])
                    P.dma("sp", sm.rearrange("j h -> h j"), mo[:, 1:17], B_mo.dsem, reads=[B_mo], allow_slow_non_contiguous=True)
                    rows = [Ar, iw, em, ww, nG, em2]
                    rows_to_cols(lambda q, t: rows[q][:, t * 128:(t + 1) * 128], 6, NT, tokc, 24, 0, B_g, B_tokc)
                    B_nGd = P.buf("nGd", dma=True)
                    P.dma("sp", s_negG, nG[:], B_nGd.dsem, reads=[B_g])
                    P.op("pool", lambda e: e.iota(seli[:], [[1, 4], [0, 128]], base=0, channel_multiplier=-1), writes=[B_g])
                    P.op("dve", lambda e: e.tensor_copy(sel[:], seli[:]), reads=[B_g], writes=[B_g])
                    P.op("dve", lambda e: e.tensor_single_scalar(sel[:], sel[:], 0.0, ALU.is_equal), reads=[B_g], writes=[B_g])

                    def fdec(e):
                        for h in range(4):
                            ins = e.matmul(ps_sm[:, h * 24:(h + 1) * 24], sel[:, h * 128:(h + 1) * 128], decr[:], start=True, stop=True)
                        return ins
                    P.op("pe", fdec, reads=[B_g], writes=[B_sm])
                    P.op("dve", lambda e: e.tensor_copy(decb[:], ps_sm[:, 0:96]), reads=[B_sm], writes=[B_decb])
                    P.end_phase(_mk792)

                stg = Stage(ph, "mstg", 5, 512, BF16)
                stf = Stage(ph, "mstf", 2, 512, F32)
                gbh = [sb(ph, "gbh%d" % i, [128, 512], F32) for i in range(2)]
                B_gbh = [P.buf("gbh%d" % i, dma=True) for i in range(2)]
                rb = (sb(ph, "mrx", [128, 512], F32), sb(ph, "mra", [128, 256], F32), sb(ph, "mrb", [128, 256], F32), P.buf("mrx"))
                rot_bf = [sb(ph, "mrot%d" % i, [128, 512], BF16) for i in range(3)]
                B_rot = [P.buf("mrot%d" % i, dma=True) for i in range(3)]
                jobs = []
                for h in range(HM):
                    jobs.append(("qk", [w_in[:, QM + h * 256: QM + (h + 1) * 256], w_in[:, KM + h * 256: KM + (h + 1) * 256]], h))
                    jobs.append(("v", [w_in[:, VM + h * 512: VM + (h + 1) * 512]], h))
                    jobs.append(("o", [w_in[:, OM + h * 512: OM + (h + 1) * 512]], h))
                for i in range(2):
                    jobs.append(("rq", [w_in[:, QR + i * 512: QR + (i + 1) * 512]], i))
                for i in range(2):
                    jobs.append(("rk", [w_in[:, KR + i * 512: KR + (i + 1) * 512]], i))
                for i in range(4):
                    jobs.append(("rv", [w_in[:, VR + i * 512: VR + (i + 1) * 512]], i))
                for i in range(4):
                    jobs.append(("rg", [w_in[:, GR + i * 512: GR + (i + 1) * 512]], i))
                wn = load_w(g, jobs[0][1], 32)
                rc = 0
                for ji, (kind, segs, idx) in enumerate(jobs):
                    w = wn
                    if ji + 1 < len(jobs):
                        wn = load_w(g, jobs[ji + 1][1], 32)
                    if kind == "qk":
                        h = idx
                        pendk = []
                        kc_ = 0
                        for sbk in range(4):
                            dstT = s_qmT if sbk < 2 else s_kmT
                            row0 = h * 256 + (sbk % 2) * 128
                            for b in range(3):
                                ps, B = mm_fm(g, w, sbk * 128, 128, b * 384, 384)
                                for f_ in pendk:
                                    f_()
                                pendk = []
                                sg, Bs = stg.nxt()
                                evac_copy(sg[:, 0:384], ps, [B], [Bs], scale=(DKM ** -0.5 if sbk < 2 else None))
                                P.dma("sp", dstT[row0:row0 + 128, b * 384:(b + 1) * 384], sg[:, 0:384], Bs.dsem, reads=[Bs])
                                if sbk >= 2:
                                    def later(sg=sg, Bs=Bs, b=b, col0=h * 256 + (sbk - 2) * 128, hh_=kc_ % 2):
                                        def tr(e):
                                            for j in range(3):
                                                ins = e.transpose(tp[hh_][:, j * 128:(j + 1) * 128], sg[:, j * 128:(j + 1) * 128], identb[:])
                                            return ins
                                        P.op("pe", tr, reads=[Bs, B_const], writes=[B_tp[hh_]])
                                        s2, Bs2 = stg.nxt()
                                        evac_copy(s2[:, 0:384], tp[hh_][:, 0:384], [B_tp[hh_]], [Bs2])
                                        P.dma("sp", s_km[b * 384:(b + 1) * 384, col0:col0 + 128].rearrange("(j p) c -> p j c", p=128),
                                              s2[:, 0:384].rearrange("p (j c) -> p j c", j=3), Bs2.dsem, reads=[Bs2])
                                    pendk.append(later)
                                    kc_ += 1
                        for f_ in pendk:
                            f_()
                        pendk = []
                    elif kind in ("v", "rv"):
                        dstd = s_vm if kind == "v" else s_vr
                        for t in range(NT):
                            ps, B = mm_tm(g, w, t, 0, 512)
                            sg, Bs = stg.nxt()
                            evac_copy(sg[:], ps, [B], [Bs])
                            P.dma("sp", dstd[t * 128:(t + 1) * 128, idx * 512:(idx + 1) * 512], sg[:], Bs.dsem, reads=[Bs])
                    elif kind in ("o", "rg"):
                        gi = idx % 2
                        if kind == "o":
                            P.dma("sp", gbh[gi][:], g_mh[idx:idx + 1, :].to_broadcast([128, 512]), B_gbh[gi].dsem, writes=[B_gbh[gi]])
                            dstd = s_gsm
                        else:
                            for k2 in range(2):
                                P.dma("sp", gbh[gi][:, k2 * 256:(k2 + 1) * 256],
                                      g_rh[idx * 2 + k2: idx * 2 + k2 + 1, :].to_broadcast([128, 256]), B_gbh[gi].dsem,
                                      writes=[B_gbh[gi]])
                            dstd = s_gsr
                        for t in range(NT):
                            ps, B = mm_tm(g, w, t, 0, 512)
                            sf, Bf = stf.nxt()
                            sg, Bs = stg.nxt()
                            func = AF.Sigmoid if kind == "o" else AF.Silu
                            P.op("act", lambda e, sf=sf, ps=ps, func=func: e.activation(sf[:], ps, func), reads=[B], writes=[Bf])
                            P.op("dve", lambda e, sf=sf, sg=sg, gi=gi: e.tensor_mul(sg[:], sf[:], gbh[gi][:]),
                                 reads=[Bf, B_gbh[gi]], writes=[Bs])
                            P.dma("sp", dstd[t * 128:(t + 1) * 128, idx * 512:(idx + 1) * 512], sg[:], Bs.dsem, reads=[Bs])
                    elif kind in ("rq", "rk"):
                        dstT = s_qrT if kind == "rq" else s_krT

                        def tr_out(t, r, dstT=dstT, idx=idx):
                            hh_ = t % 2

                            def tr(e):
                                for i in range(4):
                                    ins = e.transpose(tp[hh_][:, i * 128:(i + 1) * 128], rot_bf[r][:, i * 128:(i + 1) * 128], identb[:])
                                return ins
                            P.op("pe", tr, reads=[B_rot[r], B_const], writes=[B_tp[hh_]])
                            sg, Bs = stg.nxt()
                            evac_copy(sg[:], tp[hh_], [B_tp[hh_]], [Bs])
                            P.dma("sp", dstT[idx * 512:(idx + 1) * 512, t * 128:(t + 1) * 128].rearrange("(h p) c -> p h c", p=128),
                                  sg[:].rearrange("p (h c) -> p h c", h=4), Bs.dsem, reads=[Bs])
                        prev = None
                        for t in range(NT):
                            ps, B = mm_tm(g, w, t, 0, 512)
                            r = rc % 3
                            rc += 1
                            rotary_evac(rb, ps, B, cosM, sinM, B_rtM, t, NT, "q" if kind == "rq" else "k", 4, rot_bf[r][:], B_rot[r])
                            if kind == "rk":
                                P.dma("sp", s_kr[t * 128:(t + 1) * 128, idx * 512:(idx + 1) * 512], rot_bf[r][:], B_rot[r].dsem,
                                      reads=[B_rot[r]])
                            if prev is not None:
                                tr_out(*prev)
                            prev = (t, r)
                        tr_out(*prev)
                P.end_phase(_mk786)

        if want("mix"):
            with ExitStack() as ph:
                _mk974 = P.mark()
                qT = [sb(ph, "qT%d" % i, [128, 2 * TOK], BF16) for i in range(2)]
                kT = [sb(ph, "kT%d" % i, [128, 2 * TOK], BF16) for i in range(2)]
                ktm = [sb(ph, "ktm%d" % i, [128, NT * 256], BF16) for i in range(2)]
                vtm = [sb(ph, "vtm%d" % i, [128, NT * 512], BF16) for i in range(2)]
                gsm_ = [sb(ph, "gsm%d" % i, [128, NT * 512], BF16) for i in range(2)]
                B_hd = [P.buf("hd%d" % i, dma=True) for i in range(2)]
                Cst = sb(ph, "Cst", [128, 2 * 512], F32)
                Cbf = sb(ph, "Cbf", [128, 2 * 512], BF16)
                nst = sb(ph, "nst", [128, 2], F32)
                nbf = sb(ph, "nbf", [128, 2], BF16)
                B_C = P.buf("Cst", dma=True)
                B_Cbf = P.buf("Cbf")
                B_Cbfh = [P.buf("Cbf0"), P.buf("Cbf1")]
                Cin = [sb(ph, "Cin%d" % i, [128, 2 * 512], F32) for i in range(2)]
                B_Cin = [P.buf("Cin%d" % i, dma=True) for i in range(2)]
                B_n = P.buf("nst")
                B_nbf = P.buf("nbf")
                qTz = sb(ph, "qTz", [128, 2 * 16 * 128], BF16)
                B_qTz = P.buf("qTz")
                C0b = [sb(ph, "C0b%d" % i, [128, 2 * 512], BF16) for i in range(4)]
                B_C0b = [P.buf("C0b%d" % i, dma=True, fresh=True) for i in range(4)]
                C0f = [sb(ph, "C0f%d" % i, [128, 2 * 512], F32) for i in range(4)]
                B_C0f = [P.buf("C0f%d" % i, dma=True) for i in range(4)]
                n0T = sb(ph, "n0T", [128, 128], F32)
                n0b = sb(ph, "n0b", [128, 128], BF16)
                n0r = sb(ph, "n0r", [128, 128], F32)
                snT = sb(ph, "snT", [128, 128], F32)
                pnT = sb(ph, "pnT", [128, 8], F32)
                B_n0 = P.buf("n0", dma=True)
                B_snT = P.buf("snT")
                B_pnT = P.buf("pnT")
                R2 = range(2)
                negGb = [sb(ph, "negGb%d" % i, [128, TOK], F32) for i in R2]
                B_negGb = [P.buf("negGb%d" % i, dma=True) for i in R2]
                ex_all = sb(ph, "ex_all", [128, TOK], F32)
                B_exall = P.buf("ex_all")
                cbp = sb(ph, "cbp", [128, 128], F32)
                cbs = sb(ph, "cbs", [128, 128], F32)
                DT = [sb(ph, "DT%d" % i, [128, 128], F32) for i in R2]
                sT = [sb(ph, "sT%d" % i, [128, 128], BF16) for i in R2]
                B_DT = [P.buf("DT%d" % i) for i in R2]
                B_sT = [P.buf("sT%d" % i) for i in R2]
                tmpB = [sb(ph, "tmpB%d" % i, [128, 512], F32) for i in R2]
                num = [sb(ph, "num%d" % i, [128, 512], F32) for i in R2]
                B_tmpB = [P.buf("tmpB%d" % i) for i in R2]
                B_num = [P.buf("num%d" % i) for i in R2]
                junk = sb(ph, "junk", [128, 512], BF16)
                B_junk = P.buf("junk")
                smc = [sb(ph, "smc%d" % i, [128, 16], F32) for i in R2]
                B_smc = [P.buf("smc%d" % i) for i in R2]
                og = Stage(ph, "og", 3, 512, F32)
                wvb = [sb(ph, "wvb%d" % i, [128, 512], BF16) for i in range(4)]
                B_wvb = [P.buf("wvb%d" % i) for i in range(4)]
                wcb = [sb(ph, "wcb%d" % i, [128, 1], BF16) for i in R2]
                B_wcb = [P.buf("wcb%d" % i) for i in R2]
                wm = sb(ph, "wm", [128, 16], F32)
                wmb = sb(ph, "wmb", [128, 16], BF16)
                B_wm = P.buf("wm")
                Dp = sb(ph, "Dp", [128, 128], F32)
                Ds = sb(ph, "Ds", [128, 128], F32)
                rcol = sb(ph, "rcol", [128, 4], F32)
                kd16 = sb(ph, "kd16", [128, 16], F32)
                B_rc = P.buf("rc")
                pnst = sb(ph, "pnst", [8, 128], F32)
                prod = [sb(ph, "prod%d" % i, [128, 128], F32) for i in range(2)]
                B_prod = [P.buf("prod%d" % i) for i in range(2)]
                B_pnst = P.buf("pnst", dma=True)
                STb = [pst[2], pst[1]]
                B_STb = [B_ps[2], B_ps[1]]
                Ab = [pst[3], pst[0]]
                B_Ab = [B_ps[3], B_ps[0]]

                P.op("dve", lambda e: e.memset(qTz[:], 0.0), writes=[B_qTz])
                P.op("dve", lambda e: e.tensor_scalar(cbp[:], causal[:], -1.0, 30000.0, ALU.add, ALU.mult), reads=[B_const], writes=[B_exall])
                P.op("dve", lambda e: e.tensor_scalar(cbs[:], scausal[:], -1.0, 30000.0, ALU.add, ALU.mult), reads=[B_const], writes=[B_exall])
                P.dma("sp", n0r[:], n0, B_n0.dsem, writes=[B_n0])
                P.op("pe", lambda e: e.matmul(ps_ST, n0r[:], identf[:], start=True, stop=True), reads=[B_n0, B_const], writes=[B_ST])
                P.op("dve", lambda e: e.tensor_copy(n0T[:], ps_ST), reads=[B_ST], writes=[B_n0])
                P.op("dve", lambda e: e.tensor_copy(n0b[:], n0T[:]), reads=[B_n0], writes=[B_n0])

                cnt = {"c0f": 0, "c0b": 0, "wv": 0}

                def load_head_m(h, s):
                    d = B_hd[s].dsem
                    P.dma("sp", qT[s][:].rearrange("p (c n) -> p c n", c=2),
                          s_qmT[h * 256:(h + 1) * 256, :].rearrange("(c p) n -> p c n", p=128), d, writes=[B_hd[s]])
                    P.dma("sp", kT[s][:].rearrange("p (c n) -> p c n", c=2),
                          s_kmT[h * 256:(h + 1) * 256, :].rearrange("(c p) n -> p c n", p=128), d, writes=[B_hd[s]])
                    P.dma("sp", ktm[s][:].rearrange("p (t c) -> p t c", t=NT),
                          s_km[:, h * 256:(h + 1) * 256].rearrange("(t p) c -> p t c", p=128), d, writes=[B_hd[s]])
                    P.dma("sp", vtm[s][:].rearrange("p (t c) -> p t c", t=NT),
                          s_vm[:, h * 512:(h + 1) * 512].rearrange("(t p) c -> p t c", p=128), d, writes=[B_hd[s]])
                    P.dma("sp", gsm_[s][:].rearrange("p (t c) -> p t c", t=NT),
                          s_gsm[:, h * 512:(h + 1) * 512].rearrange("(t p) c -> p t c", p=128), d, writes=[B_hd[s]])

                def load_head_r(h, s):
                    d = B_hd[s].dsem
                    P.dma("sp", qT[s][:, 0:TOK], s_qrT[h * 128:(h + 1) * 128, :], d, writes=[B_hd[s]])
                    P.dma("sp", kT[s][:, 0:TOK], s_krT[h * 128:(h + 1) * 128, :], d, writes=[B_hd[s]])
                    P.dma("sp", ktm[s][:, 0:NT * 128].rearrange("p (t c) -> p t c", t=NT),
                          s_kr[:, h * 128:(h + 1) * 128].rearrange("(t p) c -> p t c", p=128), d, writes=[B_hd[s]])
                    P.dma("sp", vtm[s][:, 0:NT * 256].rearrange("p (t c) -> p t c", t=NT),
                          s_vr[:, h * 256:(h + 1) * 256].rearrange("(t p) c -> p t c", p=128), d, writes=[B_hd[s]])
                    P.dma("sp", gsm_[s][:, 0:NT * 256].rearrange("p (t c) -> p t c", t=NT),
                          s_gsr[:, h * 256:(h + 1) * 256].rearrange("(t p) c -> p t c", p=128), d, writes=[B_hd[s]])

                def out_gated(r, dv, t, s, col0):
                    o, Bo = og.nxt()
                    P.op("dve", lambda e: e.scalar_tensor_tensor(out=o[:, 0:dv], in0=num[r][:, 0:dv], scalar=smc[r][:, 9:10],
                                                                 in1=gsm_[s][:, t * dv:(t + 1) * dv], op0=ALU.mult, op1=ALU.mult),
                         reads=[B_num[r], B_smc[r], B_hd[s]], writes=[Bo])
                    P.dma("sp", s_cat[t * 128:(t + 1) * 128, col0:col0 + dv], o[:, 0:dv], Bo.dsem, reads=[Bo])

                def m_cols(h, t):
                    tc0 = t * 24
                    return dict(A=tokc[:, tc0 + h: tc0 + h + 1], iw=tokc[:, tc0 + 4 + h: tc0 + 5 + h],
                                em=tokc[:, tc0 + 8 + h: tc0 + 9 + h], w=tokc[:, tc0 + 12 + h: tc0 + 13 + h],
                                nG=tokc[:, tc0 + 16 + h: tc0 + 17 + h], em2=tokc[:, tc0 + 20 + h: tc0 + 21 + h])

                def m_front_a(h, s, t):
                    r = t % 2
                    c = m_cols(h, t)
                    smp = (t == NT - 1)
                    st_ap = STb[r][:, 0:128]
                    P.op("pe", lambda e: (e.matmul(st_ap, kT[s][:, t * 128:(t + 1) * 128], qT[s][:, t * 128:(t + 1) * 128], start=True, stop=False),
                                          e.matmul(st_ap, kT[s][:, TOK + t * 128:TOK + (t + 1) * 128],
                                                   qT[s][:, TOK + t * 128:TOK + (t + 1) * 128], start=False, stop=True))[1],
                         reads=[B_hd[s]], writes=[B_STb[r]])
                    P.op("act", lambda e: e.activation(DT[r][:], ex_all[:, t * 128:(t + 1) * 128], AF.Exp, bias=c["A"]),
                         reads=[B_exall, B_tokc], writes=[B_DT[r]])
                    P.op("dve", lambda e: e.tensor_mul(sT[r][:], st_ap, DT[r][:]), reads=[B_STb[r], B_DT[r]], writes=[B_sT[r]])
                    if not smp:
                        i = t % 2
                        P.op("dve", lambda e: e.tensor_scalar_mul(wvb[i][:], vtm[s][:, t * 512:(t + 1) * 512], c["w"]),
                             reads=[B_hd[s], B_tokc], writes=[B_wvb[i]])
                        P.op("dve", lambda e: e.tensor_copy(wcb[r][:], c["w"]), reads=[B_tokc], writes=[B_wcb[r]])
                        return i
                    return None

                def m_front_b(h, s, t):
                    r = t % 2
                    P.op("pe", lambda e: (e.matmul(Ab[r][:], sT[r][:], vtm[s][:, t * 512:(t + 1) * 512], start=True, stop=True),
                                          e.matmul(STb[r][:, 128:129], sT[r][:], ones_bf[:], start=True, stop=True))[1],
                         reads=[B_sT[r], B_hd[s], B_const], writes=[B_Ab[r], B_STb[r]])

                def m_inter(h, s, t):
                    r = t % 2

                    def finter(e):
                        for c in range(2):
                            e.matmul(pst[4][:], qT[s][:, c * TOK + t * 128:c * TOK + (t + 1) * 128], Cbf[:, c * 512:(c + 1) * 512],
                                     start=(c == 0), stop=(c == 1))
                        for c in range(2):
                            ins = e.matmul(STb[r][:, 129:130], qT[s][:, c * TOK + t * 128:c * TOK + (t + 1) * 128], nbf[:, c:c + 1],
                                           start=(c == 0), stop=(c == 1))
                        return ins
                    P.op("pe", finter, reads=[B_hd[s], B_Cbfh[0], B_Cbfh[1], B_nbf], writes=[B_ps[4], B_STb[r]])

                def m_upd_mm(h, s, t, i):
                    r = t % 2

                    def fupd(e):
                        for c in range(2):
                            e.matmul(pst[5 + c][:], ktm[s][:, t * 256 + c * 128:t * 256 + (c + 1) * 128], wvb[i][:], start=True, stop=True)
                        for c in range(2):
                            ins = e.matmul(STb[r][:, 130 + c:131 + c], ktm[s][:, t * 256 + c * 128:t * 256 + (c + 1) * 128], wcb[r][:],
                                           start=True, stop=True)
                        return ins
                    P.op("pe", fupd, reads=[B_hd[s], B_wvb[i], B_wcb[r]], writes=[B_ps[5], B_ps[6], B_STb[r]])

                def m_upd_ew(h, s, t):
                    r = t % 2
                    dc = decb[:, h * 24 + t: h * 24 + t + 1]
                    csrc, Bsrc = (Cin[h % 2], B_Cin[h % 2]) if t == 0 else (Cst, B_C)
                    for c in range(2):
                        P.op("dve", lambda e, c=c: e.scalar_tensor_tensor(
                            out=Cst[:, c * 512:(c + 1) * 512], in0=csrc[:, c * 512:(c + 1) * 512], scalar=dc, in1=pst[5 + c][:],
                            op0=ALU.mult, op1=ALU.add), reads=[Bsrc, B_C, B_decb, B_ps[5 + c]], writes=[B_C])
                    P.op("act", lambda e: e.copy(Cbf[:, 0:512], Cst[:, 0:512]), reads=[B_C], writes=[B_Cbfh[0]])
                    P.op("act", lambda e: e.copy(Cbf[:, 512:1024], Cst[:, 512:1024]), reads=[B_C], writes=[B_Cbfh[1]])
                    P.op("dve", lambda e: e.scalar_tensor_tensor(out=nst[:], in0=nst[:], scalar=dc, in1=STb[r][:, 130:132],
                                                                 op0=ALU.mult, op1=ALU.add), reads=[B_n, B_decb, B_STb[r]], writes=[B_n])
                    P.op("dve", lambda e: e.tensor_copy(nbf[:], nst[:]), reads=[B_n], writes=[B_nbf])
                    if t == NT - 2:
                        P.dma("sp", pC[h * 256:(h + 1) * 256, :].rearrange("(c p) e -> p c e", p=128),
                              Cst[:].rearrange("p (c e) -> p c e", c=2), B_C.dsem, reads=[B_C])
                        P.op("dve", lambda e: e.tensor_copy(pnT[:, 2 * h:2 * h + 2], nst[:]), reads=[B_n], writes=[B_pnT])

                def m_back1(h, s, t):
                    r = t % 2
                    c = m_cols(h, t)
                    bsrc, Bb = (b7f[:, :], B7) if t == NT - 1 else (pst[4][:], B_ps[4])
                    P.op("act", lambda e: e.activation(tmpB[r][:], bsrc, AF.Copy, scale=c["iw"]), reads=[Bb, B_tokc], writes=[B_tmpB[r]])
                    P.op("dve", lambda e: e.tensor_add(num[r][:], tmpB[r][:], Ab[r][:]), reads=[B_tmpB[r], B_Ab[r]], writes=[B_num[r]])
                    P.op("act", lambda e: e.activation(junk[:], num[r][:], AF.Square, accum_out=smc[r][:, 0:1]), reads=[B_num[r]],
                         writes=[B_smc[r], B_junk])
                    P.op("dve", lambda e: e.tensor_copy(smc[r][:, 1:3], STb[r][:, 128:130]), reads=[B_STb[r], B_smc[r]], writes=[B_smc[r]])
                    P.op("dve", lambda e: e.scalar_tensor_tensor(out=smc[r][:, 3:4], in0=smc[r][:, 2:3], scalar=c["iw"], in1=smc[r][:, 1:2],
                                                                 op0=ALU.mult, op1=ALU.add), reads=[B_smc[r], B_tokc], writes=[B_smc[r]])
                    P.op("dve", lambda e: e.tensor_mul(smc[r][:, 4:5], smc[r][:, 3:4], smc[r][:, 3:4]), reads=[B_smc[r]], writes=[B_smc[r]])
                    P.op("dve", lambda e: e.tensor_scalar(smc[r][:, 4:5], smc[r][:, 4:5], c["em2"], EPS, ALU.max, ALU.mult),
                         reads=[B_smc[r], B_tokc], writes=[B_smc[r]])
                    P.op("dve", lambda e: e.scalar_tensor_tensor(out=smc[r][:, 6:7], in0=smc[r][:, 0:1], scalar=1.0 / DVM, in1=smc[r][:, 4:5],
                                                                 op0=ALU.mult, op1=ALU.add), reads=[B_smc[r]], writes=[B_smc[r]])
                    P.op("act", lambda e: e.activation(smc[r][:, 7:8], smc[r][:, 6:7], AF.Ln), reads=[B_smc[r]], writes=[B_smc[r]])
                    P.op("act", lambda e: e.activation(smc[r][:, 9:10], smc[r][:, 7:8], AF.Exp, scale=-0.5), reads=[B_smc[r]], writes=[B_smc[r]])

                def m_back2(h, s, t):
                    r = t % 2
                    out_gated(r, DVM, t, s, h * DVM)

                def m_sample_pre(h, s):
                    t = NT - 1
                    c = m_cols(h, t)
                    for cc in range(2):
                        P.op("dve", lambda e, cc=cc: e.tensor_copy(
                            _diag_view(qTz, cc), qT[s][:, cc * TOK + 1024: cc * TOK + 1152].rearrange("p (j t) -> p j t", j=16)),
                            reads=[B_hd[s]], writes=[B_qTz])
                    P.op("dve", lambda e: e.tensor_scalar_mul(wm[:], bm16[:], c["w"]), reads=[B_tokc, B_const], writes=[B_wm])
                    P.op("dve", lambda e: e.tensor_copy(wmb[:], wm[:]), reads=[B_wm], writes=[B_wm])
                    nu = STb[1][:, 160:192]
                    P.op("pe", lambda e: (e.matmul(nu[:, 0:16], ktm[s][:, t * 256:t * 256 + 128], wmb[:], start=True, stop=True),
                                          e.matmul(nu[:, 16:32], ktm[s][:, t * 256 + 128:t * 256 + 256], wmb[:], start=True, stop=True))[1],
                         reads=[B_hd[s], B_wm], writes=[B_STb[1]])
                    v_o = snT[:].rearrange("p (j h c) -> p j h c", j=16, h=HM)[:, :, h, :]
                    v_i = n0T[:].rearrange("p (j h c) -> p j h c", j=16, h=HM)[:, :, h, :]
                    v_d = decb[:, h * 24 + 8: h * 24 + 24].unsqueeze(2).to_broadcast([128, 16, 2])
                    v_u = nu.rearrange("p (c j) -> p j c", c=2)
                    P.op("dve", lambda e: e.tensor_mul(v_o, v_i, v_d), reads=[B_n0, B_decb, B_snT], writes=[B_snT])
                    P.op("dve", lambda e: e.tensor_add(v_o, v_o, v_u), reads=[B_STb[1], B_snT], writes=[B_snT])

                c0slot = {}

                def m_sample_load(h, j):
                    k4 = cnt["c0f"] % 4
                    cnt["c0f"] += 1
                    c0slot[j] = k4
                    r0 = (j * HM + h) * 256
                    P.dma("sp", C0f[k4][:].rearrange("p (c e) -> p c e", c=2),
                          C0[r0:r0 + 256, :].rearrange("(c p) e -> p c e", p=128), B_C0f[k4].dsem, writes=[B_C0f[k4]])
                    P.dma("pool", C0b[k4][:].rearrange("p (c e) -> p c e", c=2),
                          C0[r0:r0 + 256, :].rearrange("(c p) e -> p c e", p=128), B_C0b[k4].dsem, writes=[B_C0b[k4]])

                def m_sample_j(h, s, j):
                    t = NT - 1
                    k4 = c0slot[j]
                    i2 = k4
                    iw_ = 2 + cnt["wv"] % 2
                    cnt["wv"] += 1
                    r0 = (j * HM + h) * 256

                    def fint(e):
                        for c_ in range(2):
                            ins = e.matmul(b7f[:, :], qTz[:, (c_ * 16 + j) * 128:(c_ * 16 + j + 1) * 128], C0b[i2][:, c_ * 512:(c_ + 1) * 512],
                                           start=(j == 0 and c_ == 0), stop=(j == 15 and c_ == 1))
                        return ins
                    P.op("pe", fint, reads=[B_qTz, B_C0b[i2]], writes=[B7])
                    P.op("dve", lambda e: e.tensor_scalar_mul(wvb[iw_][:], vtm[s][:, t * 512:(t + 1) * 512], wm[:, j:j + 1]),
                         reads=[B_hd[s], B_wm], writes=[B_wvb[iw_]])

                    def fupd(e):
                        for c_ in range(2):
                            ins = e.matmul(pst[5 + c_][:], ktm[s][:, t * 256 + c_ * 128:t * 256 + (c_ + 1) * 128], wvb[iw_][:],
                                           start=True, stop=True)
                        return ins
                    P.op("pe", fupd, reads=[B_hd[s], B_wvb[iw_]], writes=[B_ps[5], B_ps[6]])
                    dc = decb[:, h * 24 + 8 + j: h * 24 + 9 + j]
                    for c_ in range(2):
                        P.op("dve", lambda e, c_=c_: e.scalar_tensor_tensor(
                            out=C0f[k4][:, c_ * 512:(c_ + 1) * 512], in0=C0f[k4][:, c_ * 512:(c_ + 1) * 512], scalar=dc,
                            in1=pst[5 + c_][:], op0=ALU.mult, op1=ALU.add), reads=[B_C0f[k4], B_decb, B_ps[5 + c_], B_C0b[i2]],
                            writes=[B_C0f[k4]])
                    P.dma("sp", sC[r0:r0 + 256, :].rearrange("(c p) e -> p c e", p=128),
                          C0f[k4][:].rearrange("p (c e) -> p c e", c=2), B_C0f[k4].dsem, reads=[B_C0f[k4]])

                def m_sample_post(h, s):
                    for c_ in range(2):
                        nv = n0T[:].rearrange("p (j h c) -> p j h c", j=16, h=HM)[:, :, h, c_].unsqueeze(2).to_broadcast([128, 16, 8])
                        P.op("dve", lambda e, c_=c_, nv=nv: e.tensor_tensor(
                            prod[c_][:].rearrange("p (j t) -> p j t", j=16),
                            qT[s][:, c_ * TOK + 1024: c_ * TOK + 1152].rearrange("p (j t) -> p j t", j=16), nv, ALU.mult),
                            reads=[B_hd[s], B_n0], writes=[B_prod[c_]])
                    P.op("pe", lambda e: (e.matmul(STb[0][:, 129:130], prod[0][:], onesf[:, 0:1], start=True, stop=False),
                                          e.matmul(STb[0][:, 129:130], prod[1][:], onesf[:, 0:1], start=False, stop=True))[1],
                         reads=[B_prod[0], B_prod[1], B_const], writes=[B_STb[0]])

                def load_cin_m(h):
                    P.dma("sp", Cin[h % 2][:].rearrange("p (c e) -> p c e", c=2),
                          s_stC[h * 256:(h + 1) * 256, :].rearrange("(c p) e -> p c e", p=128), B_Cin[h % 2].dsem, writes=[B_Cin[h % 2]])

                def load_cin_r(h):
                    P.dma("sp", Cin[h % 2][:, 0:256], s_stS[h * 128:(h + 1) * 128, :], B_Cin[h % 2].dsem, writes=[B_Cin[h % 2]])
                load_cin_m(0)
                load_head_m(0, 0)
                P.dma("sp", negGb[0][:], s_negG[0:1, :].to_broadcast([128, TOK]), B_negGb[0].dsem, writes=[B_negGb[0]])
                for h in range(HM):
                    s = h % 2
                    if h + 1 < HM:
                        load_cin_m(h + 1)
                        load_head_m(h + 1, (h + 1) % 2)
                    else:
                        load_cin_r(0)
                        load_head_r(0, (h + 1) % 2)
                    hp = h % 2
                    P.op("act", lambda e, hp=hp: e.copy(Cbf[:, 0:512], Cin[hp][:, 0:512]), reads=[B_Cin[hp]], writes=[B_Cbfh[0]])
                    P.op("act", lambda e, hp=hp: e.copy(Cbf[:, 512:1024], Cin[hp][:, 512:1024]), reads=[B_Cin[hp]], writes=[B_Cbfh[1]])
                    P.op("dve", lambda e, h=h: e.tensor_copy(nst[:], npre[:, 2 * h:2 * h + 2]), reads=[B_npre], writes=[B_n])
                    P.op("dve", lambda e: e.tensor_copy(nbf[:], nst[:]), reads=[B_n], writes=[B_nbf])
                    nb_ = h % 2
                    P.op("dve", lambda e, nb_=nb_: e.tensor_tensor(
                        ex_all[:, 0:1024].rearrange("p (t l) -> p t l", t=8), negGb[nb_][:, 0:1024].rearrange("p (t l) -> p t l", t=8),
                        cbp[:].unsqueeze(1).to_broadcast([128, 8, 128]), ALU.add), reads=[B_negGb[nb_]], writes=[B_exall])
                    P.op("dve", lambda e, nb_=nb_: e.tensor_add(ex_all[:, 1024:1152], negGb[nb_][:, 1024:1152], cbs[:]),
                         reads=[B_negGb[nb_], B_exall], writes=[B_exall])
                    if h + 1 < HM:
                        P.dma("sp", negGb[1 - nb_][:], s_negG[h + 1:h + 2, :].to_broadcast([128, TOK]), B_negGb[1 - nb_].dsem,
                              writes=[B_negGb[1 - nb_]])
                    wi = {}
                    m_sample_load(h, 0)
                    m_sample_load(h, 1)
                    m_sample_pre(h, s)
                    wi[0] = m_front_a(h, s, 0)
                    m_front_b(h, s, 0)
                    for t in range(NT):
                        if 2 * t + 3 < 16:
                            m_sample_load(h, 2 * t + 2)
                            m_sample_load(h, 2 * t + 3)
                        if t + 1 < NT:
                            wi[t + 1] = m_front_a(h, s, t + 1)
                        if t < NT - 1:
                            m_inter(h, s, t)
                            m_upd_mm(h, s, t, wi[t])
                            if t + 1 < NT:
                                m_front_b(h, s, t + 1)
                            m_upd_ew(h, s, t)
                            m_sample_j(h, s, 2 * t)
                        else:
                            m_sample_post(h, s)
                        if t > 0:
                            m_back2(h, s, t - 1)
                        m_back1(h, s, t)
                        if t < NT - 1:
                            m_sample_j(h, s, 2 * t + 1)
                    m_back2(h, s, NT - 1)

                P.op("pe", lambda e: e.matmul(ps_ST, snT[:], identf[:], start=True, stop=True), reads=[B_snT, B_const], writes=[B_ST])
                P.op("dve", lambda e: e.tensor_copy(n0r[:], ps_ST), reads=[B_ST, B_n0], writes=[B_n0])
                P.dma("sp", sn, n0r[:], B_n0.dsem, reads=[B_n0])
                P.op("pe", lambda e: e.matmul(ps_Gb[0:8, :], pnT[:], identf[:], start=True, stop=True), reads=[B_pnT, B_const],
                     writes=[B7])
                P.op("dve", lambda e: e.tensor_copy(pnst[:], ps_Gb[0:8, :]), reads=[B7], writes=[B_pnst])
                P.dma("sp", pn, pnst[:], B_pnst.dsem, reads=[B_pnst])

                def r_front_a(h, s, t):
                    r = t % 2
                    smp = (t == NT - 1)
                    st_ap = STb[r][:, 0:128]
                    P.op("pe", lambda e: e.matmul(st_ap, kT[s][:, t * 128:(t + 1) * 128], qT[s][:, t * 128:(t + 1) * 128], start=True, stop=True),
                         reads=[B_hd[s]], writes=[B_STb[r]])
                    Dm = Ds if smp else Dp
                    P.op("dve", lambda e: e.tensor_mul(sT[r][:], st_ap, Dm[:]), reads=[B_STb[r], B_rc], writes=[B_sT[r]])
                    if not smp:
                        i = t % 2
                        P.op("dve", lambda e: e.tensor_scalar_mul(wvb[i][:, 0:256], vtm[s][:, t * 256:(t + 1) * 256], rcol[:, 1:2]),
                             reads=[B_hd[s], B_rc], writes=[B_wvb[i]])
                        return i
                    return None

                def r_front_b(h, s, t):
                    r = t % 2
                    P.op("pe", lambda e: e.matmul(Ab[r][:, 0:256], sT[r][:], vtm[s][:, t * 256:(t + 1) * 256], start=True, stop=True),
                         reads=[B_sT[r], B_hd[s]], writes=[B_Ab[r]])

                def r_inter(h, s, t):
                    P.op("pe", lambda e: e.matmul(pst[4][:, 0:256], qT[s][:, t * 128:(t + 1) * 128], Cbf[:, 0:256], start=True, stop=True),
                         reads=[B_hd[s], B_Cbf], writes=[B_ps[4]])

                def r_upd(h, s, t, i, sdec_p):
                    P.op("pe", lambda e: e.matmul(pst[5][:, 0:256], ktm[s][:, t * 128:(t + 1) * 128], wvb[i][:, 0:256], start=True, stop=True),
                         reads=[B_hd[s], B_wvb[i]], writes=[B_ps[5]])
                    csrc, Bsrc = (Cin[h % 2], B_Cin[h % 2]) if t == 0 else (Cst, B_C)
                    P.op("dve", lambda e: e.scalar_tensor_tensor(out=Cst[:, 0:256], in0=csrc[:, 0:256], scalar=sdec_p, in1=pst[5][:, 0:256],
                                                                 op0=ALU.mult, op1=ALU.add), reads=[Bsrc, B_C, B_ps[5]], writes=[B_C])
                    P.op("act", lambda e: e.copy(Cbf[:, 0:256], Cst[:, 0:256]), reads=[B_C], writes=[B_Cbf])
                    if t == NT - 2:
                        P.dma("sp", pS[h * 128:(h + 1) * 128, :], Cst[:, 0:256], B_C.dsem, reads=[B_C])

                def r_back1(h, s, t):
                    r = t % 2
                    smp = (t == NT - 1)
                    ic = rcol[:, 2:3] if smp else rcol[:, 0:1]
                    bsrc, Bb = (b7f[:, 0:256], B7) if smp else (pst[4][:, 0:256], B_ps[4])
                    P.op("act", lambda e: e.activation(tmpB[r][:, 0:256], bsrc, AF.Copy, scale=ic), reads=[Bb, B_rc],
                         writes=[B_tmpB[r]])
                    P.op("dve", lambda e: e.tensor_add(num[r][:, 0:256], tmpB[r][:, 0:256], Ab[r][:, 0:256]), reads=[B_tmpB[r], B_Ab[r]],
                         writes=[B_num[r]])
                    P.op("act", lambda e: e.activation(junk[:, 0:256], num[r][:, 0:256], AF.Square, accum_out=smc[r][:, 0:1]), reads=[B_num[r]],
                         writes=[B_smc[r], B_junk])
                    P.op("dve", lambda e: e.tensor_scalar(smc[r][:, 6:7], smc[r][:, 0:1], 1.0 / DVR, EPS, ALU.mult, ALU.add), reads=[B_smc[r]],
                         writes=[B_smc[r]])
                    P.op("act", lambda e: e.activation(smc[r][:, 7:8], smc[r][:, 6:7], AF.Ln), reads=[B_smc[r]], writes=[B_smc[r]])
                    P.op("act", lambda e: e.activation(smc[r][:, 9:10], smc[r][:, 7:8], AF.Exp, scale=-0.5), reads=[B_smc[r]], writes=[B_smc[r]])

                def r_back2(h, s, t):
                    r = t % 2
                    out_gated(r, DVR, t, s, HM * DVM + h * DVR)

                def r_sample_pre(h, s):
                    P.op("dve", lambda e: e.tensor_copy(_diag_view(qTz, 0), qT[s][:, 1024:1152].rearrange("p (j t) -> p j t", j=16)),
                         reads=[B_hd[s]], writes=[B_qTz])

                def r_sample_load(h, j):
                    k4 = cnt["c0f"] % 4
                    cnt["c0f"] += 1
                    c0slot[j] = k4
                    r0 = (j * HR + h) * 128
                    P.dma("sp", C0f[k4][:, 0:256], S0[r0:r0 + 128, :], B_C0f[k4].dsem, writes=[B_C0f[k4]])
                    P.dma("pool", C0b[k4][:, 0:256], S0[r0:r0 + 128, :], B_C0b[k4].dsem, writes=[B_C0b[k4]])

                def r_sample_j(h, s, j, sdec_s):
                    t = NT - 1
                    k4 = c0slot[j]
                    i2 = k4
                    iw_ = 2 + cnt["wv"] % 2
                    cnt["wv"] += 1
                    r0 = (j * HR + h) * 128
                    P.op("pe", lambda e: e.matmul(b7f[:, 0:256], qTz[:, j * 128:(j + 1) * 128], C0b[i2][:, 0:256],
                                                  start=(j == 0), stop=(j == 15)), reads=[B_qTz, B_C0b[i2]], writes=[B7])
                    P.op("dve", lambda e: e.tensor_scalar_mul(wvb[iw_][:, 0:256], vtm[s][:, t * 256:(t + 1) * 256],
                                                              kd16[:, j:j + 1]), reads=[B_hd[s], B_rc], writes=[B_wvb[iw_]])
                    P.op("pe", lambda e: e.matmul(pst[6][:, 0:256], ktm[s][:, t * 128:(t + 1) * 128], wvb[iw_][:, 0:256],
                                                  start=True, stop=True), reads=[B_hd[s], B_wvb[iw_]], writes=[B_ps[6]])
                    P.op("dve", lambda e: e.scalar_tensor_tensor(
                        out=C0f[k4][:, 0:256], in0=C0f[k4][:, 0:256], scalar=sdec_s, in1=pst[6][:, 0:256], op0=ALU.mult,
                        op1=ALU.add), reads=[B_C0f[k4], B_ps[6], B_C0b[i2]], writes=[B_C0f[k4]])
                    P.dma("sp", sS[r0:r0 + 128, :], C0f[k4][:, 0:256], B_C0f[k4].dsem, reads=[B_C0f[k4]])

                for h in range(HR):
                    s = (HM + h) % 2
                    if h + 1 < HR:
                        load_cin_r(h + 1)
                        load_head_r(h + 1, (HM + h + 1) % 2)
                    lg = math.log(1.0 - 2.0 ** (-5.0 - h))
                    P.op("act", lambda e, lg=lg: e.activation(Dp[:], rdiff[:], AF.Exp, scale=lg), reads=[B_const, B_sT[0], B_sT[1]], writes=[B_rc])
                    P.op("dve", lambda e: e.tensor_mul(Ds[:], Dp[:], scausal[:]), reads=[B_rc, B_const], writes=[B_rc])
                    P.op("dve", lambda e: e.tensor_mul(Dp[:], Dp[:], causal[:]), reads=[B_rc, B_const], writes=[B_rc])
                    for k_ in range(4):
                        P.op("act", lambda e, k_=k_, lg=lg: e.activation(rcol[:, k_:k_ + 1], pcols[:, k_:k_ + 1], AF.Exp, scale=lg),
                             reads=[B_const, B_rc], writes=[B_rc])
                    P.op("dve", lambda e: e.tensor_scalar_mul(kd16[:], bm16[:], rcol[:, 3:4]), reads=[B_rc, B_const], writes=[B_rc])
                    sdec_p = math.exp(lg * 128.0)
                    sdec_s = math.exp(lg * 8.0)
                    hp = h % 2
                    P.op("act", lambda e, hp=hp: e.copy(Cbf[:, 0:256], Cin[hp][:, 0:256]), reads=[B_Cin[hp]], writes=[B_Cbf])
                    wi = {}
                    r_sample_load(h, 0)
                    r_sample_load(h, 1)
                    r_sample_pre(h, s)
                    wi[0] = r_front_a(h, s, 0)
                    r_front_b(h, s, 0)
                    for t in range(NT):
                        if 2 * t + 3 < 16:
                            r_sample_load(h, 2 * t + 2)
                            r_sample_load(h, 2 * t + 3)
                        if t + 1 < NT:
                            wi[t + 1] = r_front_a(h, s, t + 1)
                        if t < NT - 1:
                            r_inter(h, s, t)
                            if t + 1 < NT:
                                r_front_b(h, s, t + 1)
                            r_upd(h, s, t, wi[t], sdec_p)
                            r_sample_j(h, s, 2 * t, sdec_s)
                        if t > 0:
                            r_back2(h, s, t - 1)
                        r_back1(h, s, t)
                        if t < NT - 1:
                            r_sample_j(h, s, 2 * t + 1, sdec_s)
                    r_back2(h, s, NT - 1)
                P.end_phase(_mk974)

        if want("wout"):
            with ExitStack() as ph:
                _mk1445 = P.mark()
                g = alloc_gemm(ph, TOK)
                loadt(g, s_cat, NT, False, None)
                alloc_w(ph, g)
                xr = Stage(ph, "xr", 3, 512, F32)
                wn = load_w(g, [w_out[:, 0:512]], 32)
                for j in range(8):
                    w = wn
                    if j + 1 < 8:
                        wn = load_w(g, [w_out[:, (j + 1) * 512:(j + 2) * 512]], 32)
                    ld = {}

                    def issue_x(t, j=j):
                        xt_, Bx = xr.nxt()
                        P.dma("act", xt_[:], xm[t * 128:(t + 1) * 128, j * 512:(j + 1) * 512], Bx.dsem, writes=[Bx])
                        ld[t] = (xt_, Bx)
                    issue_x(0)
                    issue_x(1)
                    for t in range(NT):
                        xt_, Bx = ld.pop(t)
                        ps, B = mm_tm(g, w, t, 0, 512)
                        P.op("dve", lambda e, xt_=xt_, ps=ps: e.tensor_add(xt_[:], xt_[:], ps), reads=[B, Bx], writes=[Bx])
                        P.dma("sp", s_yacc[t * 128:(t + 1) * 128, j * 512:(j + 1) * 512], xt_[:], Bx.dsem, reads=[Bx])
                        if t + 2 < NT:
                            issue_x(t + 2)
                P.end_phase(_mk1445)

        if want("ffn"):
            with ExitStack() as ph:
                _mk1476 = P.mark()
                g = alloc_gemm(ph, TOK)
                loadt(g, s_yacc, NT, True, g_ffn)
                alloc_w(ph, g)
                aT = sb(ph, "aT", [128, 16 * TOK], BF16)
                B_aT = [[P.buf("aT%d_%d" % (i, b)) for b in range(3)] for i in range(16)]
                rl = Stage(ph, "rl", 2, 384, F32)
                ya = Stage(ph, "ya", 4, 512, F32)
                B_y = [[P.buf("y%d_%d" % (t, c)) for c in range(8)] for t in range(NT)]
                NG = DFF // 2048
                seq = []
                for gi in range(NG):
                    for i in range(4):
                        seq.append(("up", gi, i))
                    for i in range(4):
                        seq.append(("dn", gi, i))

                def wsrc(job):
                    kind, gi, i = job
                    if kind == "up":
                        return [w_up[:, gi * 2048 + i * 512: gi * 2048 + (i + 1) * 512]], 32
                    return [w_down[gi * 2048:(gi + 1) * 2048, i * 1024:(i + 1) * 1024]], 16
                a, b_ = wsrc(seq[0])
                wn = load_w(g, a, b_)
                for si, job in enumerate(seq):
                    kind, gi, i = job
                    w = wn
                    if si + 1 < len(seq):
                        a, b_ = wsrc(seq[si + 1])
                        wn = load_w(g, a, b_)
                    if kind == "up":
                        for sbk in range(4):
                            fc = i * 4 + sbk
                            for b in range(3):
                                ps, B = mm_fm(g, w, sbk * 128, 128, b * 384, 384)
                                r_, Br = rl.nxt()
                                P.op("act", lambda e, r_=r_, ps=ps: e.activation(r_[:], ps, AF.Relu), reads=[B], writes=[Br])
                                P.op("dve", lambda e, r_=r_, fc=fc, b=b: e.tensor_mul(aT[:, fc * TOK + b * 384: fc * TOK + (b + 1) * 384],
                                                                                     r_[:], r_[:]), reads=[Br], writes=[B_aT[fc][b]])
                    else:
                        blocks = [(cb, t) for cb in range(2) for t in range(NT)]
                        loaded = {}

                        def issue(n, i=i):
                            cb, t = blocks[n]
                            c0 = i * 1024 + cb * 512
                            yt, By = ya.nxt()
                            P.dma("act", yt[:], s_yacc[t * 128:(t + 1) * 128, c0:c0 + 512], By.dsem, reads=[B_y[t][c0 // 512]], writes=[By])
                            loaded[n] = (yt, By)
                        for n in range(min(3, len(blocks))):
                            issue(n)
                        for n, (cb, t) in enumerate(blocks):
                            c0 = i * 1024 + cb * 512
                            yt, By = loaded.pop(n)
                            ps, B = mm_tm(g, w, t, cb * 512, 512,
                                          xsrc=(lambda kc, t=t: aT[:, kc * TOK + t * 128: kc * TOK + (t + 1) * 128]),
                                          xbufs=[B_aT[fc][t // 3] for fc in range(16)])
                            P.op("dve", lambda e, yt=yt, ps=ps: e.tensor_add(yt[:], yt[:], ps), reads=[B, By], writes=[By])
                            P.dma("sp", s_yacc[t * 128:(t + 1) * 128, c0:c0 + 512], yt[:], By.dsem, reads=[By], writes=[B_y[t][c0 // 512]])
                            if n + 3 < len(blocks):
                                issue(n + 3)
                P.end_phase(_mk1476)

        if want("fin"):
            with ExitStack() as ph:
                _mk1543 = P.mark()
                xs = [sb(ph, "fx%d" % i, [128, D], F32) for i in range(2)]
                jb = sb(ph, "fj", [128, D], BF16)
                gb = sb(ph, "fgb", [128, D], F32)
                st_ = sb(ph, "fst", [128, 8], F32)
                B_xs = [P.buf("fx%d" % i, dma=True) for i in range(2)]
                B_gb = P.buf("fgb", dma=True)
                B_st = [P.buf("fst0"), P.buf("fst1")]
                B_jb = P.buf("fj")
                P.dma("sp", gb[:], g_fin.to_broadcast([128, D]), B_gb.dsem, writes=[B_gb])
                P.dma("act", xs[0][:], s_yacc[0:128, :], B_xs[0].dsem, writes=[B_xs[0]])
                for t in range(NT):
                    s = t % 2
                    c = s * 4
                    if t + 1 < NT:
                        P.dma("act", xs[1 - s][:], s_yacc[(t + 1) * 128:(t + 2) * 128, :], B_xs[1 - s].dsem, writes=[B_xs[1 - s]])
                    P.op("act", lambda e, s=s, c=c: e.activation(jb[:], xs[s][:], AF.Square, accum_out=st_[:, c:c + 1]),
                         reads=[B_xs[s]], writes=[B_jb, B_st[s]])
                    P.op("act", lambda e, c=c: e.activation(st_[:, c + 2:c + 3], st_[:, c:c + 1], AF.Ln, scale=1.0 / D, bias=EPS), reads=[B_st[s]],
                         writes=[B_st[s]])
                    P.op("act", lambda e, c=c: e.activation(st_[:, c + 3:c + 4], st_[:, c + 2:c + 3], AF.Exp, scale=-0.5), reads=[B_st[s]],
                         writes=[B_st[s]])
                    P.op("dve", lambda e, s=s, c=c: e.scalar_tensor_tensor(out=xs[s][:], in0=xs[s][:], scalar=st_[:, c + 3:c + 4], in1=gb[:],
                                                                           op0=ALU.mult, op1=ALU.mult),
                         reads=[B_xs[s], B_st[s], B_gb], writes=[B_xs[s]])
                    P.dma("sp", y[t * 128:(t + 1) * 128, :], xs[s][:], B_xs[s].dsem, reads=[B_xs[s]])
                P.end_phase(_mk1543)
        P.barrier()
        P.emit()
        if LATE:
            raise RuntimeError("late binding bugs:\n" + "\n".join(sorted(set(LATE))))
        P.stats = {e: len(P.q[e]) for e in P.ENG}
        nc._prog_stats = (P.stats, {e: P.esem[e].count for e in P.ENG}, len(P.sems))
    return nc


def _diag_view(qTz, c):
    base = qTz[:, c * 16 * 128:(c + 1) * 16 * 128]
    a = base.ap
    return bass.AP(base.tensor, base.offset, [list(a[0]), [a[-1][0] * 136, 16], [a[-1][0], 8]])


def _mk_dsem(P, B):
    B.dsem = P.new_sem("dsx%d" % len(P.sems))
    return B.dsem


_NC_CACHE = {}


def _get_nc():
    if "nc" not in _NC_CACHE:
        _NC_CACHE["nc"] = build_program()
    return _NC_CACHE["nc"]


def make_in_maps(inputs, cores=range(8)):
    f = lambda a: np.ascontiguousarray(np.asarray(a, dtype=np.float32))
    x_prompt = f(inputs["x_prompt"])
    x_sample = f(inputs["x_sample"])
    sCs = f(inputs["state_mlstm_C"])[0]
    sns = f(inputs["state_mlstm_n"])[0]
    sms = f(inputs["state_mlstm_m"])[0]
    sSs = f(inputs["state_ret_S"])[0]
    shared = {
        "w_in": f(inputs["w_in"])[0], "w_out": f(inputs["w_out"])[0], "w_up": f(inputs["w_up"])[0],
        "w_down": f(inputs["w_down"])[0],
        "b_ig": f(inputs["b_igate"]).reshape(HM, 1), "b_fg": f(inputs["b_fgate"]).reshape(HM, 1),
        "g_mh": f(inputs["g_mlstm_head"])[0], "g_rh": f(inputs["g_ret_head"])[0],
        "g_mix": f(inputs["g_norm_mix"]).reshape(1, D), "g_ffn": f(inputs["g_norm_ffn"]).reshape(1, D),
        "g_fin": f(inputs["g_final"]).reshape(1, D),
    }
    freqs = (np.float32(10000.0) ** (-np.arange(0, DKR, 2, dtype=np.float32) / np.float32(DKR))).astype(np.float32)
    frq = np.ascontiguousarray(np.broadcast_to(freqs[None, :], (128, 64))).astype(np.float32)
    p = np.arange(128, dtype=np.float32)
    maps = []
    for c in cores:
        b, half = c // 2, c % 2
        xmain = np.concatenate([x_prompt[b, half * 1024:(half + 1) * 1024, :],
                                x_sample[c * 16:(c + 1) * 16].reshape(128, D)], axis=0)
        xpre = x_prompt[b, 0:1024, :]
        aux = np.zeros((128, 16), np.float32)
        aux[:, 0] = float(half)
        for t in range(8):
            aux[:, 1 + t] = half * 1024 + t * 128 + p
        aux[:, 9] = 16384.0 + (p % 8)
        aux[:, 10] = p
        aux[:, 11] = p % 8
        auxp = np.zeros((128, 8), np.float32)
        for t in range(8):
            auxp[:, t] = t * 128 + p
        m = dict(shared)
        m.update({
            "xm": np.ascontiguousarray(xmain), "xp": np.ascontiguousarray(xpre),
            "C0": np.ascontiguousarray(sCs[c * 16:(c + 1) * 16].reshape(16 * HM * DKM, DVM)),
            "n0": np.ascontiguousarray(sns[c * 16:(c + 1) * 16].reshape(128, 128)),
            "m0": np.ascontiguousarray(sms[c * 16:(c + 1) * 16]),
            "S0": np.ascontiguousarray(sSs[c * 16:(c + 1) * 16].reshape(16 * HR * DKR, DVR)),
            "aux": aux, "frq": frq, "auxp": auxp,
        })
        maps.append(m)
    return maps


def kernel(**inputs):
    nc = _get_nc()
    maps = make_in_maps(inputs)
    res = run_bass_kernel_spmd(nc, maps, core_ids=list(range(8))).results
    B = 4
    y_prompt = np.zeros((B, 2048, D), np.float32)
    y_sample = np.zeros((128, 8, D), np.float32)
    pC = np.zeros((1, B, HM, DKM, DVM), np.float32)
    pn = np.zeros((1, B, HM, DKM), np.float32)
    pm = np.zeros((1, B, HM), np.float32)
    pS = np.zeros((1, B, HR, DKR, DVR), np.float32)
    sC = np.zeros((1, 128, HM, DKM, DVM), np.float32)
    sn = np.zeros((1, 128, HM, DKM), np.float32)
    sm = np.zeros((1, 128, HM), np.float32)
    sS = np.zeros((1, 128, HR, DKR, DVR), np.float32)
    for c in range(8):
        r = res[c]
        b, half = c // 2, c % 2
        y_prompt[b, half * 1024:(half + 1) * 1024] = r["y"][0:1024]
        y_sample[c * 16:(c + 1) * 16] = r["y"][1024:1152].reshape(16, 8, D)
        if half == 1:
            pC[0, b] = r["pC"].reshape(HM, DKM, DVM)
            pn[0, b] = r["pn"].reshape(HM, DKM)
            pm[0, b] = r["pm"].reshape(HM)
            pS[0, b] = r["pS"].reshape(HR, DKR, DVR)
        sC[0, c * 16:(c + 1) * 16] = r["sC"].reshape(16, HM, DKM, DVM)
        sn[0, c * 16:(c + 1) * 16] = r["sn"].reshape(16, HM, DKM)
        sm[0, c * 16:(c + 1) * 16] = r["sm"]
        sS[0, c * 16:(c + 1) * 16] = r["sS"].reshape(16, HR, DKR, DVR)
    return (y_prompt, y_sample, pC, pn, pm, pS, sC, sn, sm, sS)
```
